# Optimizing a Trainium2 kernel written in Bass

```python
import math
import jax, jax.numpy as jnp
from jax import lax
import numpy as np

D_MODEL = 1024
BATCH = 8
SEQ = 4096
DEPTH = 2

N_MEM = 256
MEM_HEADS = 4
MEM_WIDTH = D_MODEL // 4
MEM_HEAD_DIM = MEM_WIDTH // MEM_HEADS
MIX_WIDTH = D_MODEL - MEM_WIDTH
BRANCH_WIDTH = D_MODEL
EPS = 1e-6

QK_NOPE_DIM = 128
QK_ROPE_DIM = 64
V_HEAD_DIM = 128
MLA_HEADS = MIX_WIDTH // V_HEAD_DIM
Q_LORA_RANK = 384
KV_LORA_RANK = 256
ROPE_THETA = 10000.0
Q_BLOCK = 128

MLSTM_HEADS = 4
MLSTM_V_DIM = MIX_WIDTH // MLSTM_HEADS
MLSTM_QK_DIM = MLSTM_V_DIM // 2
CONV_WIDTH = 4
CHUNK = 64

N_LAYERS_A = (DEPTH + 1) // 2
N_LAYERS_B = DEPTH // 2
A_IN_SIZES = [Q_LORA_RANK, KV_LORA_RANK, QK_ROPE_DIM, MEM_WIDTH, BRANCH_WIDTH]
B_IN_SIZES = [MIX_WIDTH, 2 * MLSTM_HEADS, MIX_WIDTH, MEM_WIDTH, BRANCH_WIDTH]
A_IN_COLS = sum(A_IN_SIZES)
B_IN_COLS = sum(B_IN_SIZES)

kernel_name = 'hybrid_mla_mlstm_memory_trunk'


def rms_norm(x, g):
    xf = x.astype(jnp.float32)
    y = xf * lax.rsqrt(jnp.mean(xf * xf, axis=-1, keepdims=True) + EPS)
    return (y * g.astype(jnp.float32)).astype(x.dtype)


def split_cols(t, sizes):
    offs = np.cumsum(sizes)[:-1].tolist()
    return jnp.split(t, offs, axis=-1)


def rope_tables(positions):
    inv_freq = ROPE_THETA ** (-jnp.arange(0, QK_ROPE_DIM, 2, dtype=jnp.float32) / QK_ROPE_DIM)
    ang = positions.astype(jnp.float32)[..., None] * inv_freq
    return jnp.cos(ang), jnp.sin(ang)


def apply_rope(x, cos, sin):
    x1, x2 = jnp.split(x, 2, axis=-1)
    c = cos.astype(x.dtype)
    s = sin.astype(x.dtype)
    return jnp.concatenate([x1 * c - x2 * s, x1 * s + x2 * c], axis=-1)


def causal_block_attention(q, k, v, scale):
    B, S, H, Dqk = q.shape
    nb = S // Q_BLOCK
    qb = jnp.moveaxis(q.reshape(B, nb, Q_BLOCK, H, Dqk), 1, 0)
    kpos = jnp.arange(S)

    def one_block(args):
        qi, blk = args
        s = jnp.einsum('bqhd,bkhd->bhqk', qi, k).astype(jnp.float32) * scale
        qpos = blk * Q_BLOCK + jnp.arange(Q_BLOCK)
        s = jnp.where(kpos[None, :] <= qpos[:, None], s, -jnp.inf)
        p = jax.nn.softmax(s, axis=-1).astype(v.dtype)
        return jnp.einsum('bhqk,bkhd->bqhd', p, v)

    out = lax.map(one_block, (qb, jnp.arange(nb)))
    return jnp.moveaxis(out, 0, 1).reshape(B, S, H, v.shape[-1])


def memory_attention(q_mem, mem, mem_g, w_mem_kv):
    B, S, _ = q_mem.shape
    kv = rms_norm(mem, mem_g) @ w_mem_kv
    k, v = jnp.split(kv, 2, axis=-1)
    q = q_mem.reshape(B, S, MEM_HEADS, MEM_HEAD_DIM)
    k = k.reshape(B, N_MEM, MEM_HEADS, MEM_HEAD_DIM)
    v = v.reshape(B, N_MEM, MEM_HEADS, MEM_HEAD_DIM)
    s = jnp.einsum('bshd,bmhd->bhsm', q, k).astype(jnp.float32) * (MEM_HEAD_DIM ** -0.5)
    p = jax.nn.softmax(s, axis=-1).astype(v.dtype)
    o = jnp.einsum('bhsm,bmhd->bshd', p, v)
    return o.reshape(B, S, MEM_WIDTH)


def mla_mixer(c_q, c_kv, k_rope, cos, sin, q_a_g, w_uq, kv_a_g, w_ukv):
    B, S, _ = c_q.shape
    q = (rms_norm(c_q, q_a_g) @ w_uq).reshape(B, S, MLA_HEADS, QK_NOPE_DIM + QK_ROPE_DIM)
    q_nope, q_rope = split_cols(q, [QK_NOPE_DIM, QK_ROPE_DIM])
    q_rope = apply_rope(q_rope, cos[:, :, None, :], sin[:, :, None, :])
    kv = (rms_norm(c_kv, kv_a_g) @ w_ukv).reshape(B, S, MLA_HEADS, QK_NOPE_DIM + V_HEAD_DIM)
    k_nope, v = split_cols(kv, [QK_NOPE_DIM, V_HEAD_DIM])
    k_rope = apply_rope(k_rope, cos, sin)
    k = jnp.concatenate(
        [k_nope, jnp.broadcast_to(k_rope[:, :, None, :], (B, S, MLA_HEADS, QK_ROPE_DIM))], axis=-1)
    q = jnp.concatenate([q_nope, q_rope], axis=-1)
    o = causal_block_attention(q, k, v, (QK_NOPE_DIM + QK_ROPE_DIM) ** -0.5)
    return o.reshape(B, S, MIX_WIDTH)


def mlstm_chunkwise(q, k, v, i_pre, f_pre):
    B, S, H, Dk = q.shape
    Dv = v.shape[-1]
    nc = S // CHUNK

    def to_chunks(t):
        t = t.astype(jnp.float32).reshape((B, nc, CHUNK, H) + t.shape[3:])
        return jnp.moveaxis(t, (1, 3), (0, 2))

    log_i = i_pre.astype(jnp.float32)
    log_f = jax.nn.log_sigmoid(f_pre.astype(jnp.float32))
    xs = (to_chunks(q), to_chunks(k), to_chunks(v), to_chunks(log_i), to_chunks(log_f))
    causal = jnp.tril(jnp.ones((CHUNK, CHUNK), dtype=bool))

    def step(carry, xs_c):
        C, n, m = carry
        qc, kc, vc, li, lf = xs_c
        b = jnp.cumsum(lf, axis=-1)
        d = jnp.where(causal, b[..., :, None] - b[..., None, :] + li[..., None, :], -jnp.inf)
        inter = b + m[..., None]
        m_t = jnp.maximum(inter, jnp.max(d, axis=-1))
        w_intra = jnp.exp(d - m_t[..., None])
        w_inter = jnp.exp(inter - m_t)
        sqk = jnp.einsum('bhtd,bhsd->bhts', qc, kc) * w_intra
        num = (w_inter[..., None] * jnp.einsum('bhtd,bhde->bhte', qc, C)
               + jnp.einsum('bhts,bhse->bhte', sqk, vc))
        den = w_inter * jnp.einsum('bhtd,bhd->bht', qc, n) + jnp.sum(sqk, axis=-1)
        den = jnp.maximum(jnp.abs(den), jnp.exp(-m_t))
        h = num / den[..., None]
        bL = b[..., -1]
        dec = bL[..., None] - b + li
        m_new = jnp.maximum(bL + m, jnp.max(dec, axis=-1))
        a = jnp.exp(bL + m - m_new)
        ws = jnp.exp(dec - m_new[..., None])
        C_new = a[..., None, None] * C + jnp.einsum('bhs,bhsd,bhse->bhde', ws, kc, vc)
        n_new = a[..., None] * n + jnp.einsum('bhs,bhsd->bhd', ws, kc)
        return (C_new, n_new, m_new), h

    init = (jnp.zeros((B, H, Dk, Dv), jnp.float32), jnp.zeros((B, H, Dk), jnp.float32),
            jnp.zeros((B, H), jnp.float32))
    _, h = lax.scan(step, init, xs)
    return jnp.moveaxis(h, (0, 2), (1, 3)).reshape(B, S, H, Dv).astype(v.dtype)


def mlstm_mixer(u, if_pre, o_pre, gate_bias, conv_w, conv_b, w_q, w_k, w_v, head_g, skip):
    B, S, _ = u.shape
    u_pad = jnp.pad(u, ((0, 0), (CONV_WIDTH - 1, 0), (0, 0)))
    conv = conv_b + sum(u_pad[:, t:t + S, :] * conv_w[t] for t in range(CONV_WIDTH))
    uc = jax.nn.silu(conv)
    uch = uc.reshape(B, S, MLSTM_HEADS, MLSTM_V_DIM)
    uh = u.reshape(B, S, MLSTM_HEADS, MLSTM_V_DIM)
    q = jnp.einsum('bshd,hde->bshe', uch, w_q)
    k = jnp.einsum('bshd,hde->bshe', uch, w_k) * (MLSTM_QK_DIM ** -0.5)
    v = jnp.einsum('bshd,hde->bshe', uh, w_v)
    gates = if_pre + gate_bias
    i_pre, f_pre = jnp.split(gates, 2, axis=-1)
    h = mlstm_chunkwise(q, k, v, i_pre, f_pre)
    h = rms_norm(h, head_g).reshape(B, S, MIX_WIDTH)
    return jax.nn.sigmoid(o_pre) * h + skip * uc


def mla_layer(x, mem, cos, sin, pre_g, w_in, q_a_g, w_uq, kv_a_g, w_ukv, mem_g, w_mem_kv, w_out, post_g):
    h = rms_norm(x, pre_g)
    c_q, c_kv, k_rope, q_mem, gate = split_cols(h @ w_in, A_IN_SIZES)
    mix = mla_mixer(c_q, c_kv, k_rope, cos, sin, q_a_g, w_uq, kv_a_g, w_ukv)
    mo = memory_attention(q_mem, mem, mem_g, w_mem_kv)
    y = (jnp.concatenate([mix, mo], axis=-1) * jax.nn.silu(gate)) @ w_out
    return x + rms_norm(y, post_g)


def mlstm_layer(x, mem, pre_g, w_in, gate_bias, conv_w, conv_b, w_q, w_k, w_v, head_g, skip,
                mem_g, w_mem_kv, w_out, post_g):
    h = rms_norm(x, pre_g)
    u, if_pre, o_pre, q_mem, gate = split_cols(h @ w_in, B_IN_SIZES)
    mix = mlstm_mixer(u, if_pre, o_pre, gate_bias, conv_w, conv_b, w_q, w_k, w_v, head_g, skip)
    mo = memory_attention(q_mem, mem, mem_g, w_mem_kv)
    y = (jnp.concatenate([mix, mo], axis=-1) * jax.nn.silu(gate)) @ w_out
    return x + rms_norm(y, post_g)


def setup_inputs(seed: int = 0) -> dict:
    key = jax.random.key(seed)
    ks = jax.random.split(key, 32)
    f32 = jnp.float32
    NA, NB, H = N_LAYERS_A, N_LAYERS_B, MLSTM_HEADS

    def nrm(k, shape, fan_in):
        return jax.random.normal(k, shape, f32) * (fan_in ** -0.5)

    def gain(k, shape):
        return 1.0 + 0.02 * jax.random.normal(k, shape, f32)

    x = jax.random.normal(ks[0], (BATCH, SEQ, D_MODEL), f32)
    mem = jax.random.normal(ks[1], (BATCH, N_MEM, D_MODEL), f32)
    offs = jax.random.randint(ks[2], (BATCH, 1), 0, 1024, dtype=jnp.int32)
    positions = (jnp.arange(SEQ, dtype=jnp.int32)[None, :] + offs).astype(jnp.int32)
    gb_i = 0.1 * jax.random.normal(ks[23], (NB, H), f32)
    gb_f = 3.0 + 0.1 * jax.random.normal(ks[24], (NB, H), f32)
    return {
        'x': x, 'mem': mem, 'positions': positions,
        'a_pre_g': gain(ks[3], (NA, D_MODEL)),
        'a_w_in': nrm(ks[4], (NA, D_MODEL, A_IN_COLS), D_MODEL),
        'a_q_a_g': gain(ks[5], (NA, Q_LORA_RANK)),
        'a_w_uq': nrm(ks[6], (NA, Q_LORA_RANK, MLA_HEADS * (QK_NOPE_DIM + QK_ROPE_DIM)), Q_LORA_RANK),
        'a_kv_a_g': gain(ks[7], (NA, KV_LORA_RANK)),
        'a_w_ukv': nrm(ks[8], (NA, KV_LORA_RANK, MLA_HEADS * (QK_NOPE_DIM + V_HEAD_DIM)), KV_LORA_RANK),
        'a_mem_g': gain(ks[9], (NA, D_MODEL)),
        'a_w_mem_kv': nrm(ks[10], (NA, D_MODEL, 2 * MEM_WIDTH), D_MODEL),
        'a_w_out': nrm(ks[11], (NA, BRANCH_WIDTH, D_MODEL), BRANCH_WIDTH),
        'a_post_g': gain(ks[12], (NA, D_MODEL)),
        'b_pre_g': gain(ks[13], (NB, D_MODEL)),
        'b_w_in': nrm(ks[14], (NB, D_MODEL, B_IN_COLS), D_MODEL),
        'b_gate_bias': jnp.concatenate([gb_i, gb_f], axis=-1),
        'b_conv_w': nrm(ks[15], (NB, CONV_WIDTH, MIX_WIDTH), CONV_WIDTH),
        'b_conv_b': 0.01 * jax.random.normal(ks[16], (NB, MIX_WIDTH), f32),
        'b_w_q': nrm(ks[17], (NB, H, MLSTM_V_DIM, MLSTM_QK_DIM), MLSTM_V_DIM),
        'b_w_k': nrm(ks[18], (NB, H, MLSTM_V_DIM, MLSTM_QK_DIM), MLSTM_V_DIM),
        'b_w_v': nrm(ks[19], (NB, H, MLSTM_V_DIM, MLSTM_V_DIM), MLSTM_V_DIM),
        'b_head_g': gain(ks[20], (NB, H, MLSTM_V_DIM)),
        'b_skip': gain(ks[21], (NB, MIX_WIDTH)),
        'b_mem_g': gain(ks[22], (NB, D_MODEL)),
        'b_w_mem_kv': nrm(ks[25], (NB, D_MODEL, 2 * MEM_WIDTH), D_MODEL),
        'b_w_out': nrm(ks[26], (NB, BRANCH_WIDTH, D_MODEL), BRANCH_WIDTH),
        'b_post_g': gain(ks[27], (NB, D_MODEL)),
    }


def reference(x, mem, positions, a_pre_g, a_w_in, a_q_a_g, a_w_uq, a_kv_a_g, a_w_ukv, a_mem_g,
              a_w_mem_kv, a_w_out, a_post_g, b_pre_g, b_w_in, b_gate_bias, b_conv_w, b_conv_b,
              b_w_q, b_w_k, b_w_v, b_head_g, b_skip, b_mem_g, b_w_mem_kv, b_w_out, b_post_g):
    cos, sin = rope_tables(positions)
    for i in range(DEPTH):
        j = i // 2
        if i % 2 == 0:
            x = mla_layer(x, mem, cos, sin, a_pre_g[j], a_w_in[j], a_q_a_g[j], a_w_uq[j],
                          a_kv_a_g[j], a_w_ukv[j], a_mem_g[j], a_w_mem_kv[j], a_w_out[j], a_post_g[j])
        else:
            x = mlstm_layer(x, mem, b_pre_g[j], b_w_in[j], b_gate_bias[j], b_conv_w[j], b_conv_b[j],
                            b_w_q[j], b_w_k[j], b_w_v[j], b_head_g[j], b_skip[j], b_mem_g[j],
                            b_w_mem_kv[j], b_w_out[j], b_post_g[j])
    return x
```

```python
import math
from contextlib import ExitStack

import numpy as np
import ml_dtypes
import concourse.bass as bass
import concourse.mybir as mybir
from concourse.bass_utils import run_bass_kernel_spmd

F32 = mybir.dt.float32
BF16 = mybir.dt.bfloat16
I32 = mybir.dt.int32
AF = mybir.ActivationFunctionType
ALU = mybir.AluOpType

import os as _os
STQ = _os.environ.get("STQ", "sp")
D = 1024
NMEM = 256
EPS = 1e-6
PI = math.pi
PI_LO = 3.1415925
A_COLS = 1984
B_COLS = 2824
SCALE_A = 192.0 ** -0.5
SCALE_M = 64.0 ** -0.5
SCALE_K = 96.0 ** -0.5
FT = []
for _h in range(4):
    FT.append((_h * 192, 128))
    FT.append((_h * 192 + 128, 64))
FTP = [(r0, 128) for (r0, sz) in FT]


class Sched:
    ENGS = ("pe", "act", "dve", "pool", "sp")
    NPOOL = 56

    def __init__(self, nc, top):
        self.nc = nc
        self.phase_no = 0
        self.active = False
        self.excl = set()
        self.sem = {e: top.enter_context(nc.semaphore(f"eng_{e}")) for e in self.ENGS}
        self.tick = {e: 0 for e in self.ENGS}
        self.top = top
        self.dpool = []
        self.waited = {}
        self.rec = None

    def record(self, f):
        assert self.rec is None
        self.rec = []
        try:
            f()
        finally:
            lst, self.rec = self.rec, None
        return lst

    @staticmethod
    def merge(lists):
        pos = [0] * len(lists)
        out = []
        while True:
            best, bf = None, None
            for i, l in enumerate(lists):
                if pos[i] < len(l):
                    f = pos[i] / len(l)
                    if best is None or f < bf:
                        best, bf = i, f
            if best is None:
                return out
            out.append(lists[best][pos[best]])
            pos[best] += 1

    def zip_emit(self, lists):
        pos = [0] * len(lists)
        while True:
            best, bf = None, None
            for i, l in enumerate(lists):
                if pos[i] < len(l):
                    f = pos[i] / len(l)
                    if best is None or f < bf:
                        best, bf = i, f
            if best is None:
                break
            kind, args, kw = lists[best][pos[best]]
            pos[best] += 1
            getattr(self, kind)(*args, **kw)

    def begin(self):
        assert not self.active
        self.active = True
        self.phase_no += 1
        self.es = ExitStack()
        self.q = {e: [] for e in self.ENGS}
        self.res = {}
        self.dkey = {}
        self.nops = 0
        return self.es

    def _st(self, key):
        st = self.res.get(key)
        if st is None:
            st = {"w": None, "r": {}}
            self.res[key] = st
        return st

    def _deps(self, reads, writes):
        deps = []
        for r in reads:
            st = self._st(r)
            if st["w"] is not None:
                deps.append(st["w"])
            if r in self.excl:
                for src, t in st["r"].items():
                    deps.append((src, t))
        for w in writes:
            st = self._st(w)
            if st["w"] is not None:
                deps.append(st["w"])
            for src, t in st["r"].items():
                deps.append((src, t))
        return deps

    def _waits(self, engine, deps):
        need = {}
        for src, t in deps:
            if src == engine and engine == "pe":
                continue
            if self.waited.get((engine, src), 0) >= t:
                continue
            if need.get(src, 0) < t:
                need[src] = t
        out = []
        for src, t in need.items():
            self.waited[(engine, src)] = t
            if isinstance(src, str):
                out.append((self.sem[src], t))
            else:
                out.append((self.dpool[src][0], t))
        return out

    def _commit(self, token_src, t, reads, writes):
        for r in reads:
            st = self._st(r)
            if st["r"].get(token_src, 0) < t:
                st["r"][token_src] = t
        for w in writes:
            st = self._st(w)
            st["w"] = (token_src, t)
            st["r"] = {}

    def op(self, engine, fn, reads=(), writes=()):
        if self.rec is not None:
            self.rec.append(("op", (engine, fn), dict(reads=reads, writes=writes)))
            return
        waits = self._waits(engine, self._deps(reads, writes))
        self.tick[engine] += 1
        t = self.tick[engine]
        self.q[engine].append((waits, fn, (self.sem[engine], 1)))
        self._commit(engine, t, reads, writes)
        self.nops += 1

    def dma(self, queue, key, fn, reads=(), writes=()):
        if self.rec is not None:
            self.rec.append(("dma", (queue, key, fn), dict(reads=reads, writes=writes)))
            return
        if key not in self.dkey:
            idx = len(self.dkey)
            assert idx < self.NPOOL, "too many DMA keys in one phase"
            if idx >= len(self.dpool):
                self.dpool.append([self.top.enter_context(self.nc.semaphore(f"dma_{idx}")), 0])
            self.dkey[key] = idx
        idx = self.dkey[key]
        waits = self._waits(queue, self._deps(reads, writes))
        self.dpool[idx][1] += 16
        cnt = self.dpool[idx][1]
        self.q[queue].append((waits, fn, (self.dpool[idx][0], 16)))
        self._commit(idx, cnt, reads, writes)
        self.nops += 1

    def end(self):
        nc = self.nc
        final_waits = []
        for key, idx in self.dkey.items():
            final_waits.append((self.dpool[idx][0], self.dpool[idx][1]))
        for e in self.ENGS:
            if e != "sp" and self.tick[e] > 0:
                final_waits.append((self.sem[e], self.tick[e]))
        for e in self.ENGS:
            self.q[e].append((list(final_waits), None, None))

        def replay(e, eng):
            for waits, fn, inc in self.q[e]:
                for s, v in waits:
                    eng.wait_ge(s, v)
                if fn is not None:
                    ins = fn(eng)
                    ins.then_inc(inc[0], inc[1])

        with nc.Block() as block:
            @block.tensor
            def _(eng):
                replay("pe", eng)

            @block.scalar
            def _(eng):
                replay("act", eng)

            @block.vector
            def _(eng):
                replay("dve", eng)

            @block.gpsimd
            def _(eng):
                replay("pool", eng)

            @block.sync
            def _(eng):
                replay("sp", eng)
        self.es.close()
        self.active = False


class Rec:
    def __init__(self):
        self.items = []

    def op(self, *a, **k):
        self.items.append(("op", a, k))

    def dma(self, *a, **k):
        self.items.append(("dma", a, k))


class Rot:
    def __init__(self, name, n):
        self.name, self.n, self.i = name, n, 0

    def next(self):
        k = self.i % self.n
        self.i += 1
        return k


class Builder:
    def __init__(self, S, debug=False, stop=None):
        self.stop = stop
        self.S = S
        self.NT = S // 128
        self.NG = S // 512
        self.debug = debug
        self.nc = bass.Bass("TRN2", target_bir_lowering=False)
        self.sch = None
        self.T = {}
        self._excl = set()

    def din(self, name, shape, dt=F32):
        self.T[name] = self.nc.dram_tensor(name, list(shape), dt, kind="ExternalInput").ap()
        return self.T[name]

    def dscratch(self, name, shape, dt):
        if self.debug:
            t = self.nc.dram_tensor(name, list(shape), dt, kind="ExternalOutput").ap()
        else:
            t = self.nc.dram_tensor(name, list(shape), dt).ap()
        self.T[name] = t
        return t

    def sb(self, es, name, shape, dt):
        self._uid = getattr(self, "_uid", 0) + 1
        return es.enter_context(self.nc.sbuf_tensor(f"{name}_u{self._uid}", list(shape), dt))

    def ps(self, es, name, shape, dt):
        self._excl.add(name)
        self._uid = getattr(self, "_uid", 0) + 1
        return es.enter_context(self.nc.psum_tensor(f"{name}_u{self._uid}", list(shape), dt))

    def load_weight(self, es, dst, w_ap, ncols, kchunks, g_ap=None, tag="w", col0=0):
        sch, nc = self.sch, self.nc
        if not hasattr(self, "_wstg"):
            raise RuntimeError
        stg = self._wstg
        wt_ = self._wtag
        gsb = None
        if g_ap is not None:
            gsb = self.sb(es, f"g_{tag}", [128, len(kchunks)], F32)
            for c, (r0, sz) in enumerate(kchunks):
                def f(e, c=c, r0=r0, sz=sz):
                    return e.dma_start(out=gsb[0:sz, c:c + 1],
                                       in_=g_ap[r0:r0 + sz].rearrange("(p o) -> p o", o=1))
                sch.dma("sp", f"g_{tag}{c}", f, writes=[f"g_{tag}"])
        for c, ch in enumerate(kchunks):
            pieces = ch if isinstance(ch, list) else [(ch[0], ch[1], 0)]
            sz = max(p0 + n for (_, n, p0) in pieces)
            slot = self._wrot.next()
            for (r0, n, p0) in pieces:
                def ld(e, slot=slot, r0=r0, n=n, p0=p0):
                    return e.dma_start(out=stg[slot][p0:p0 + n, 0:ncols], in_=w_ap[r0:r0 + n, col0:col0 + ncols])
                sch.dma("sp", f"wstg{wt_}_{slot}_{p0}", ld, writes=[f"wstg{wt_}_{slot}"])
            eng = "act" if (c % 2 == 0) else "dve"
            if gsb is not None:
                if eng == "act":
                    def cv(e, slot=slot, c=c, sz=sz):
                        return e.activation(out=dst[0:sz, c, 0:ncols], in_=stg[slot][0:sz, 0:ncols],
                                            func=AF.Copy, scale=gsb[0:sz, c:c + 1])
                else:
                    def cv(e, slot=slot, c=c, sz=sz):
                        return e.tensor_scalar(out=dst[0:sz, c, 0:ncols], in0=stg[slot][0:sz, 0:ncols],
                                               scalar1=gsb[0:sz, c:c + 1], scalar2=None, op0=ALU.mult)
                rd = [f"wstg{wt_}_{slot}", f"g_{tag}"]
            else:
                if eng == "act":
                    def cv(e, slot=slot, c=c, sz=sz):
                        return e.activation(out=dst[0:sz, c, 0:ncols], in_=stg[slot][0:sz, 0:ncols], func=AF.Copy)
                else:
                    def cv(e, slot=slot, c=c, sz=sz):
                        return e.tensor_copy(out=dst[0:sz, c, 0:ncols], in_=stg[slot][0:sz, 0:ncols])
                rd = [f"wstg{wt_}_{slot}"]
            sch.op(eng, cv, reads=rd, writes=[tag])

    def wstage(self, es, ncols, n=3):
        self._wstg = [self.sb(es, f"wstg{i}", [128, ncols], F32) for i in range(n)]
        self._wrot = Rot("wstg", n)
        self._wtag = getattr(self, "_wtag", 0) + 1

    def norm_rows_T(self, es, src_ap, nrows, dstT, tag, ident):
        sch = self.sch
        nt = nrows // 128
        xs = self.sb(es, f"{tag}_xs", [128, nt, D], F32)
        xn = self.sb(es, f"{tag}_xn", [128, nt, D], BF16)
        junk = self.sb(es, f"{tag}_junk", [128, D], BF16)
        ss = self.sb(es, f"{tag}_ss", [128, nt], F32)
        pT = self.ps(es, f"{tag}_pT", [128, 1024], BF16)
        sch.dma("sp", f"{tag}_x", lambda e: e.dma_start(out=xs[:], in_=src_ap.rearrange("(j p) d -> p j d", p=128)),
                writes=[f"{tag}_xs"])
        for j in range(nt):
            sch.op("act", lambda e, j=j: e.activation(out=junk[:], in_=xs[:, j, :], func=AF.Square,
                                                     accum_out=ss[:, j:j + 1]),
                   reads=[f"{tag}_xs"], writes=[f"{tag}_ss{j}"])
            sch.op("act", lambda e, j=j: e.activation(out=ss[:, j:j + 1], in_=ss[:, j:j + 1], func=AF.Sqrt,
                                                     scale=1.0 / D, bias=EPS),
                   reads=[f"{tag}_ss{j}"], writes=[f"{tag}_ss{j}"])
            sch.op("dve", lambda e, j=j: e.reciprocal(out=ss[:, j:j + 1], in_=ss[:, j:j + 1]),
                   reads=[f"{tag}_ss{j}"], writes=[f"{tag}_ss{j}"])
            sch.op("dve", lambda e, j=j: e.tensor_scalar(out=xn[:, j, :], in0=xs[:, j, :], scalar1=ss[:, j:j + 1],
                                                        scalar2=None, op0=ALU.mult),
                   reads=[f"{tag}_xs", f"{tag}_ss{j}"], writes=[f"{tag}_xn{j}"])
        for c in range(8):
            def tr(e, c=c):
                for j in range(nt):
                    i = e.transpose(out=pT[:, j * 128:(j + 1) * 128], in_=xn[:, j, c * 128:(c + 1) * 128],
                                    identity=ident[:])
                return i
            sch.op("pe", tr, reads=[f"{tag}_xn{j}" for j in range(nt)] + ["ident"], writes=[f"{tag}_pT"])
            sch.op("act", lambda e, c=c: e.activation(out=dstT[:, c, 0:nrows], in_=pT[:, 0:nrows], func=AF.Copy),
                   reads=[f"{tag}_pT"], writes=[f"{tag}_T"])

    def mem_kv(self, es, wmkv, memnT, kmemT, vmem, tag):
        sch = self.sch
        pk = self.ps(es, f"{tag}_pk", [128, 512], F32)
        for m in range(2):
            def f(e, m=m):
                for c in range(8):
                    i = e.matmul(pk[:, 0:256], lhsT=wmkv[:, c, m * 128:(m + 1) * 128], rhs=memnT[:, c, :],
                                 start=(c == 0), stop=(c == 7))
                return i
            sch.op("pe", f, reads=[f"{tag}w", f"{tag}n_T"], writes=[f"{tag}_pk"])
            sch.op("dve", lambda e, m=m: e.tensor_copy(out=kmemT[:, m, :], in_=pk[:, 0:256]),
                   reads=[f"{tag}_pk"], writes=["kmemT"])
        sch.op("dve", lambda e: e.memset(vmem[:], 0.0), writes=["vmem"])
        for mt in range(2):
            def f(e, mt=mt):
                for c in range(8):
                    i = e.matmul(pk[:, 0:256], lhsT=memnT[:, c, mt * 128:(mt + 1) * 128], rhs=wmkv[:, c, 256:512],
                                 start=(c == 0), stop=(c == 7))
                return i
            sch.op("pe", f, reads=[f"{tag}w", f"{tag}n_T"], writes=[f"{tag}_pk"])
            for hh in range(4):
                half = hh % 2
                sch.op("dve", lambda e, mt=mt, hh=hh, half=half: e.tensor_copy(
                    out=vmem[:, mt, hh, half * 64:(half + 1) * 64], in_=pk[:, hh * 64:(hh + 1) * 64]),
                    reads=[f"{tag}_pk"], writes=["vmem"])

    def build(self):
        nc, S = self.nc, self.S
        x = self.din("x", [S, D])
        mem = self.din("mem", [NMEM, D])
        pos = self.din("pos", [1, S], I32)
        for n, shp in [("a_pre_g", [D]), ("a_w_in", [D, A_COLS]), ("a_q_a_g", [384]), ("a_w_uq", [384, 1152]),
                       ("a_kv_a_g", [256]), ("a_w_ukv", [256, 1536]), ("a_mem_g", [D]), ("a_w_mem_kv", [D, 512]),
                       ("a_w_out", [D, D]), ("a_post_g", [1, D]),
                       ("b_pre_g", [D]), ("b_w_in", [D, B_COLS]), ("b_gate_bias", [8, 1]), ("b_conv_wT", [768, 4]),
                       ("b_conv_b", [768]), ("b_w_q", [768, 96]), ("b_w_k", [768, 96]), ("b_w_v", [768, 192]),
                       ("b_head_g", [1, 768]), ("b_skip", [768]), ("b_mem_g", [D]), ("b_w_mem_kv", [D, 512]),
                       ("b_w_out", [D, D]), ("b_post_g", [1, D])]:
            self.din(n, shp)
        self.din("c_ident", [128, 128], BF16)
        self.din("c_ones", [128, 128], BF16)
        self.din("c_maskT", [128, 128], BF16)
        self.din("c_mask01", [128, 128], BF16)
        self.din("c_onesh", [128, 256], BF16)
        self.din("c_invf", [64, 1])
        self.din("c_sgn", [64, 1])
        self.din("c_identf", [128, 128])
        self.din("c_sel", [4, 4 * 128])
        out = nc.dram_tensor("out", [S, D], F32, kind="ExternalOutput").ap()
        self.T["out"] = out
        self.dscratch("x1", [S, D], F32)
        self.dscratch("gT", [D, S], BF16)
        self.dscratch("cgT", [D, S], BF16)

        with ExitStack() as top:
            self.sch = Sched(nc, top)
            self.sch.excl = self._excl
            ident = self.sb(top, "ident", [128, 128], BF16)
            ones = self.sb(top, "ones", [128, 128], BF16)
            maskT = self.sb(top, "maskT", [128, 128], BF16)
            mask01 = self.sb(top, "mask01", [128, 128], BF16)
            onesh = self.sb(top, "onesh", [128, 2, 128], BF16)
            self.C = dict(ident=ident, ones=ones, maskT=maskT, mask01=mask01, onesh=onesh)
            if self.debug == "A":
                self.layer_A(top)
            else:
                self.layer_A(top, do_out=False)
                self.layer_B(top)
        return nc

    def layer_A(self, top, do_out=True):
        nc, sch, S, T = self.nc, self.sch, self.S, self.T
        C = self.C
        ident, ones, maskT = C["ident"], C["ones"], C["maskT"]
        KC8 = [(c * 128, 128) for c in range(8)]
        self.dscratch("cqT", [384, S], BF16)
        self.dscratch("ckvT", [256, S], BF16)
        self.dscratch("krT", [64, S], BF16)
        with ExitStack() as LA:
            cs1 = self.sb(LA, "cs1", [64, S], F32)
            cs2 = self.sb(LA, "cs2", [64, S], F32)
            with ExitStack() as LA1:
                win = self.sb(LA1, "winA", [128, 8, A_COLS], BF16)
                winsw = self.sb(LA1, "winswA", [128, 8, 64], BF16)
                kmemT = self.sb(LA1, "kmemT", [128, 2, 256], BF16)
                vmem = self.sb(LA1, "vmem", [128, 2, 4, 128], BF16)
                es = sch.begin()
                for nm, tl in [("ident", ident), ("ones", ones), ("maskT", maskT), ("mask01", C["mask01"])]:
                    sch.dma("sp", f"c_{nm}", lambda e, nm=nm, tl=tl: e.dma_start(out=tl[:], in_=T[f"c_{nm}"]),
                            writes=[nm])
                sch.dma("sp", "c_onesh", lambda e: e.dma_start(out=C["onesh"][:].rearrange("p a b -> p (a b)"),
                                                                in_=T["c_onesh"]), writes=["onesh"])
                self.wstage(es, A_COLS)
                wmkv = self.sb(es, "wmkvA", [128, 8, 512], BF16)
                memnT = self.sb(es, "memnT", [128, 8, 256], BF16)
                self.load_weight(es, win, T["a_w_in"], A_COLS, KC8, T["a_pre_g"], tag="winA")
                sch.op("dve", lambda e: e.tensor_copy(out=winsw[:, :, 0:32], in_=win[:, :, 672:704]),
                       reads=["winA"], writes=["winswA"])
                sch.op("dve", lambda e: e.tensor_copy(out=winsw[:, :, 32:64], in_=win[:, :, 640:672]),
                       reads=["winA"], writes=["winswA"])
                self.load_weight(es, wmkv, T["a_w_mem_kv"], 512, KC8, T["a_mem_g"], tag="memAw")
                self.norm_rows_T(es, T["mem"], NMEM, memnT, "memAn", ident)
                self.mem_kv(es, wmkv, memnT, kmemT, vmem, "memA")
                sch.end()
                if self.stop == "A0":
                    return
                es = sch.begin()
                rope_chunks = self.rope_tables(es, cs1, cs2)
                self.proj_phase(es, "A", T["x"], win, winsw, kmemT, vmem, dict(cs1=cs1, cs2=cs2, rope=rope_chunks))
                sch.end()
                if self.stop == "A1":
                    return
            self.attn_phase(cs1, cs2)
            if self.stop == "A2":
                return
        if do_out:
            self.out_phase("A", T["a_w_out"], T["a_post_g"], T["x"], T["x1"], KC8)

    def rope_tables(self, es, cs1, cs2):
        sch, S, T = self.sch, self.S, self.T
        CW = min(S, 1024)
        posi = self.sb(es, "posi", [64, CW], I32)
        ang = self.sb(es, "ang", [64, CW], F32)
        u = self.sb(es, "rt_u", [64, CW], F32)
        ki = self.sb(es, "rt_ki", [64, CW], I32)
        r = self.sb(es, "rt_r", [64, CW], F32)
        m = self.sb(es, "rt_m", [64, CW], F32)
        invf = self.sb(es, "invf", [64, 1], F32)
        sgn = self.sb(es, "sgn", [64, 1], F32)
        sch.dma("sp", "invf", lambda e: e.dma_start(out=invf[:], in_=T["c_invf"]), writes=["invf"])
        sch.dma("sp", "sgn", lambda e: e.dma_start(out=sgn[:], in_=T["c_sgn"]), writes=["sgn"])

        def chunk(ci):
            cols = slice(ci * CW, (ci + 1) * CW)
            sch.dma("sp", "posi", lambda e: e.dma_start(out=posi[:], in_=T["pos"][:, cols].partition_broadcast(64)),
                    writes=["posi"])
            sch.op("dve", lambda e: e.tensor_copy(out=ang[:], in_=posi[:]), reads=["posi"], writes=["ang"])
            sch.op("dve", lambda e: e.tensor_scalar(out=ang[:], in0=ang[:], scalar1=invf[:, 0:1], scalar2=None,
                                                    op0=ALU.mult), reads=["ang", "invf"], writes=["ang"])
            for which, dst in (("sin", cs2), ("cos", cs1)):
                off = 0.0 if which == "sin" else PI / 2
                sch.op("dve", lambda e, off=off: e.tensor_scalar(out=u[:], in0=ang[:], scalar1=off,
                                                                scalar2=1.0 / (2 * PI), op0=ALU.add, op1=ALU.mult),
                       reads=["ang"], writes=["rt_u"])
                sch.op("dve", lambda e: e.tensor_copy(out=ki[:], in_=u[:]), reads=["rt_u"], writes=["rt_ki"])
                sch.op("dve", lambda e: e.tensor_copy(out=u[:], in_=ki[:]), reads=["rt_ki"], writes=["rt_u"])
                sch.op("dve", lambda e: e.scalar_tensor_tensor(out=r[:], in0=u[:], scalar=-2 * PI, in1=ang[:],
                                                              op0=ALU.mult, op1=ALU.add),
                       reads=["rt_u", "ang"], writes=["rt_r"])
                if off != 0.0:
                    sch.op("dve", lambda e, off=off: e.tensor_scalar(out=r[:], in0=r[:], scalar1=off, scalar2=None,
                                                                    op0=ALU.add), reads=["rt_r"], writes=["rt_r"])
                sch.op("dve", lambda e: e.tensor_scalar(out=m[:], in0=r[:], scalar1=PI, scalar2=2 * PI,
                                                        op0=ALU.is_gt, op1=ALU.mult), reads=["rt_r"], writes=["rt_m"])
                sch.op("dve", lambda e: e.tensor_tensor(out=r[:], in0=r[:], in1=m[:], op=ALU.subtract),
                       reads=["rt_r", "rt_m"], writes=["rt_r"])
                sch.op("dve", lambda e: e.tensor_scalar(out=m[:], in0=r[:], scalar1=-PI, scalar2=2 * PI,
                                                        op0=ALU.is_lt, op1=ALU.mult), reads=["rt_r"], writes=["rt_m"])
                sch.op("dve", lambda e: e.tensor_tensor(out=r[:], in0=r[:], in1=m[:], op=ALU.add),
                       reads=["rt_r", "rt_m"], writes=["rt_r"])
                sch.op("dve", lambda e: e.tensor_scalar(out=r[:], in0=r[:], scalar1=-PI_LO, scalar2=PI_LO,
                                                        op0=ALU.max, op1=ALU.min), reads=["rt_r"], writes=["rt_r"])
                sch.op("act", lambda e, dst=dst: e.activation(out=dst[:, cols], in_=r[:], func=AF.Sin),
                       reads=["rt_r"], writes=[f"cs{ci}"])
            sch.op("dve", lambda e: e.tensor_scalar(out=cs2[:, cols], in0=cs2[:, cols], scalar1=sgn[:, 0:1],
                                                    scalar2=None, op0=ALU.mult), reads=[f"cs{ci}", "sgn"], writes=[f"cs{ci}"])

        return [sch.record(lambda ci=ci: chunk(ci)) for ci in range(S // CW)]

    def proj_phase(self, es, L, x_src, win, winsw, kmemT, vmem, P):
        nc, sch, S, T = self.nc, self.sch, self.S, self.T
        NG = self.NG
        ident, ones = self.C["ident"], self.C["ones"]
        V = {}
        V["xs"] = xs = [self.sb(es, f"xs{i}", [128, D], F32) for i in range(4)]
        V["xn"] = xn = [self.sb(es, f"xn{i}", [128, D], BF16) for i in range(4)]
        V["xnT"] = xnT = [self.sb(es, f"xnT{i}", [128, 8, 512], BF16) for i in range(2)]
        V["junk"] = junk = self.sb(es, "junk", [128, D], BF16)
        V["ss"] = ss = [self.sb(es, f"ss{i}", [128, 1], F32) for i in range(4)]
        V["gt"] = gt = [self.sb(es, f"gt{i}", [128, 512], BF16) for i in range(3)]
        V["gmem"] = gmem = [self.sb(es, f"gmem{i}", [128, 2, 512], BF16) for i in range(2)]
        V["qmT"] = qmT = self.sb(es, "qmT", [128, 2, 512], BF16)
        V["pTm"] = pTm = [self.sb(es, f"pTm{i}", [128, 512], BF16) for i in range(3)]
        V["rz"] = rz = self.sb(es, "rzm", [128, 512], F32)
        V["om"] = om = self.sb(es, "om", [128, 512], F32)
        V["cgm"] = cgm = [self.sb(es, f"cgm{i}", [128, 512], BF16) for i in range(2)]
        NPG = 4 if L == "A" else 5
        V["ptr"] = ptr = [self.ps(es, f"ptr{i}", [128, 1024], BF16) for i in range(1)]
        V["pg"] = pg = [self.ps(es, f"pg{i}", [128, 512], F32) for i in range(NPG)]
        if L == "A":
            V["pss"] = self.ps(es, "pss", [128, 512], F32)
        V["pom"] = self.ps(es, "pom", [128, 512], F32)
        V["pzm"] = self.ps(es, "pzm", [128, 512], F32)
        V["rpg"] = rpg = Rot("pg", NPG)
        V["rptr"] = Rot("ptr", 1)
        V["rpT"] = Rot("pTm", 3)
        V["rgt"] = Rot("gt", 3)
        V["rcgm"] = Rot("cgm", 2)
        V.update(L=L, win=win, winsw=winsw, kmemT=kmemT, vmem=vmem, P=P, vmask=vmem, onesh=self.C["onesh"])
        if L == "A":
            V["craw"] = self.sb(es, "craw", [128, 5, 512], F32)
            V["sq"] = self.sb(es, "sq", [128, 5, 512], BF16)
            V["rinv"] = self.sb(es, "rinv", [128, 512], F32)
            V["t1"] = self.sb(es, "rp_t1", [64, 512], F32)
            V["t2"] = self.sb(es, "rp_t2", [64, 512], F32)
            V["lat"] = [self.sb(es, f"lat{i}", [128, 5, 512], BF16) for i in range(2)]
            V["kro"] = [self.sb(es, f"kro{i}", [64, 512], BF16) for i in range(2)]
            V["GATE0"], V["QM0"] = 960, 704
        else:
            V["GATE0"], V["QM0"] = 1800, 1544
            self.proj_B_alloc(es, V)

        def load_x(t):
            sl = t % 4
            sch.dma("sp", f"xs{sl}", lambda e: e.dma_start(out=xs[sl][:], in_=x_src[t * 128:(t + 1) * 128, :]),
                    reads=["xsrc_dram"], writes=[f"xs{sl}"])
        V["load_x"] = load_x

        def fm_matmul(sl, wt, c0, msz, wname):
            k = rpg.next()
            def f(e):
                for c in range(8):
                    i = e.matmul(pg[k][0:msz, :], lhsT=wt[:, c, c0:c0 + msz], rhs=xnT[sl][:, c, :],
                                 start=(c == 0), stop=(c == 7))
                return i
            sch.op("pe", f, reads=[wname, f"xnT{sl}"], writes=[f"pg{k}"])
            return k
        V["fm_matmul"] = fm_matmul

        for t in range(4):
            load_x(t)
        rope = P.get("rope") if L == "A" else None
        if rope:
            sch.zip_emit([rope[0]])
        for g in range(NG):
            body = self._proj_group(g, V)
            extra = rope[g + 1] if (rope and g + 1 < len(rope)) else []
            sch.zip_emit([body, extra])

    def _proj_group(self, g, V):
        sch, T, NT = self.sch, self.T, self.NT
        L, xs, xn, xnT, junk, ss, gt, gmem, qmT, pTm, rz, om, cgm = [V[k] for k in (
            "L", "xs", "xn", "xnT", "junk", "ss", "gt", "gmem", "qmT", "pTm", "rz", "om", "cgm")]
        ptr, pg, pom, pzm, rpg, rptr, rpT, rgt, rcgm = [V[k] for k in (
            "ptr", "pg", "pom", "pzm", "rpg", "rptr", "rpT", "rgt", "rcgm")]
        win, kmemT, vmem, fm_matmul, GATE0, QM0, load_x = [V[k] for k in (
            "win", "kmemT", "vmem", "fm_matmul", "GATE0", "QM0", "load_x")]
        ident, ones = self.C["ident"], self.C["ones"]
        sl = g % 2
        tok = slice(g * 512, (g + 1) * 512)

        def front_a(t):
            xsl = t % 4
            sch.op("act", lambda e: e.activation(out=junk[:], in_=xs[xsl][:], func=AF.Square, accum_out=ss[xsl][:]),
                   reads=[f"xs{xsl}"], writes=[f"ss{xsl}"])
            sch.op("act", lambda e: e.activation(out=ss[xsl][:], in_=ss[xsl][:], func=AF.Ln, scale=1.0 / D, bias=EPS),
                   reads=[f"ss{xsl}"], writes=[f"ss{xsl}"])
            sch.op("act", lambda e: e.activation(out=ss[xsl][:], in_=ss[xsl][:], func=AF.Exp, scale=-0.5),
                   reads=[f"ss{xsl}"], writes=[f"ss{xsl}"])
            if t % 2 == 0:
                sch.op("dve", lambda e: e.tensor_scalar(out=xn[xsl][:], in0=xs[xsl][:], scalar1=ss[xsl][:, 0:1],
                                                        scalar2=None, op0=ALU.mult),
                       reads=[f"xs{xsl}", f"ss{xsl}"], writes=[f"xn{xsl}"])
            else:
                sch.op("act", lambda e: e.activation(out=xn[xsl][:], in_=xs[xsl][:], func=AF.Copy,
                                                     scale=ss[xsl][:, 0:1]),
                       reads=[f"xs{xsl}", f"ss{xsl}"], writes=[f"xn{xsl}"])

        def front_b(t):
            nsl = t % 4
            j = t % 4
            tsl = (t // 4) % 2
            k = rptr.next()
            def tr(e):
                for c in range(8):
                    i = e.transpose(out=ptr[k][:, c * 128:(c + 1) * 128], in_=xn[nsl][:, c * 128:(c + 1) * 128],
                                    identity=ident[:])
                return i
            sch.op("pe", tr, reads=[f"xn{nsl}", "ident"], writes=[f"ptr{k}"])
            src = ptr[k][:].rearrange("p (c t) -> p c t", c=8)
            dst = xnT[tsl][:, :, j * 128:(j + 1) * 128]
            if j % 2 == 0:
                sch.op("act", lambda e: e.activation(out=dst, in_=src, func=AF.Copy),
                       reads=[f"ptr{k}"], writes=[f"xnT{tsl}"])
            else:
                sch.op("dve", lambda e: e.tensor_copy(out=dst, in_=src), reads=[f"ptr{k}"], writes=[f"xnT{tsl}"])

        nxt = [4 * (g + 1) + j for j in range(4)] if g + 1 < self.NG else []
        wn = f"win{L}"

        def gate_tile(m):
            k = fm_matmul(sl, win, GATE0 + m * 128, 128, wn)
            if m >= 6:
                sch.op("act", lambda e: e.activation(out=gmem[sl][:, m - 6, :], in_=pg[k][:], func=AF.Silu),
                       reads=[f"pg{k}"], writes=[f"gmem{sl}"])
            else:
                gs = rgt.next()
                sch.op("act", lambda e: e.activation(out=gt[gs][:], in_=pg[k][:], func=AF.Silu),
                       reads=[f"pg{k}"], writes=[f"gt{gs}"])
                dst = T["gT"][m * 128:(m + 1) * 128, tok]
                sch.dma(STQ, f"gt{gs}", lambda e: e.dma_start(out=dst, in_=gt[gs][:]),
                        reads=[f"gt{gs}"], writes=["gT_dram"])

        def qm_tile(m):
            k = fm_matmul(sl, win, QM0 + m * 128, 128, wn)
            sch.op("dve", lambda e: e.tensor_copy(out=qmT[:, m, :], in_=pg[k][:]), reads=[f"pg{k}"], writes=["qmT"])

        vmask, onesh = V["vmask"], V["onesh"]
        units = [(pr, hb, mt) for pr in range(2) for hb in range(2) for mt in range(2)]
        kq = {}

        def m_qk(i):
            pr, hb, mt = units[i]
            p0 = hb * 64
            k = rpg.next()
            kq[i] = k
            sch.op("pe", lambda e: e.matmul(pg[k][:], lhsT=kmemT[p0:p0 + 64, pr, mt * 128:(mt + 1) * 128],
                                            rhs=qmT[p0:p0 + 64, pr, :], start=True, stop=True),
                   reads=["kmemT", "qmT"], writes=[f"pg{k}"])

        def m_rest(i):
            pr, hb, mt = units[i]
            k = kq[i]
            n = hb * 2 + mt
            kp = rpT.next()
            sch.op("act", lambda e: e.activation(out=pTm[kp][:], in_=pg[k][:], func=AF.Exp, scale=SCALE_M),
                   reads=[f"pg{k}"], writes=[f"pTm{kp}"])
            def pv(e):
                e.matmul(pom[:], lhsT=vmask[:, mt, 2 * pr + hb, :], rhs=pTm[kp][:], start=(n == 0), stop=(n == 3))
                return e.matmul(pzm[:], lhsT=onesh[:, hb, :], rhs=pTm[kp][:], start=(n == 0), stop=(n == 3))
            sch.op("pe", pv, reads=["vmask", "onesh", f"pTm{kp}"], writes=["pom", "pzm"])
            if n == 3:
                cs_ = rcgm.next()
                sch.op("act", lambda e: e.activation(out=rz[:], in_=pzm[:], func=AF.Ln), reads=["pzm"], writes=["rzm"])
                sch.op("act", lambda e: e.activation(out=rz[:], in_=rz[:], func=AF.Exp, scale=-1.0),
                       reads=["rzm"], writes=["rzm"])
                sch.op("dve", lambda e: e.tensor_tensor(out=om[:], in0=pom[:], in1=rz[:], op=ALU.mult),
                       reads=["pom", "rzm"], writes=["om"])
                sch.op("pool", lambda e: e.tensor_tensor(out=cgm[cs_][:], in0=om[:], in1=gmem[sl][:, pr, :],
                                                         op=ALU.mult),
                       reads=["om", f"gmem{sl}"], writes=[f"cgm{cs_}"])
                if L == "A":
                    dst = T["cgT"][768 + pr * 128:768 + (pr + 1) * 128, tok]
                else:
                    dst = T["cgTB"][8 + pr, :, tok]
                sch.dma(STQ, f"cgm{cs_}", lambda e: e.dma_start(out=dst, in_=cgm[cs_][:]),
                        reads=[f"cgm{cs_}"], writes=["cgT_dram"])

        def mem_attention():
            MAH = 2
            for i in range(MAH):
                m_qk(i)
            for i in range(len(units)):
                if i + MAH < len(units):
                    m_qk(i + MAH)
                m_rest(i)

        def seg1():
            if g == 0:
                for j in range(4):
                    front_a(j)
                    front_b(j)
            for t in nxt:
                load_x(t)
            if L == "B":
                self.proj_B_extra(g, sl, tok, V, part=0)

        def X():
            if L == "A":
                for m in range(8):
                    gate_tile(m)
                for m in range(2):
                    qm_tile(m)
                self.proj_A_latents(g, sl, tok, V)
            else:
                self.proj_B_gates(g, sl, tok, V)
                self.proj_B_extra(g, sl, tok, V, part=3)
                for m in range(2):
                    qm_tile(m)
                self.proj_B_extra(g, sl, tok, V, part=2)
            mem_attention()

        def Y():
            if L == "B":
                self.proj_B_extra(g, sl, tok, V, part=1)
            for t in nxt:
                front_a(t)

        def seg3():
            for t in nxt:
                front_b(t)
            if L == "B":
                self.proj_B_qkv(g, sl, tok, V)

        l1 = sch.record(seg1)
        lx = sch.record(X)
        ly = sch.record(Y)
        l3 = sch.record(seg3)
        return l1 + Sched.merge([lx, ly]) + l3

    def proj_A_latents(self, g, sl, tok, V):
        sch, T = self.sch, self.T
        ones = self.C["ones"]
        win, winsw, fm_matmul, pg, pss, craw, sq, rinv, t1, t2, P, lat, kro = [V[k] for k in (
            "win", "winsw", "fm_matmul", "pg", "pss", "craw", "sq", "rinv", "t1", "t2", "P", "lat", "kro")]
        cs1, cs2 = P["cs1"], P["cs2"]
        SK = _os.environ.get("SKIP", "").split(",")
        for base, nt_, dim in ((0, 3, 384.0), (3, 2, 256.0)):
            for m in range(nt_):
                i5 = base + m
                k = fm_matmul(sl, win, i5 * 128, 128, "winA")
                sch.op("dve", lambda e, i5=i5, k=k: e.tensor_copy(out=craw[:, i5, :], in_=pg[k][:]),
                       reads=[f"pg{k}"], writes=[f"craw{i5}"])
                sch.op("act", lambda e, i5=i5, k=k: e.activation(out=sq[:, i5, :], in_=craw[:, i5, :], func=AF.Square),
                       reads=[f"craw{i5}"], writes=[f"sq{i5}"])
            def f(e, base=base, nt_=nt_):
                for m in range(nt_):
                    i = e.matmul(pss[:], lhsT=ones[:], rhs=sq[:, base + m, :], start=(m == 0), stop=(m == nt_ - 1))
                return i
            sch.op("pe", f, reads=["ones"] + [f"sq{base + m}" for m in range(nt_)], writes=["pss"])
            sch.op("act", lambda e, dim=dim: e.activation(out=rinv[:], in_=pss[:], func=AF.Ln, scale=1.0 / dim,
                                                         bias=EPS), reads=["pss"], writes=["rinv"])
            sch.op("act", lambda e: e.activation(out=rinv[:], in_=rinv[:], func=AF.Exp, scale=-0.5),
                   reads=["rinv"], writes=["rinv"])
            for m in range(nt_):
                eng = "dve" if (m % 2 == 0 or "latpool" in SK) else "pool"
                sch.op(eng, lambda e, m=m, base=base: e.tensor_tensor(
                    out=lat[sl][:, base + m, :], in0=craw[:, base + m, :], in1=rinv[:], op=ALU.mult),
                    reads=[f"craw{base + m}", "rinv"], writes=[f"lat{sl}"])
        if "latdma" not in SK:
          sch.dma(STQ, f"latq{sl}", lambda e: e.dma_start(
            out=T["cqT"][:, tok].rearrange("(c p) t -> p c t", p=128), in_=lat[sl][:, 0:3, :]),
            reads=[f"lat{sl}"], writes=["lat_dram"])
        if "latdma" not in SK:
          sch.dma(STQ, f"latkv{sl}", lambda e: e.dma_start(
            out=T["ckvT"][:, tok].rearrange("(c p) t -> p c t", p=128), in_=lat[sl][:, 3:5, :]),
            reads=[f"lat{sl}"], writes=["lat_dram"])
        if "krope" in SK:
            return
        kn_ = fm_matmul(sl, win, 640, 64, "winA")
        ks_ = fm_matmul(sl, winsw, 0, 64, "winswA")
        sch.op("dve", lambda e: e.tensor_tensor(out=t1[:], in0=pg[kn_][0:64, :], in1=cs1[:, tok], op=ALU.mult),
               reads=[f"pg{kn_}", f"cs{(g * 512) // min(self.S, 1024)}"], writes=["rp_t1"])
        sch.op("dve", lambda e: e.tensor_tensor(out=t2[:], in0=pg[ks_][0:64, :], in1=cs2[:, tok], op=ALU.mult),
               reads=[f"pg{ks_}", f"cs{(g * 512) // min(self.S, 1024)}"], writes=["rp_t2"])
        sch.op("pool", lambda e: e.tensor_tensor(out=kro[sl][:], in0=t1[:], in1=t2[:], op=ALU.add),
               reads=["rp_t1", "rp_t2"], writes=[f"kro{sl}"])
        sch.dma(STQ, f"kro{sl}", lambda e: e.dma_start(out=T["krT"][:, tok], in_=kro[sl][:]),
                reads=[f"kro{sl}"], writes=["lat_dram"])

    def attn_phase(self, cs1, cs2):
        nc, sch, S, T = self.nc, self.sch, self.S, self.T
        NG, NT = self.NG, self.NT
        C = self.C
        ident, ones, maskT = C["ident"], C["ones"], C["maskT"]
        es = sch.begin()
        cqnT = self.sb(es, "cqnT", [128, 3, S], BF16)
        ckvnT = self.sb(es, "ckvnT", [128, 2, S], BF16)
        kropeT = self.sb(es, "kropeT", [128, S], BF16)
        sch.op("pool", lambda e: e.memset(kropeT[64:128, :], 0.0), writes=["kropeTz"])
        wuq = self.sb(es, "wuq", [128, 3, 1152], BF16)
        wuqsw = self.sb(es, "wuqsw", [128, 3, 6, 64], BF16)
        wukv = self.sb(es, "wukv", [128, 2, 1536], BF16)
        self.wstage(es, 1536)
        self.load_weight(es, wuq, T["a_w_uq"], 1152, [(0, 128), (128, 128), (256, 128)], T["a_q_a_g"], tag="wuq")
        for h in range(6):
            sch.op("dve", lambda e, h=h: e.tensor_copy(out=wuqsw[:, :, h, 0:32],
                                                      in_=wuq[:, :, h * 192 + 160:h * 192 + 192]),
                   reads=["wuq"], writes=["wuqsw"])
            sch.op("dve", lambda e, h=h: e.tensor_copy(out=wuqsw[:, :, h, 32:64],
                                                      in_=wuq[:, :, h * 192 + 128:h * 192 + 160]),
                   reads=["wuq"], writes=["wuqsw"])
        self.load_weight(es, wukv, T["a_w_ukv"], 1536, [(0, 128), (128, 128)], T["a_kv_a_g"], tag="wukv")
        for g in range(NG):
            tk = slice(g * 512, (g + 1) * 512)
            sch.dma("sp", f"ldq{g % 2}", lambda e, tk=tk: e.dma_start(
                out=cqnT[:, :, tk], in_=T["cqT"][:, tk].rearrange("(c p) t -> p c t", p=128)), writes=["cqnT"])
            sch.dma("sp", f"ldkv{g % 2}", lambda e, tk=tk: e.dma_start(
                out=ckvnT[:, :, tk], in_=T["ckvT"][:, tk].rearrange("(c p) t -> p c t", p=128)), writes=["ckvnT"])
        sch.dma("sp", "ldkr", lambda e: e.dma_start(out=kropeT[0:64, :], in_=T["krT"]), writes=["kropeT"])
        qn = [self.sb(es, f"qn{i}", [128, S], BF16) for i in range(2)]
        qr = [self.sb(es, f"qr{i}", [128, S], BF16) for i in range(2)]
        for i_ in range(2):
            sch.op("pool", lambda e, i_=i_: e.memset(qr[i_][64:128, :], 0.0), writes=[f"qrz{i_}"])
        kn = [self.sb(es, f"kn{i}", [128, S], BF16) for i in range(2)]
        vv = [self.sb(es, f"vv{i}", [128, NT, 128], BF16) for i in range(2)]
        pT = [self.sb(es, f"pT{i}", [128, 512], BF16) for i in range(3)]
        t1 = self.sb(es, "at_t1", [64, 512], F32)
        t2 = self.sb(es, "at_t2", [64, 512], F32)
        rz = [self.sb(es, f"rz{i}", [128, 512], F32) for i in range(2)]
        of = [self.sb(es, f"of{i}", [128, 512], F32) for i in range(2)]
        gl = [self.sb(es, f"gl{i}", [128, 512], BF16) for i in range(2)]
        cg = [self.sb(es, f"cg{i}", [128, 512], BF16) for i in range(2)]
        NSC = 4
        psc = [self.ps(es, f"psc{i}", [128, 512], F32) for i in range(NSC)]
        po = [self.ps(es, f"po{i}", [128, 512], F32) for i in range(2)]
        pz = [self.ps(es, f"pz{i}", [128, 512], F32) for i in range(2)]
        pq = psc
        zacc = [[self.sb(es, f"zacc{i}_{p}", [128, 512], F32) for p in range(2)] for i in range(2)]
        onesf = self.sb(es, "onesf_a", [128, 128], F32)
        sch.op("dve", lambda e: e.memset(onesf[:], 1.0), writes=["onesf"])
        rpq = Rot("pq", NSC)

        def prod_group(h, hs, g):
            tok = slice(g * 512, (g + 1) * 512)
            k = rpq.next()
            def f(e):
                for c in range(3):
                    i = e.matmul(pq[k][:], lhsT=wuq[:, c, h * 192:h * 192 + 128], rhs=cqnT[:, c, tok],
                                 start=(c == 0), stop=(c == 2))
                return i
            sch.op("pe", f, reads=["wuq", "cqnT"], writes=[f"psc{k}"])
            sch.op("act", lambda e: e.activation(out=qn[hs][:, tok], in_=pq[k][:], func=AF.Copy),
                   reads=[f"psc{k}"], writes=[f"qn{hs}_{g}"])
            k1 = rpq.next()
            def f1(e):
                for c in range(3):
                    i = e.matmul(pq[k1][0:64, :], lhsT=wuq[:, c, h * 192 + 128:h * 192 + 192], rhs=cqnT[:, c, tok],
                                 start=(c == 0), stop=(c == 2))
                return i
            sch.op("pe", f1, reads=["wuq", "cqnT"], writes=[f"psc{k1}"])
            sch.op("dve", lambda e: e.tensor_tensor(out=t1[:], in0=pq[k1][0:64, :], in1=cs1[:, tok], op=ALU.mult),
                   reads=[f"psc{k1}", "cs"], writes=["at_t1"])
            k2 = rpq.next()
            def f2(e):
                for c in range(3):
                    i = e.matmul(pq[k2][0:64, :], lhsT=wuqsw[:, c, h, :], rhs=cqnT[:, c, tok],
                                 start=(c == 0), stop=(c == 2))
                return i
            sch.op("pe", f2, reads=["wuqsw", "cqnT"], writes=[f"psc{k2}"])
            sch.op("dve", lambda e: e.tensor_tensor(out=t2[:], in0=pq[k2][0:64, :], in1=cs2[:, tok], op=ALU.mult),
                   reads=[f"psc{k2}", "cs"], writes=["at_t2"])
            sch.op("pool", lambda e: e.tensor_tensor(out=qr[hs][0:64, tok], in0=t1[:], in1=t2[:], op=ALU.add),
                   reads=["at_t1", "at_t2"], writes=[f"qr{hs}_{g}"])
            k3 = rpq.next()
            def f3(e):
                for c in range(2):
                    i = e.matmul(pq[k3][:], lhsT=wukv[:, c, h * 256:h * 256 + 128], rhs=ckvnT[:, c, tok],
                                 start=(c == 0), stop=(c == 1))
                return i
            sch.op("pe", f3, reads=["wukv", "ckvnT"], writes=[f"psc{k3}"])
            sch.op("act", lambda e: e.activation(out=kn[hs][:, tok], in_=pq[k3][:], func=AF.Copy),
                   reads=[f"psc{k3}"], writes=[f"kn{hs}_{g}"])
            k4 = rpq.next()
            def f4(e):
                for j in range(4):
                    t0 = g * 512 + j * 128
                    for c in range(2):
                        i = e.matmul(pq[k4][:, j * 128:(j + 1) * 128], lhsT=ckvnT[:, c, t0:t0 + 128],
                                     rhs=wukv[:, c, h * 256 + 128:h * 256 + 256], start=(c == 0), stop=(c == 1))
                return i
            sch.op("pe", f4, reads=["wukv", "ckvnT"], writes=[f"psc{k4}"])
            sch.op("dve", lambda e: e.tensor_copy(
                out=vv[hs][:, g * 4:(g + 1) * 4, :].rearrange("p j d -> p (j d)"), in_=pq[k4][:]),
                reads=[f"psc{k4}"], writes=[f"vv{hs}_{g}"])

        def emit_qk(h, hs, j, uu, kt):
            q0 = j * 512
            r = kt - 4 * j
            c0 = r * 128 if r > 0 else 0
            sb_ = uu % NSC
            def f(e):
                e.matmul(psc[sb_][:, c0:512], lhsT=kn[hs][:, kt * 128:(kt + 1) * 128],
                         rhs=qn[hs][:, q0 + c0:q0 + 512], start=True, stop=False)
                i = e.matmul(psc[sb_][:, c0:512], lhsT=kropeT[:, kt * 128:(kt + 1) * 128],
                             rhs=qr[hs][:, q0 + c0:q0 + 512], start=False, stop=(r < 0))
                if r >= 0:
                    i = e.matmul(psc[sb_][:, c0:c0 + 128], lhsT=ident[:], rhs=maskT[:], start=False, stop=True)
                return i
            sch.op("pe", f, reads=[f"kn{hs}_{kt // 4}", f"qn{hs}_{j}", f"qr{hs}_{j}", f"qrz{hs}", "kropeT", "kropeTz",
                                   "ident", "maskT"], writes=[f"psc{sb_}"])
            return c0

        def emit_rest(h, hs, j, js, uu, kt, c0, last):
            sb_ = uu % NSC
            pb = uu % 3
            sch.op("act", lambda e: e.activation(out=pT[pb][:, c0:512], in_=psc[sb_][:, c0:512], func=AF.Exp,
                                                 scale=SCALE_A),
                   reads=[f"psc{sb_}"], writes=[f"pT{pb}"])
            def pv(e):
                return e.matmul(po[js][:, c0:512], lhsT=vv[hs][:, kt, :], rhs=pT[pb][:, c0:512],
                                start=(kt == 0), stop=last)
            sch.op("pe", pv, reads=[f"vv{hs}_{kt // 4}", f"pT{pb}"], writes=[f"po{js}"])
            if kt % 6 == 3:
                sch.op("pe", lambda e: e.matmul(pz[js][:, c0:512], lhsT=ones[:], rhs=pT[pb][:, c0:512],
                                                start=(kt == 3), stop=False, skip_group_check=True),
                       reads=["ones", f"pT{pb}"], writes=[f"pz{js}"])
                return
            par = 1 if kt % 6 == 1 else 0
            eng = "dve" if par == 0 else "pool"
            if kt < 2:
                if c0 > 0:
                    sch.op(eng, lambda e: e.memset(zacc[js][par][:, 0:c0], 0.0), writes=[f"zacc{js}_{par}"])
                sch.op(eng, lambda e: e.tensor_copy(out=zacc[js][par][:, c0:512], in_=pT[pb][:, c0:512]),
                       reads=[f"pT{pb}"], writes=[f"zacc{js}_{par}"])
            else:
                sch.op(eng, lambda e: e.tensor_tensor(out=zacc[js][par][:, c0:512], in0=zacc[js][par][:, c0:512],
                                                      in1=pT[pb][:, c0:512], op=ALU.add),
                       reads=[f"pT{pb}", f"zacc{js}_{par}"], writes=[f"zacc{js}_{par}"])

        def finalize(h, j, js):
            q0 = j * 512
            sch.op("dve", lambda e: e.tensor_tensor(out=zacc[js][0][:], in0=zacc[js][0][:], in1=zacc[js][1][:],
                                                    op=ALU.add),
                   reads=[f"zacc{js}_0", f"zacc{js}_1"], writes=[f"zacc{js}_0"])
            sch.op("pe", lambda e: e.matmul(pz[js][:], lhsT=onesf[:], rhs=zacc[js][0][:], start=False, stop=True,
                                            skip_group_check=True),
                   reads=["onesf", f"zacc{js}_0"], writes=[f"pz{js}"])
            sch.op("act", lambda e: e.activation(out=rz[js][:], in_=pz[js][:], func=AF.Ln),
                   reads=[f"pz{js}"], writes=[f"rz{js}"])
            sch.op("act", lambda e: e.activation(out=rz[js][:], in_=rz[js][:], func=AF.Exp, scale=-1.0),
                   reads=[f"rz{js}"], writes=[f"rz{js}"])
            sch.op("dve", lambda e: e.tensor_tensor(out=of[js][:], in0=po[js][:], in1=rz[js][:], op=ALU.mult),
                   reads=[f"po{js}", f"rz{js}"], writes=[f"of{js}"])
            sch.op("pool", lambda e: e.tensor_tensor(out=cg[js][:], in0=of[js][:], in1=gl[js][:], op=ALU.mult),
                   reads=[f"of{js}", f"gl{js}"], writes=[f"cg{js}"])
            sch.dma(STQ, f"cg{js}", lambda e: e.dma_start(
                out=T["cgT"][h * 128:(h + 1) * 128, q0:q0 + 512], in_=cg[js][:]),
                reads=[f"cg{js}"], writes=["cgT_dram"])

        def gate_load(h, j, js):
            q0 = j * 512
            sch.dma("sp", f"gl{js}", lambda e: e.dma_start(
                out=gl[js][:], in_=T["gT"][h * 128:(h + 1) * 128, q0:q0 + 512]),
                reads=["gT_dram"], writes=[f"gl{js}"])

        fin = 0
        u = 0
        AH = 2
        for h in range(6):
            hs = h % 2
            for g in range(NG):
                prod_group(h, hs, g)
            flat = []
            jsl = {}
            for j in range(NG):
                jsl[j] = fin % 2
                fin += 1
                n = 4 * j + 4
                for kt in range(n):
                    flat.append((j, jsl[j], kt, kt == n - 1))
            c0s = {}
            pending = []
            for a_ in range(min(AH, len(flat))):
                j_, js_, kt_, _ = flat[a_]
                c0s[a_] = emit_qk(h, hs, j_, u + a_, kt_)
            for idx, (j, js, kt, last) in enumerate(flat):
                if kt == 0:
                    gate_load(h, j, js)
                if idx + AH < len(flat):
                    j_, js_, kt_, _ = flat[idx + AH]
                    c0s[idx + AH] = emit_qk(h, hs, j_, u + idx + AH, kt_)
                emit_rest(h, hs, j, js, u + idx, kt, c0s[idx], last)
                if last:
                    pending.append((idx + 2, j, js))
                while pending and pending[0][0] <= idx:
                    _, pj, pjs = pending.pop(0)
                    finalize(h, pj, pjs)
            for _, pj, pjs in pending:
                finalize(h, pj, pjs)
            u += len(flat)
        sch.end()

    def out_weights(self, es, L, wout_ap, postg_ap, kchunks, wout, postg):
        sch = self.sch
        sch.dma("sp", f"postg{L}", lambda e: e.dma_start(out=postg[:], in_=postg_ap.partition_broadcast(128)),
                writes=["postg"])
        self.wstage(es, D)
        self.load_weight(es, wout, wout_ap, D, kchunks, None, tag=f"wout{L}")

    def out_phase(self, L, wout_ap, postg_ap, x_src, x_dst, kchunks, extra_fn=None, pre=None):
        nc, sch, S, T = self.nc, self.sch, self.S, self.T
        NG = self.NG
        nk = len(kchunks)
        ksz = [max(p0 + n for (_, n, p0) in ch) if isinstance(ch, list) else ch[1] for ch in kchunks]
        es = sch.begin()
        if pre is not None:
            wout, postg = pre
        else:
            wout = self.sb(es, f"wout{L}", [128, nk, D], BF16)
            postg = self.sb(es, f"postg{L}", [128, D], F32)
            self.out_weights(es, L, wout_ap, postg_ap, kchunks, wout, postg)
        cgs = [self.sb(es, f"cgs{i}", [128, nk, 512], BF16) for i in range(2)]
        xs = [self.sb(es, f"oxs{i}", [128, 4, D], F32) for i in range(2)]
        tt = [self.sb(es, f"ott{i}", [128, D], F32) for i in range(3)]
        junk = self.sb(es, "ojunk", [128, D], BF16)
        ssq = [self.sb(es, f"ossq{i}", [128, 3], F32) for i in range(3)]
        py = [[self.ps(es, f"py{i}_{n}", [128, 512], F32) for n in range(2)] for i in range(3)]
        cg_ap = T["cgT"] if L == "A" else T["cgTB"]

        def load(g):
            sl = g % 2
            tok = slice(g * 512, (g + 1) * 512)
            if L == "A":
                sch.dma("sp", f"cgs{sl}", lambda e: e.dma_start(
                    out=cgs[sl][:], in_=cg_ap[:, tok].rearrange("(c p) t -> p c t", p=128)),
                    reads=["cgT_dram"], writes=[f"cgs{sl}"])
            else:
                sch.dma("sp", f"cgs{sl}", lambda e: e.dma_start(
                    out=cgs[sl][:, 0:4, :], in_=cg_ap[0:8:2, :, tok].rearrange("c p t -> p c t")),
                    reads=["cgT_dram"], writes=[f"cgs{sl}"])
                for i_, (fa, fb) in enumerate(((1, 3), (5, 7))):
                    sch.dma("sp", f"cgsa{sl}", lambda e, i_=i_, fa=fa: e.dma_start(
                        out=cgs[sl][0:64, 4 + i_, :], in_=cg_ap[fa, 0:64, tok]),
                        reads=["cgT_dram"], writes=[f"cgs{sl}"])
                    sch.dma("sp", f"cgsb{sl}", lambda e, i_=i_, fb=fb: e.dma_start(
                        out=cgs[sl][64:128, 4 + i_, :], in_=cg_ap[fb, 0:64, tok]),
                        reads=["cgT_dram"], writes=[f"cgs{sl}"])
                sch.dma("sp", f"cgsm{sl}", lambda e: e.dma_start(
                    out=cgs[sl][:, 6:8, :], in_=cg_ap[8:10, :, tok].rearrange("c p t -> p c t")),
                    reads=["cgT_dram"], writes=[f"cgs{sl}"])
            sch.dma("sp", f"oxs{sl}", lambda e: e.dma_start(
                out=xs[sl][:], in_=x_src[tok, :].rearrange("(j p) d -> p j d", p=128)),
                reads=["xsrc_dram"], writes=[f"oxs{sl}"])

        def tile(g, sl, j, b):
            def f(e):
                for n in range(2):
                    for c, sz in enumerate(ksz):
                        i = e.matmul(py[b][n][:], lhsT=cgs[sl][0:sz, c, j * 128:(j + 1) * 128],
                                     rhs=wout[0:sz, c, n * 512:(n + 1) * 512], start=(c == 0), stop=(c == nk - 1))
                return i
            sch.op("pe", f, reads=[f"cgs{sl}", f"wout{L}"], writes=[f"py{b}_0", f"py{b}_1"])
            for n in range(2):
                sch.op("act", lambda e, n=n: e.activation(out=junk[:, 0:512], in_=py[b][n][:], func=AF.Square,
                                                         accum_out=ssq[b][:, n:n + 1]),
                       reads=[f"py{b}_{n}"], writes=[f"ossq{b}"])
            sch.op("dve", lambda e: e.tensor_tensor(out=ssq[b][:, 2:3], in0=ssq[b][:, 0:1], in1=ssq[b][:, 1:2],
                                                    op=ALU.add), reads=[f"ossq{b}"], writes=[f"ossq{b}"])
            sch.op("act", lambda e: e.activation(out=ssq[b][:, 2:3], in_=ssq[b][:, 2:3], func=AF.Ln, scale=1.0 / D,
                                                 bias=EPS), reads=[f"ossq{b}"], writes=[f"ossq{b}"])
            sch.op("act", lambda e: e.activation(out=ssq[b][:, 2:3], in_=ssq[b][:, 2:3], func=AF.Exp, scale=-0.5),
                   reads=[f"ossq{b}"], writes=[f"ossq{b}"])
            for n in range(2):
                sch.op("dve", lambda e, n=n: e.scalar_tensor_tensor(
                    out=tt[b][:, n * 512:(n + 1) * 512], in0=py[b][n][:], scalar=ssq[b][:, 2:3],
                    in1=postg[:, n * 512:(n + 1) * 512], op0=ALU.mult, op1=ALU.mult),
                    reads=[f"py{b}_{n}", f"ossq{b}", "postg"], writes=[f"ott{b}"])
            sch.op("pool", lambda e: e.tensor_tensor(out=xs[sl][:, j, :], in0=xs[sl][:, j, :], in1=tt[b][:],
                                                     op=ALU.add),
                   reads=[f"ott{b}", f"oxs{sl}"], writes=[f"oxs{sl}"])

        def store(g):
            sl = g % 2
            tok = slice(g * 512, (g + 1) * 512)
            sch.dma(STQ, f"ost{sl}", lambda e: e.dma_start(
                out=x_dst[tok, :].rearrange("(j p) d -> p j d", p=128), in_=xs[sl][:]),
                reads=[f"oxs{sl}"], writes=["xdst_dram"])

        extra = sch.record(lambda: extra_fn(es)) if extra_fn is not None else []
        npart = NG
        load(0)
        it = 0
        for g in range(NG):
            if g + 1 < NG:
                load(g + 1)
            body = []
            for j in range(4):
                body += sch.record(lambda j=j: tile(g, g % 2, j, it % 3))
                it += 1
            lo, hi = (len(extra) * g) // npart, (len(extra) * (g + 1)) // npart
            sch.zip_emit([body, extra[lo:hi]])
            store(g)
        sch.end()

    def layer_B(self, top):
        nc, sch, S, T = self.nc, self.sch, self.S, self.T
        NT = self.NT
        C = self.C
        ident = C["ident"]
        KC8 = [(c * 128, 128) for c in range(8)]
        for nm, shp, dt in [("gTB", [10, 128, S], BF16), ("cgTB", [10, 128, S], BF16), ("ucTB", [8, 128, S], BF16),
                            ("soTB", [8, 128, S], BF16), ("qTB", [4, 96, S], BF16), ("kTB", [4, 96, S], BF16),
                            ("ktokB", [S, 384], BF16), ("vtokB", [S, 768], BF16), ("ifB", [2, 4, S], F32)]:
            self.dscratch(nm, shp, dt)
        with ExitStack() as LB:
            kch = [[(FT[2 * i][0], 128, 0)] for i in range(4)]
            kch += [[(FT[1][0], 64, 0), (FT[3][0], 64, 64)], [(FT[5][0], 64, 0), (FT[7][0], 64, 64)]]
            kch += [[(768, 128, 0)], [(896, 128, 0)]]
            woutB = self.sb(LB, "woutB", [128, 8, D], BF16)
            postgB = self.sb(LB, "postgB", [128, D], F32)
            wsT = self.sb(LB, "wsT", [128, NT, 4], F32)
            eT = self.sb(LB, "eT", [128, NT, 4], F32)
            abc = self.sb(LB, "abc", [128, 4, NT], F32)
            with ExitStack() as LB1:
                win = self.sb(LB1, "winB", [128, 8, B_COLS], BF16)
                kmemT = self.sb(LB1, "kmemTB", [128, 2, 256], BF16)
                vmem = self.sb(LB1, "vmemB", [128, 2, 4, 128], BF16)
                wq = self.sb(LB1, "wqB", [128, 8, 96], BF16)
                wk = self.sb(LB1, "wkB", [128, 8, 96], BF16)
                wv = self.sb(LB1, "wvB", [128, 8, 192], BF16)
                cvw = self.sb(LB1, "cvw", [128, 8, 4], F32)
                cvb = self.sb(LB1, "cvb", [128, 8], F32)
                def b0(es):
                  if True:
                    self.wstage(es, B_COLS, n=3)
                    wmkv = self.sb(es, "wmkvB", [128, 8, 512], BF16)
                    memnT = self.sb(es, "memnTB", [128, 8, 256], BF16)
                    self.load_weight(es, win, T["b_w_in"], B_COLS, KC8, T["b_pre_g"], tag="winB")
                    self.load_weight(es, wmkv, T["b_w_mem_kv"], 512, KC8, T["b_mem_g"], tag="memBw")
                    self.load_weight(es, wq, T["b_w_q"], 96, FT, None, tag="wqB")
                    self.load_weight(es, wk, T["b_w_k"], 96, FT, None, tag="wkB")
                    self.load_weight(es, wv, T["b_w_v"], 192, FT, None, tag="wvB")
                    sch.op("dve", lambda e: e.memset(cvw[:], 0.0), writes=["cvw"])
                    sch.op("dve", lambda e: e.memset(cvb[:], 0.0), writes=["cvb"])
                    for ft, (r0, sz) in enumerate(FTP):
                        sz = min(sz, 768 - r0)
                        sch.dma("sp", f"cvw{ft % 2}", lambda e, ft=ft, r0=r0, sz=sz: e.dma_start(
                            out=cvw[0:sz, ft, :], in_=T["b_conv_wT"][r0:r0 + sz, :]), writes=["cvw"])
                        sch.dma("sp", f"cvb{ft % 2}", lambda e, ft=ft, r0=r0, sz=sz: e.dma_start(
                            out=cvb[0:sz, ft:ft + 1], in_=T["b_conv_b"][r0:r0 + sz].rearrange("(p o) -> p o", o=1)),
                            writes=["cvb"])
                    self.norm_rows_T(es, T["mem"], NMEM, memnT, "memBn", ident)
                    self.mem_kv(es, wmkv, memnT, kmemT, vmem, "memB")
                KC8 = [(c * 128, 128) for c in range(8)]
                self.out_phase("A", T["a_w_out"], T["a_post_g"], T["x"], T["x1"], KC8)
                es = sch.begin()
                b0(es)
                sch.end()
                es = sch.begin()
                self.proj_phase(es, "B", T["x1"], win, None, kmemT, vmem,
                                dict(wq=wq, wk=wk, wv=wv, cvw=cvw, cvb=cvb))
                sch.end()
            if self.stop == "B1":
                return
            self.gate_phase(wsT, eT, abc)
            if self.stop == "B2":
                return
            self.chunk_phase(wsT, eT, abc, prefetch=lambda es: self.out_weights(
                es, "B", T["b_w_out"], T["b_post_g"], kch, woutB, postgB))
            if self.stop == "B3":
                return
            self.out_phase("B", T["b_w_out"], T["b_post_g"], T["x1"], T["out"], kch, pre=(woutB, postgB))

    def proj_B_alloc(self, es, V):
        V["ug"] = [self.sb(es, f"ug{i}", [128, 8, 515], BF16) for i in range(2)]
        V["acc"] = [self.sb(es, f"cacc{i}", [128, 512], F32) for i in range(8)]
        V["ucg"] = [self.sb(es, f"ucg{i}", [128, 8, 512], BF16) for i in range(2)]
        V["sot"] = [self.sb(es, f"sot{i}", [128, 512], BF16) for i in range(3)]
        V["qkt"] = [self.sb(es, f"qkt{i}", [96, 512], BF16) for i in range(3)]
        V["ktk"] = [self.sb(es, f"ktk{i}", [128, 384], BF16) for i in range(2)]
        V["vtk"] = [self.sb(es, f"vtk{i}", [128, 768], BF16) for i in range(2)]
        V["ift"] = [self.sb(es, f"ift{i}", [4, 512], F32) for i in range(2)]
        V["rsot"] = Rot("sot", 3)
        V["rqkt"] = Rot("qkt", 3)
        V["rktk"] = Rot("ktk", 2)
        V["rvtk"] = Rot("vtk", 2)
        V["rift"] = Rot("ift", 2)
        V["racc"] = Rot("cacc", 2)
        ug = V["ug"]
        self.sch.op("dve", lambda e: e.memset(ug[0][:], 0.0), writes=[f"ug0_{ft}" for ft in range(8)] + ["ugh0"])
        self.sch.op("dve", lambda e: e.memset(ug[1][:], 0.0), writes=[f"ug1_{ft}" for ft in range(8)] + ["ugh1"])

    def proj_B_gates(self, g, sl, tok, V):
        sch, T = self.sch, self.T
        win, fm_matmul, pg, gt, gmem, rgt, GATE0 = [V[k] for k in ("win", "fm_matmul", "pg", "gt", "gmem", "rgt", "GATE0")]
        for ft, (r0, sz) in enumerate(FTP):
            k = fm_matmul(sl, win, GATE0 + r0, sz, "winB")
            gs = rgt.next()
            sch.op("act", lambda e, k=k, gs=gs, sz=sz: e.activation(out=gt[gs][0:sz, :], in_=pg[k][0:sz, :], func=AF.Silu),
                   reads=[f"pg{k}"], writes=[f"gt{gs}"])
            sch.dma(STQ, f"gt{gs}", lambda e, gs=gs, ft=ft, sz=sz: e.dma_start(
                out=T["gTB"][ft, 0:sz, tok], in_=gt[gs][0:sz, :]), reads=[f"gt{gs}"], writes=["gT_dram"])
        for m in range(2):
            k = fm_matmul(sl, win, GATE0 + 768 + m * 128, 128, "winB")
            sch.op("act", lambda e, k=k, m=m: e.activation(out=gmem[sl][:, m, :], in_=pg[k][:], func=AF.Silu),
                   reads=[f"pg{k}"], writes=[f"gmem{sl}"])

    def proj_B_extra(self, g, sl, tok, V, part):
        sch, T = self.sch, self.T
        win, fm_matmul, pg, P = V["win"], V["fm_matmul"], V["pg"], V["P"]
        ug, acc, ucg, sot, ift = [V[k] for k in ("ug", "acc", "ucg", "sot", "ift")]
        rsot, rift = V["rsot"], V["rift"]
        cvw, cvb = P["cvw"], P["cvb"]
        us = g % 2
        if part == 0:
            for ft, (r0, sz) in enumerate(FTP):
                k = fm_matmul(sl, win, r0, sz, "winB")
                sch.op("dve", lambda e, k=k, ft=ft, sz=sz: e.tensor_copy(out=ug[us][0:sz, ft, 3:515], in_=pg[k][0:sz, :]),
                       reads=[f"pg{k}"], writes=[f"ug{us}_{ft}"])
            return
        if part == 1:
            for ft, (r0, sz) in enumerate(FTP):
                sch.op("act", lambda e, ft=ft, sz=sz: e.activation(
                    out=acc[ft][0:sz, :], in_=ug[us][0:sz, ft, 0:512], func=AF.Identity,
                    scale=cvw[0:sz, ft, 0:1], bias=cvb[0:sz, ft:ft + 1]),
                    reads=[f"ug{us}_{ft}", f"ugh{us}", "cvw", "cvb"], writes=[f"cacc{ft}"])
            for j in range(1, 4):
                for ft, (r0, sz) in enumerate(FTP):
                    sch.op("dve", lambda e, ft=ft, sz=sz, j=j: e.scalar_tensor_tensor(
                        out=acc[ft][0:sz, :], in0=ug[us][0:sz, ft, j:j + 512], scalar=cvw[0:sz, ft, j:j + 1],
                        in1=acc[ft][0:sz, :], op0=ALU.mult, op1=ALU.add),
                        reads=[f"ug{us}_{ft}", f"ugh{us}", "cvw", f"cacc{ft}"], writes=[f"cacc{ft}"])
            for ft, (r0, sz) in enumerate(FTP):
                sch.op("act", lambda e, ft=ft, sz=sz: e.activation(out=ucg[us][0:sz, ft, :], in_=acc[ft][0:sz, :],
                                                                  func=AF.Silu),
                       reads=[f"cacc{ft}"], writes=[f"ucg{us}"])
            sch.op("dve", lambda e: e.tensor_copy(out=ug[1 - us][:, :, 0:3], in_=ug[us][:, :, 512:515]),
                   reads=[f"ug{us}_{ft}" for ft in range(8)], writes=[f"ugh{1 - us}"])
            sch.dma(STQ, f"ucst{us}", lambda e: e.dma_start(out=T["ucTB"][:, :, tok].rearrange("f p t -> p f t"),
                                                             in_=ucg[us][:]), reads=[f"ucg{us}"], writes=["uc_dram"])
            return
        if part == 3:
            for ft, (r0, sz) in enumerate(FTP):
                k = fm_matmul(sl, win, 776 + r0, sz, "winB")
                ss_ = rsot.next()
                sch.op("act", lambda e, k=k, ss_=ss_, sz=sz: e.activation(out=sot[ss_][0:sz, :], in_=pg[k][0:sz, :],
                                                                         func=AF.Sigmoid),
                       reads=[f"pg{k}"], writes=[f"sot{ss_}"])
                sch.dma(STQ, f"sot{ss_}", lambda e, ss_=ss_, ft=ft, sz=sz: e.dma_start(
                    out=T["soTB"][ft, 0:sz, tok], in_=sot[ss_][0:sz, :]), reads=[f"sot{ss_}"], writes=["so_dram"])
            return
        for w_ in range(2):
            k = fm_matmul(sl, win, 768 + 4 * w_, 4, "winB")
            is_ = rift.next()
            sch.op("dve", lambda e, k=k, is_=is_: e.tensor_copy(out=ift[is_][:], in_=pg[k][0:4, :]),
                   reads=[f"pg{k}"], writes=[f"ift{is_}"])
            sch.dma(STQ, f"ift{is_}", lambda e, is_=is_, w_=w_: e.dma_start(out=T["ifB"][w_, :, tok], in_=ift[is_][:]),
                    reads=[f"ift{is_}"], writes=["if_dram"])

    def proj_B_qkv(self, g, sl, tok, V):
        sch, T = self.sch, self.T
        pg, P, rpg = V["pg"], V["P"], V["rpg"]
        ug, ucg, qkt, ktk, vtk = [V[k] for k in ("ug", "ucg", "qkt", "ktk", "vtk")]
        rqkt, rktk, rvtk = V["rqkt"], V["rktk"], V["rvtk"]
        wq, wk, wv = P["wq"], P["wk"], P["wv"]
        us = g % 2
        ugr = [f"ug{us}_{ft}" for ft in range(8)]
        for h in range(4):
            for which, wt, wn, dname, scale in (("q", wq, "wqB", "qTB", 1.0), ("k", wk, "wkB", "kTB", SCALE_K)):
                k = rpg.next()
                def f(e, k=k, wt=wt, h=h):
                    for i_, (ft, sz) in enumerate(((2 * h, 128), (2 * h + 1, 64))):
                        ins = e.matmul(pg[k][0:96, :], lhsT=wt[0:sz, ft, :], rhs=ucg[us][0:sz, ft, :],
                                       start=(i_ == 0), stop=(i_ == 1))
                    return ins
                sch.op("pe", f, reads=[wn, f"ucg{us}"], writes=[f"pg{k}"])
                qs = rqkt.next()
                sch.op("act", lambda e, k=k, qs=qs, scale=scale: e.activation(out=qkt[qs][:], in_=pg[k][0:96, :],
                                                                             func=AF.Copy, scale=scale),
                       reads=[f"pg{k}"], writes=[f"qkt{qs}"])
                sch.dma(STQ, f"qkt{qs}", lambda e, qs=qs, dname=dname, h=h: e.dma_start(
                    out=T[dname][h, :, tok], in_=qkt[qs][:]), reads=[f"qkt{qs}"], writes=["qk_dram"])
        for j in range(4):
            t0 = j * 128
            k = rpg.next()
            def f(e, k=k, t0=t0):
                for h in range(4):
                    for i_, (ft, sz) in enumerate(((2 * h, 128), (2 * h + 1, 64))):
                        ins = e.matmul(pg[k][:, h * 96:(h + 1) * 96], lhsT=ucg[us][0:sz, ft, t0:t0 + 128],
                                       rhs=wk[0:sz, ft, :], start=(i_ == 0), stop=(i_ == 1))
                return ins
            sch.op("pe", f, reads=["wkB", f"ucg{us}"], writes=[f"pg{k}"])
            ks = rktk.next()
            sch.op("act", lambda e, k=k, ks=ks: e.activation(out=ktk[ks][:], in_=pg[k][:, 0:384], func=AF.Copy,
                                                            scale=SCALE_K), reads=[f"pg{k}"], writes=[f"ktk{ks}"])
            tk = slice(g * 512 + t0, g * 512 + t0 + 128)
            sch.dma(STQ, f"ktk{ks}", lambda e, ks=ks, tk=tk: e.dma_start(out=T["ktokB"][tk, :], in_=ktk[ks][:]),
                    reads=[f"ktk{ks}"], writes=["qk_dram"])
            vs = rvtk.next()
            for half in range(2):
                k = rpg.next()
                def f(e, k=k, t0=t0, half=half):
                    for hh in range(2):
                        h = half * 2 + hh
                        for i_, (ft, sz) in enumerate(((2 * h, 128), (2 * h + 1, 64))):
                            ins = e.matmul(pg[k][:, hh * 192:(hh + 1) * 192], lhsT=ug[us][0:sz, ft, 3 + t0:3 + t0 + 128],
                                           rhs=wv[0:sz, ft, :], start=(i_ == 0), stop=(i_ == 1))
                    return ins
                sch.op("pe", f, reads=["wvB"] + ugr, writes=[f"pg{k}"])
                sch.op("dve", lambda e, k=k, vs=vs, half=half: e.tensor_copy(
                    out=vtk[vs][:, half * 384:(half + 1) * 384], in_=pg[k][:, 0:384]),
                    reads=[f"pg{k}"], writes=[f"vtk{vs}"])
            sch.dma(STQ, f"vtk{vs}", lambda e, vs=vs, tk=tk: e.dma_start(out=T["vtokB"][tk, :], in_=vtk[vs][:]),
                    reads=[f"vtk{vs}"], writes=["qk_dram"])

    def gate_phase(self, wsT, eT, abc):
        nc, sch, S, T = self.nc, self.sch, self.S, self.T
        NT = self.NT
        L = 128
        NC = S // L
        es = sch.begin()
        it = self.sb(es, "g_i", [4, S], F32)
        ft_ = self.sb(es, "g_f", [4, S], F32)
        spl = self.sb(es, "g_spl", [4, S], F32)
        bneg = self.sb(es, "g_bneg", [4, S], F32)
        gg = self.sb(es, "g_g", [4, S], F32)
        GG = self.sb(es, "g_G", [4, S], F32)
        dd = self.sb(es, "g_d", [4, S], F32)
        ws = self.sb(es, "g_ws", [4, S], F32)
        ee = self.sb(es, "g_e", [4, S], F32)
        da = self.sb(es, "g_da", [4, NC], F32)
        aa = self.sb(es, "g_a", [4, NC], F32)
        gb = self.sb(es, "g_gb", [4, 2], F32)
        ngb = self.sb(es, "g_ngb", [4, 1], F32)
        identf = self.sb(es, "identf", [128, 128], F32)
        sel = self.sb(es, "sel", [4, 4, 128], F32)
        pst = self.ps(es, "g_pst", [128, 512], F32)
        sch.dma("sp", "gi", lambda e: e.dma_start(out=it[:], in_=T["ifB"][0]), writes=["g_i"])
        sch.dma("sp", "gf", lambda e: e.dma_start(out=ft_[:], in_=T["ifB"][1]), writes=["g_f"])
        sch.dma("sp", "gb0", lambda e: e.dma_start(out=gb[:, 0:1], in_=T["b_gate_bias"][0:4, :]), writes=["g_gb"])
        sch.dma("sp", "gb1", lambda e: e.dma_start(out=gb[:, 1:2], in_=T["b_gate_bias"][4:8, :]), writes=["g_gb"])
        sch.dma("sp", "idf", lambda e: e.dma_start(out=identf[:], in_=T["c_identf"]), writes=["identf"])
        sch.dma("sp", "sel", lambda e: e.dma_start(out=sel[:].rearrange("p h m -> p (h m)"), in_=T["c_sel"]),
                writes=["sel"])
        sch.op("dve", lambda e: e.tensor_scalar(out=ngb[:], in0=gb[:, 1:2], scalar1=-1.0, scalar2=None, op0=ALU.mult),
               reads=["g_gb"], writes=["g_ngb"])
        sch.op("act", lambda e: e.activation(out=spl[:], in_=ft_[:], func=AF.Exp, scale=-1.0, bias=ngb[:, 0:1]),
               reads=["g_f", "g_ngb"], writes=["g_spl"])
        sch.op("act", lambda e: e.activation(out=spl[:], in_=spl[:], func=AF.Ln, scale=1.0, bias=1.0),
               reads=["g_spl"], writes=["g_spl"])
        sch.op("dve", lambda e: e.tensor_tensor_scan(out=bneg[:], data0=spl[:], data1=spl[:], initial=0.0,
                                                     op0=ALU.add, op1=ALU.max), reads=["g_spl"], writes=["g_bneg"])
        sch.op("dve", lambda e: e.scalar_tensor_tensor(out=gg[:], in0=it[:], scalar=gb[:, 0:1], in1=bneg[:],
                                                       op0=ALU.add, op1=ALU.add),
               reads=["g_i", "g_gb", "g_bneg"], writes=["g_g"])
        sch.op("dve", lambda e: e.tensor_tensor_scan(out=GG[:], data0=gg[:], data1=gg[:], initial=0.0,
                                                     op0=ALU.max, op1=ALU.max), reads=["g_g"], writes=["g_G"])
        Gv = GG[:].rearrange("p (c l) -> p c l", l=L)
        Rv = Gv[:, :, L - 1:L]
        Rb = Rv.broadcast_to([4, NC, L])
        sch.op("dve", lambda e: e.tensor_tensor(out=dd[:].rearrange("p (c l) -> p c l", l=L),
                                                in0=gg[:].rearrange("p (c l) -> p c l", l=L), in1=Rb,
                                                op=ALU.subtract), reads=["g_g", "g_G"], writes=["g_d"])
        sch.op("act", lambda e: e.activation(out=ws[:], in_=dd[:], func=AF.Exp), reads=["g_d"], writes=["g_ws"])
        sch.op("dve", lambda e: e.tensor_tensor(out=dd[:].rearrange("p (c l) -> p c l", l=L),
                                                in0=bneg[:].rearrange("p (c l) -> p c l", l=L), in1=Rb,
                                                op=ALU.subtract), reads=["g_bneg", "g_G", "g_ws"], writes=["g_d"])
        sch.op("act", lambda e: e.activation(out=ee[:], in_=dd[:], func=AF.Exp), reads=["g_d"], writes=["g_e"])
        Rflat = GG[:, L - 1::L] if False else None
        sch.op("dve", lambda e: e.memset(da[:], 0.0), writes=["g_da"])
        if NC > 1:
            sch.op("dve", lambda e: e.tensor_tensor(out=da[:, 1:NC].unsqueeze(2), in0=Rv[:, 0:NC - 1, :],
                                                    in1=Rv[:, 1:NC, :], op=ALU.subtract),
                   reads=["g_G", "g_da"], writes=["g_da"])
        sch.op("act", lambda e: e.activation(out=aa[:], in_=da[:], func=AF.Exp), reads=["g_da"], writes=["g_a"])
        for nm, src, dst in (("g_ws", ws, wsT), ("g_e", ee, eT)):
            def tr(e, src=src):
                for c in range(NT):
                    i = e.transpose(out=pst[:, c * 4:(c + 1) * 4], in_=src[0:4, c * 128:(c + 1) * 128],
                                    identity=identf[0:4, 0:4])
                return i
            sch.op("pe", tr, reads=[nm, "identf"], writes=["g_pst"])
            sch.op("dve", lambda e, dst=dst: e.tensor_copy(out=dst[:].rearrange("p c h -> p (c h)"),
                                                          in_=pst[:, 0:NT * 4]), reads=["g_pst"], writes=[nm + "T"])
        def ab(e):
            for h in range(4):
                i = e.matmul(pst[:, h * NC:(h + 1) * NC], lhsT=sel[0:4, h, :], rhs=aa[0:4, :], start=True, stop=True)
            return i
        sch.op("pe", ab, reads=["sel", "g_a"], writes=["g_pst"])
        sch.op("dve", lambda e: e.tensor_copy(out=abc[:].rearrange("p h c -> p (h c)"), in_=pst[:, 0:4 * NC]),
               reads=["g_pst"], writes=["abc"])
        sch.end()

    def chunk_phase(self, wsT, eT, abc, prefetch=None):
        nc, sch, S, T = self.nc, self.sch, self.S, self.T
        NT = self.NT
        C_ = self.C
        ident, mask01 = C_["ident"], C_["mask01"]
        es = sch.begin()
        qTc = [self.sb(es, f"qTc{i}", [96, 4, 128], BF16) for i in range(2)]
        kTc = [self.sb(es, f"kTc{i}", [96, 4, 128], BF16) for i in range(2)]
        ktc = [self.sb(es, f"ktc{i}", [128, 384], BF16) for i in range(2)]
        vtc = [self.sb(es, f"vtc{i}", [128, 768], BF16) for i in range(2)]
        soc = [self.sb(es, f"soc{i}", [128, 8, 128], BF16) for i in range(4)]
        ucc = [self.sb(es, f"ucc{i}", [128, 8, 128], BF16) for i in range(4)]
        gtc = [self.sb(es, f"gtc{i}", [128, 8, 128], BF16) for i in range(4)]
        vp = [self.sb(es, f"vp{i}", [128, 4, 193], BF16) for i in range(2)]
        Sm = [self.sb(es, f"Sm{i}", [128, 4, 128], BF16) for i in range(2)]
        Ct = self.sb(es, "Ct", [96, 4, 193], F32)
        Cst = self.sb(es, "Cst", [96, 4, 193], F32)
        Chat = [self.sb(es, f"Chat{i}", [96, 4, 193], BF16) for i in range(2)]
        hout = [self.sb(es, f"hout{i}", [128, 4, 192], F32) for i in range(2)]
        hn = [self.sb(es, f"hn{i}", [128, 896], BF16) for i in range(2)]
        den = [self.sb(es, f"den{i}", [128, 4], F32) for i in range(2)]
        ssh = [self.sb(es, f"ssh{i}", [128, 4], F32) for i in range(2)]
        junk = self.sb(es, "cjunk", [128, 192], BF16)
        m1 = [self.sb(es, f"m1_{i}", [128, 8, 128], F32) for i in range(2)]
        m2 = [self.sb(es, f"m2_{i}", [128, 8, 128], F32) for i in range(2)]
        cgc = [self.sb(es, f"cgc{i}", [128, 8, 128], BF16) for i in range(2)]
        skipb = self.sb(es, "skipb", [128, 8, 128], F32)
        skp = self.sb(es, "skp", [128, 8], F32)
        onesf = self.sb(es, "onesf", [128, 128], F32)
        headg = self.sb(es, "headg", [128, 768], F32)
        pss = [self.ps(es, f"c_pss{i}", [128, 512], F32) for i in range(1)] * 2
        pacc4 = [self.ps(es, f"c_pacc{i}", [128, 512], F32) for i in range(4)]
        pU = [self.ps(es, f"c_pU{i}", [128, 512], F32) for i in range(2)]
        pT = self.ps(es, "c_pT", [128, 1024], BF16)
        sch.dma("sp", "headg", lambda e: e.dma_start(out=headg[:], in_=T["b_head_g"].partition_broadcast(128)),
                writes=["headg"])
        sch.op("dve", lambda e: e.memset(skp[:], 0.0), writes=["skp"])
        for i_ in range(2):
            sch.op("dve", lambda e, i_=i_: e.memset(hn[i_][:], 0.0), writes=[f"hn{i_}"])
        for ft, (r0, sz) in enumerate(FTP):
            sz = min(sz, 768 - r0)
            sch.dma("sp", f"skp{ft % 2}", lambda e, ft=ft, r0=r0, sz=sz: e.dma_start(
                out=skp[0:sz, ft:ft + 1], in_=T["b_skip"][r0:r0 + sz].rearrange("(p o) -> p o", o=1)), writes=["skp"])
        sch.op("dve", lambda e: e.memset(onesf[:], 1.0), writes=["onesf"])
        sch.op("dve", lambda e: e.memset(skipb[:], 0.0), writes=["skipb"])
        sch.op("dve", lambda e: e.memset(Cst[:], 0.0), writes=["Cst"])
        for ft, (r0, sz) in enumerate(FTP):
            sch.op("dve", lambda e, ft=ft, sz=sz: e.tensor_scalar(out=skipb[0:sz, ft, :], in0=onesf[0:sz, :],
                                                                 scalar1=skp[0:sz, ft:ft + 1], scalar2=None,
                                                                 op0=ALU.mult),
                   reads=["onesf", "skp", "skipb"], writes=["skipb"])

        def load(c):
            sl = c % 2
            tk = slice(c * 128, (c + 1) * 128)
            sch.dma("sp", f"qTc{sl}", lambda e: e.dma_start(out=qTc[sl][:], in_=T["qTB"][:, :, tk].rearrange("h d t -> d h t")),
                    writes=[f"qTc{sl}"])
            sch.dma("sp", f"kTc{sl}", lambda e: e.dma_start(out=kTc[sl][:], in_=T["kTB"][:, :, tk].rearrange("h d t -> d h t")),
                    writes=[f"kTc{sl}"])
            sch.dma("sp", f"ktc{sl}", lambda e: e.dma_start(out=ktc[sl][:], in_=T["ktokB"][tk, :]), writes=[f"ktc{sl}"])
            sch.dma("sp", f"vtc{sl}", lambda e: e.dma_start(out=vtc[sl][:], in_=T["vtokB"][tk, :]), writes=[f"vtc{sl}"])
            s3 = c % 4
            sch.dma("sp", f"soc{s3}", lambda e: e.dma_start(out=soc[s3][:], in_=T["soTB"][:, :, tk].rearrange("f p t -> p f t")),
                    writes=[f"soc{s3}"])
            sch.dma("sp", f"ucc{s3}", lambda e: e.dma_start(out=ucc[s3][:], in_=T["ucTB"][:, :, tk].rearrange("f p t -> p f t")),
                    writes=[f"ucc{s3}"])
            sch.dma("sp", f"gtc{s3}", lambda e: e.dma_start(out=gtc[s3][:], in_=T["gTB"][0:8, :, tk].rearrange("f p t -> p f t")),
                    writes=[f"gtc{s3}"])

        def chunk(c, sl):
            tk = slice(c * 128, (c + 1) * 128)
            r1, r2 = Rec(), Rec()
            pacc = pacc4[2 * (c % 2):2 * (c % 2) + 2]
            pn = [f"c_pacc{2 * (c % 2) + i_}" for i_ in range(2)]
            b2 = c % 2
            for h in range(4):
                r1.op("act", lambda e, h=h: e.activation(out=vp[sl][:, h, 0:192], in_=vtc[sl][:, h * 192:(h + 1) * 192],
                                                         func=AF.Copy, scale=wsT[:, c, h:h + 1]),
                       reads=[f"vtc{sl}", "wsT"], writes=[f"vp{sl}"])
            r1.op("dve", lambda e: e.tensor_copy(out=vp[sl][:, :, 192:193], in_=wsT[:, c, :].unsqueeze(2)),
                   reads=["wsT", f"vp{sl}"], writes=[f"vp{sl}"])
            for h in range(4):
                r1.op("act", lambda e, h=h: e.activation(out=Ct[:, h, :], in_=Cst[:, h, :], func=AF.Copy,
                                                         scale=abc[0:96, h, c:c + 1]),
                       reads=["Cst", "abc"], writes=["Ct"])
            r1.op("act", lambda e: e.activation(out=Chat[b2][:], in_=Ct[:], func=AF.Copy),
                   reads=["Ct"], writes=[f"Chat{b2}"])
            def sT(e):
                for h in range(4):
                    i = e.matmul(pss[b2][:, h * 128:(h + 1) * 128], lhsT=kTc[sl][:, h, :], rhs=qTc[sl][:, h, :],
                                 start=True, stop=True)
                return i
            r1.op("pe", sT, reads=[f"kTc{sl}", f"qTc{sl}"], writes=["c_pss0"])
            r1.op("dve", lambda e: e.tensor_tensor(
                out=Sm[b2][:], in0=pss[b2][:].rearrange("p (h t) -> p h t", h=4),
                in1=mask01[:].unsqueeze(1).broadcast_to([128, 4, 128]), op=ALU.mult),
                reads=["c_pss0", "mask01"], writes=[f"Sm{b2}"])
            def accf(e):
                for h in range(4):
                    o = pacc[h // 2][:, (h % 2) * 193:(h % 2) * 193 + 193]
                    e.matmul(o, lhsT=qTc[sl][:, h, :], rhs=Chat[b2][:, h, :], start=True, stop=False)
                    i = e.matmul(o, lhsT=Sm[b2][:, h, :], rhs=vp[sl][:, h, :], start=False, stop=True)
                return i
            r1.op("pe", accf, reads=[f"qTc{sl}", f"Chat{b2}", f"Sm{b2}", f"vp{sl}"], writes=[pn[0], pn[1]])
            def uf(e):
                for h in range(4):
                    i = e.matmul(pU[h // 2][0:96, (h % 2) * 193:(h % 2) * 193 + 193], lhsT=ktc[sl][:, h * 96:(h + 1) * 96],
                                 rhs=vp[sl][:, h, :], start=True, stop=True)
                return i
            r1.op("pe", uf, reads=[f"ktc{sl}", f"vp{sl}"], writes=["c_pU0", "c_pU1"])
            for bb in range(2):
                r1.op("dve", lambda e, bb=bb: e.tensor_tensor(
                    out=Cst[:, 2 * bb:2 * bb + 2, :].rearrange("p h d -> p (h d)"),
                    in0=Ct[:, 2 * bb:2 * bb + 2, :].rearrange("p h d -> p (h d)"), in1=pU[bb][0:96, 0:386], op=ALU.add),
                    reads=["Ct", f"c_pU{bb}"], writes=["Cst"])
            for bb in range(2):
                av = pacc[bb][:, 0:386].rearrange("p (h d) -> p h d", d=193)
                r2.op("act", lambda e, bb=bb, av=av: e.activation(out=den[sl][:, 2 * bb:2 * bb + 2].unsqueeze(2),
                                                                  in_=av[:, :, 192:193], func=AF.Abs),
                       reads=[pn[bb]], writes=[f"den{sl}"])
            r2.op("dve", lambda e: e.tensor_tensor(out=den[sl][:], in0=den[sl][:], in1=eT[:, c, :], op=ALU.max),
                   reads=[f"den{sl}", "eT"], writes=[f"den{sl}"])
            r2.op("dve", lambda e: e.reciprocal(out=den[sl][:], in_=den[sl][:]), reads=[f"den{sl}"], writes=[f"den{sl}"])
            for bb in range(2):
                av = pacc[bb][:, 0:386].rearrange("p (h d) -> p h d", d=193)
                r2.op("dve", lambda e, bb=bb, av=av: e.tensor_tensor(
                    out=hout[sl][:, 2 * bb:2 * bb + 2, :], in0=av[:, :, 0:192],
                    in1=den[sl][:, 2 * bb:2 * bb + 2].unsqueeze(2).broadcast_to([128, 2, 192]), op=ALU.mult),
                    reads=[pn[bb], f"den{sl}"], writes=[f"hout{sl}"])
            for h in range(4):
                r2.op("act", lambda e, h=h: e.activation(out=junk[:], in_=hout[sl][:, h, :], func=AF.Square,
                                                         accum_out=ssh[sl][:, h:h + 1]),
                       reads=[f"hout{sl}"], writes=[f"ssh{sl}"])
            r2.op("act", lambda e: e.activation(out=ssh[sl][:], in_=ssh[sl][:], func=AF.Ln, scale=1.0 / 192, bias=EPS),
                   reads=[f"ssh{sl}"], writes=[f"ssh{sl}"])
            r2.op("act", lambda e: e.activation(out=ssh[sl][:], in_=ssh[sl][:], func=AF.Exp, scale=-0.5),
                   reads=[f"ssh{sl}"], writes=[f"ssh{sl}"])
            for h in range(4):
                r2.op("dve", lambda e, h=h: e.scalar_tensor_tensor(
                    out=hn[sl][:, h * 192:(h + 1) * 192], in0=hout[sl][:, h, :], scalar=ssh[sl][:, h:h + 1],
                    in1=headg[:, h * 192:(h + 1) * 192], op0=ALU.mult, op1=ALU.mult),
                    reads=[f"hout{sl}", f"ssh{sl}", "headg"], writes=[f"hn{sl}"])
            r2a, r2 = r2, Rec()
            def tr(e):
                for ft, (r0, sz) in enumerate(FTP):
                    i = e.transpose(out=pT[0:sz, ft * 128:(ft + 1) * 128], in_=hn[sl][:, r0:r0 + sz], identity=ident[:])
                return i
            r2.op("pe", tr, reads=[f"hn{sl}", "ident"], writes=["c_pT"])
            s3 = c % 4
            r2.op("pool", lambda e: e.tensor_tensor(out=m2[sl][:], in0=ucc[s3][:], in1=skipb[:], op=ALU.mult),
                   reads=[f"ucc{s3}", "skipb"], writes=[f"m2_{sl}"])
            r2.op("dve", lambda e: e.tensor_tensor(out=m1[sl][:].rearrange("p f t -> p (f t)"), in0=pT[:],
                                                    in1=soc[s3][:].rearrange("p f t -> p (f t)"), op=ALU.mult),
                   reads=["c_pT", f"soc{s3}"], writes=[f"m1_{sl}"])
            r2.op("pool", lambda e: e.tensor_tensor(out=m1[sl][:], in0=m1[sl][:], in1=m2[sl][:], op=ALU.add),
                   reads=[f"m2_{sl}", f"m1_{sl}"], writes=[f"m1_{sl}"])
            r2.op("pool", lambda e: e.tensor_tensor(out=cgc[sl][:], in0=m1[sl][:], in1=gtc[s3][:], op=ALU.mult),
                   reads=[f"m1_{sl}", f"gtc{s3}"], writes=[f"cgc{sl}"])
            r2.dma(STQ, f"cgc{sl}", lambda e: e.dma_start(out=T["cgTB"][0:8, :, tk].rearrange("f p t -> p f t"),
                                                          in_=cgc[sl][:]), reads=[f"cgc{sl}"], writes=["cg_dram"])

            return r1, r2a, r2

        def zip_emit(lists):
            pos = [0] * len(lists)
            while True:
                best, bf = None, None
                for i, l in enumerate(lists):
                    if pos[i] < len(l):
                        f = pos[i] / len(l)
                        if best is None or f < bf:
                            best, bf = i, f
                if best is None:
                    break
                kind, args, kw = lists[best][pos[best]]
                pos[best] += 1
                getattr(sch, kind)(*args, **kw)

        load(0)
        pa, pb = [], []
        for c in range(NT):
            if c + 1 < NT:
                load(c + 1)
            r1, r2a, r2b = chunk(c, c % 2)
            if prefetch is not None and c == min(4, NT - 1):
                prefetch(es)
            zip_emit([r1.items, pa, pb])
            pb = []
            pa, pb = r2a.items, pb
            nxt_b = r2b.items
            if c == 0:
                hold_b = nxt_b
            else:
                pb = hold_b
                hold_b = nxt_b
        zip_emit([pa, pb])
        zip_emit([hold_b])
        sch.end()


def make_consts():
    c = {}
    c["c_ident"] = np.eye(128, dtype=np.float32).astype(ml_dtypes.bfloat16)
    c["c_identf"] = np.eye(128, dtype=np.float32)
    c["c_ones"] = np.ones((128, 128), np.float32).astype(ml_dtypes.bfloat16)
    k = np.arange(128)[:, None]
    q = np.arange(128)[None, :]
    c["c_maskT"] = np.where(k <= q, 0.0, -30000.0).astype(np.float32).astype(ml_dtypes.bfloat16)
    c["c_mask01"] = np.where(k <= q, 1.0, 0.0).astype(np.float32).astype(ml_dtypes.bfloat16)
    oh = np.zeros((128, 2, 128), np.float32)
    oh[:, 0, 0:64] = 1.0
    oh[:, 1, 64:128] = 1.0
    c["c_onesh"] = oh.reshape(128, 256).astype(ml_dtypes.bfloat16)
    inv_freq = (10000.0 ** (-np.arange(0, 64, 2, dtype=np.float32) / np.float32(64))).astype(np.float32)
    c["c_invf"] = np.concatenate([inv_freq, inv_freq]).reshape(64, 1).astype(np.float32)
    c["c_sgn"] = np.concatenate([-np.ones(32), np.ones(32)]).reshape(64, 1).astype(np.float32)
    sel = np.zeros((4, 4, 128), np.float32)
    for h in range(4):
        sel[h, h, :] = 1.0
    c["c_sel"] = sel.reshape(4, 512)
    return c


def make_in_maps(inputs, S, ncores=8):
    consts = make_consts()
    shared = {}
    f = lambda a: np.ascontiguousarray(np.asarray(a, dtype=np.float32))
    shared["a_pre_g"] = f(inputs["a_pre_g"][0])
    shared["a_w_in"] = f(inputs["a_w_in"][0])
    shared["a_q_a_g"] = f(inputs["a_q_a_g"][0])
    shared["a_w_uq"] = f(inputs["a_w_uq"][0])
    shared["a_kv_a_g"] = f(inputs["a_kv_a_g"][0])
    shared["a_w_ukv"] = f(inputs["a_w_ukv"][0])
    shared["a_mem_g"] = f(inputs["a_mem_g"][0])
    shared["a_w_mem_kv"] = f(inputs["a_w_mem_kv"][0])
    shared["a_w_out"] = f(inputs["a_w_out"][0])
    shared["a_post_g"] = f(inputs["a_post_g"][0]).reshape(1, D)
    shared["b_pre_g"] = f(inputs["b_pre_g"][0])
    shared["b_w_in"] = f(inputs["b_w_in"][0])
    shared["b_gate_bias"] = f(inputs["b_gate_bias"][0]).reshape(8, 1)
    shared["b_conv_wT"] = f(np.asarray(inputs["b_conv_w"][0]).T)
    shared["b_conv_b"] = f(inputs["b_conv_b"][0])
    shared["b_w_q"] = f(inputs["b_w_q"][0]).reshape(768, 96)
    shared["b_w_k"] = f(inputs["b_w_k"][0]).reshape(768, 96)
    shared["b_w_v"] = f(inputs["b_w_v"][0]).reshape(768, 192)
    shared["b_head_g"] = f(inputs["b_head_g"][0]).reshape(1, 768)
    shared["b_skip"] = f(inputs["b_skip"][0])
    shared["b_mem_g"] = f(inputs["b_mem_g"][0])
    shared["b_w_mem_kv"] = f(inputs["b_w_mem_kv"][0])
    shared["b_w_out"] = f(inputs["b_w_out"][0])
    shared["b_post_g"] = f(inputs["b_post_g"][0]).reshape(1, D)
    shared.update(consts)
    maps = []
    for b in range(ncores):
        m = dict(shared)
        m["x"] = f(inputs["x"][b, :S])
        m["mem"] = f(inputs["mem"][b])
        m["pos"] = np.ascontiguousarray(np.asarray(inputs["positions"][b, :S], dtype=np.int32)).reshape(1, S)
        maps.append(m)
    return maps


_CACHE = {}


def kernel(**inputs):
    S = 4096
    if S not in _CACHE:
        _CACHE[S] = Builder(S).build()
    nc = _CACHE[S]
    maps = make_in_maps(inputs, S)
    res = run_bass_kernel_spmd(nc, maps, core_ids=list(range(8)))
    return np.stack([np.asarray(r["out"], dtype=np.float32) for r in res.results], axis=0)
```

```python
import math
from contextlib import ExitStack

import numpy as np
import ml_dtypes
import concourse.bass as bass
import concourse.mybir as mybir
from concourse.bass_utils import run_bass_kernel_spmd

F32 = mybir.dt.float32
BF16 = mybir.dt.bfloat16
I32 = mybir.dt.int32
AF = mybir.ActivationFunctionType
ALU = mybir.AluOpType

import os as _os
STQ = _os.environ.get("STQ", "sp")
D = 1024
NMEM = 256
EPS = 1e-6
PI = math.pi
PI_LO = 3.1415925
A_COLS = 1984
B_COLS = 2824
SCALE_A = 192.0 ** -0.5
SCALE_M = 64.0 ** -0.5
SCALE_K = 96.0 ** -0.5
FT = []
for _h in range(4):
    FT.append((_h * 192, 128))
    FT.append((_h * 192 + 128, 64))
FTP = [(r0, 128) for (r0, sz) in FT]


class Sched:
    ENGS = ("pe", "act", "dve", "pool", "sp")
    NPOOL = 56

    def __init__(self, nc, top):
        self.nc = nc
        self.phase_no = 0
        self.active = False
        self.excl = set()
        self.sem = {e: top.enter_context(nc.semaphore(f"eng_{e}")) for e in self.ENGS}
        self.tick = {e: 0 for e in self.ENGS}
        self.top = top
        self.dpool = []
        self.waited = {}
        self.rec = None

    def record(self, f):
        assert self.rec is None
        self.rec = []
        try:
            f()
        finally:
            lst, self.rec = self.rec, None
        return lst

    @staticmethod
    def merge(lists):
        pos = [0] * len(lists)
        out = []
        while True:
            best, bf = None, None
            for i, l in enumerate(lists):
                if pos[i] < len(l):
                    f = pos[i] / len(l)
                    if best is None or f < bf:
                        best, bf = i, f
            if best is None:
                return out
            out.append(lists[best][pos[best]])
            pos[best] += 1

    def zip_emit(self, lists):
        pos = [0] * len(lists)
        while True:
            best, bf = None, None
            for i, l in enumerate(lists):
                if pos[i] < len(l):
                    f = pos[i] / len(l)
                    if best is None or f < bf:
                        best, bf = i, f
            if best is None:
                break
            kind, args, kw = lists[best][pos[best]]
            pos[best] += 1
            getattr(self, kind)(*args, **kw)

    def begin(self):
        assert not self.active
        self.active = True
        self.phase_no += 1
        self.es = ExitStack()
        self.q = {e: [] for e in self.ENGS}
        self.res = {}
        self.dkey = {}
        self.nops = 0
        return self.es

    def _st(self, key):
        st = self.res.get(key)
        if st is None:
            st = {"w": None, "r": {}}
            self.res[key] = st
        return st

    def _deps(self, reads, writes):
        deps = []
        for r in reads:
            st = self._st(r)
            if st["w"] is not None:
                deps.append(st["w"])
            if r in self.excl:
                for src, t in st["r"].items():
                    deps.append((src, t))
        for w in writes:
            st = self._st(w)
            if st["w"] is not None:
                deps.append(st["w"])
            for src, t in st["r"].items():
                deps.append((src, t))
        return deps

    def _waits(self, engine, deps):
        need = {}
        for src, t in deps:
            if src == engine and engine == "pe":
                continue
            if self.waited.get((engine, src), 0) >= t:
                continue
            if need.get(src, 0) < t:
                need[src] = t
        out = []
        for src, t in need.items():
            self.waited[(engine, src)] = t
            if isinstance(src, str):
                out.append((self.sem[src], t))
            else:
                out.append((self.dpool[src][0], t))
        return out

    def _commit(self, token_src, t, reads, writes):
        for r in reads:
            st = self._st(r)
            if st["r"].get(token_src, 0) < t:
                st["r"][token_src] = t
        for w in writes:
            st = self._st(w)
            st["w"] = (token_src, t)
            st["r"] = {}

    def op(self, engine, fn, reads=(), writes=()):
        if self.rec is not None:
            self.rec.append(("op", (engine, fn), dict(reads=reads, writes=writes)))
            return
        waits = self._waits(engine, self._deps(reads, writes))
        self.tick[engine] += 1
        t = self.tick[engine]
        self.q[engine].append((waits, fn, (self.sem[engine], 1)))
        self._commit(engine, t, reads, writes)
        self.nops += 1

    def dma(self, queue, key, fn, reads=(), writes=()):
        if self.rec is not None:
            self.rec.append(("dma", (queue, key, fn), dict(reads=reads, writes=writes)))
            return
        if key not in self.dkey:
            idx = len(self.dkey)
            assert idx < self.NPOOL, "too many DMA keys in one phase"
            if idx >= len(self.dpool):
                self.dpool.append([self.top.enter_context(self.nc.semaphore(f"dma_{idx}")), 0])
            self.dkey[key] = idx
        idx = self.dkey[key]
        waits = self._waits(queue, self._deps(reads, writes))
        self.dpool[idx][1] += 16
        cnt = self.dpool[idx][1]
        self.q[queue].append((waits, fn, (self.dpool[idx][0], 16)))
        self._commit(idx, cnt, reads, writes)
        self.nops += 1

    def end(self):
        nc = self.nc
        final_waits = []
        for key, idx in self.dkey.items():
            final_waits.append((self.dpool[idx][0], self.dpool[idx][1]))
        for e in self.ENGS:
            if e != "sp" and self.tick[e] > 0:
                final_waits.append((self.sem[e], self.tick[e]))
        for e in self.ENGS:
            self.q[e].append((list(final_waits), None, None))

        def replay(e, eng):
            for waits, fn, inc in self.q[e]:
                for s, v in waits:
                    eng.wait_ge(s, v)
                if fn is not None:
                    ins = fn(eng)
                    ins.then_inc(inc[0], inc[1])

        with nc.Block() as block:
            @block.tensor
            def _(eng):
                replay("pe", eng)

            @block.scalar
            def _(eng):
                replay("act", eng)

            @block.vector
            def _(eng):
                replay("dve", eng)

            @block.gpsimd
            def _(eng):
                replay("pool", eng)

            @block.sync
            def _(eng):
                replay("sp", eng)
        self.es.close()
        self.active = False


class Rec:
    def __init__(self):
        self.items = []

    def op(self, *a, **k):
        self.items.append(("op", a, k))

    def dma(self, *a, **k):
        self.items.append(("dma", a, k))


class Rot:
    def __init__(self, name, n):
        self.name, self.n, self.i = name, n, 0

    def next(self):
        k = self.i % self.n
        self.i += 1
        return k


class Builder:
    def __init__(self, S, debug=False, stop=None):
        self.stop = stop
        self.S = S
        self.NT = S // 128
        self.NG = S // 512
        self.debug = debug
        self.nc = bass.Bass("TRN2", target_bir_lowering=False)
        self.sch = None
        self.T = {}
        self._excl = set()

    def din(self, name, shape, dt=F32):
        self.T[name] = self.nc.dram_tensor(name, list(shape), dt, kind="ExternalInput").ap()
        return self.T[name]

    def dscratch(self, name, shape, dt):
        if self.debug:
            t = self.nc.dram_tensor(name, list(shape), dt, kind="ExternalOutput").ap()
        else:
            t = self.nc.dram_tensor(name, list(shape), dt).ap()
        self.T[name] = t
        return t

    def sb(self, es, name, shape, dt):
        self._uid = getattr(self, "_uid", 0) + 1
        return es.enter_context(self.nc.sbuf_tensor(f"{name}_u{self._uid}", list(shape), dt))

    def ps(self, es, name, shape, dt):
        self._excl.add(name)
        self._uid = getattr(self, "_uid", 0) + 1
        return es.enter_context(self.nc.psum_tensor(f"{name}_u{self._uid}", list(shape), dt))

    def load_weight(self, es, dst, w_ap, ncols, kchunks, g_ap=None, tag="w", col0=0):
        sch, nc = self.sch, self.nc
        if not hasattr(self, "_wstg"):
            raise RuntimeError
        stg = self._wstg
        wt_ = self._wtag
        gsb = None
        if g_ap is not None:
            gsb = self.sb(es, f"g_{tag}", [128, len(kchunks)], F32)
            for c, (r0, sz) in enumerate(kchunks):
                def f(e, c=c, r0=r0, sz=sz):
                    return e.dma_start(out=gsb[0:sz, c:c + 1],
                                       in_=g_ap[r0:r0 + sz].rearrange("(p o) -> p o", o=1))
                sch.dma("sp", f"g_{tag}{c}", f, writes=[f"g_{tag}"])
        for c, ch in enumerate(kchunks):
            pieces = ch if isinstance(ch, list) else [(ch[0], ch[1], 0)]
            sz = max(p0 + n for (_, n, p0) in pieces)
            slot = self._wrot.next()
            for (r0, n, p0) in pieces:
                def ld(e, slot=slot, r0=r0, n=n, p0=p0):
                    return e.dma_start(out=stg[slot][p0:p0 + n, 0:ncols], in_=w_ap[r0:r0 + n, col0:col0 + ncols])
                sch.dma("sp", f"wstg{wt_}_{slot}_{p0}", ld, writes=[f"wstg{wt_}_{slot}"])
            eng = "act" if (c % 2 == 0) else "dve"
            if gsb is not None:
                if eng == "act":
                    def cv(e, slot=slot, c=c, sz=sz):
                        return e.activation(out=dst[0:sz, c, 0:ncols], in_=stg[slot][0:sz, 0:ncols],
                                            func=AF.Copy, scale=gsb[0:sz, c:c + 1])
                else:
                    def cv(e, slot=slot, c=c, sz=sz):
                        return e.tensor_scalar(out=dst[0:sz, c, 0:ncols], in0=stg[slot][0:sz, 0:ncols],
                                               scalar1=gsb[0:sz, c:c + 1], scalar2=None, op0=ALU.mult)
                rd = [f"wstg{wt_}_{slot}", f"g_{tag}"]
            else:
                if eng == "act":
                    def cv(e, slot=slot, c=c, sz=sz):
                        return e.activation(out=dst[0:sz, c, 0:ncols], in_=stg[slot][0:sz, 0:ncols], func=AF.Copy)
                else:
                    def cv(e, slot=slot, c=c, sz=sz):
                        return e.tensor_copy(out=dst[0:sz, c, 0:ncols], in_=stg[slot][0:sz, 0:ncols])
                rd = [f"wstg{wt_}_{slot}"]
            sch.op(eng, cv, reads=rd, writes=[tag])

    def wstage(self, es, ncols, n=3):
        self._wstg = [self.sb(es, f"wstg{i}", [128, ncols], F32) for i in range(n)]
        self._wrot = Rot("wstg", n)
        self._wtag = getattr(self, "_wtag", 0) + 1

    def norm_rows_T(self, es, src_ap, nrows, dstT, tag, ident):
        sch = self.sch
        nt = nrows // 128
        xs = self.sb(es, f"{tag}_xs", [128, nt, D], F32)
        xn = self.sb(es, f"{tag}_xn", [128, nt, D], BF16)
        junk = self.sb(es, f"{tag}_junk", [128, D], BF16)
        ss = self.sb(es, f"{tag}_ss", [128, nt], F32)
        pT = self.ps(es, f"{tag}_pT", [128, 1024], BF16)
        sch.dma("sp", f"{tag}_x", lambda e: e.dma_start(out=xs[:], in_=src_ap.rearrange("(j p) d -> p j d", p=128)),
                writes=[f"{tag}_xs"])
        for j in range(nt):
            sch.op("act", lambda e, j=j: e.activation(out=junk[:], in_=xs[:, j, :], func=AF.Square,
                                                     accum_out=ss[:, j:j + 1]),
                   reads=[f"{tag}_xs"], writes=[f"{tag}_ss{j}"])
            sch.op("act", lambda e, j=j: e.activation(out=ss[:, j:j + 1], in_=ss[:, j:j + 1], func=AF.Sqrt,
                                                     scale=1.0 / D, bias=EPS),
                   reads=[f"{tag}_ss{j}"], writes=[f"{tag}_ss{j}"])
            sch.op("dve", lambda e, j=j: e.reciprocal(out=ss[:, j:j + 1], in_=ss[:, j:j + 1]),
                   reads=[f"{tag}_ss{j}"], writes=[f"{tag}_ss{j}"])
            sch.op("dve", lambda e, j=j: e.tensor_scalar(out=xn[:, j, :], in0=xs[:, j, :], scalar1=ss[:, j:j + 1],
                                                        scalar2=None, op0=ALU.mult),
                   reads=[f"{tag}_xs", f"{tag}_ss{j}"], writes=[f"{tag}_xn{j}"])
        for c in range(8):
            def tr(e, c=c):
                for j in range(nt):
                    i = e.transpose(out=pT[:, j * 128:(j + 1) * 128], in_=xn[:, j, c * 128:(c + 1) * 128],
                                    identity=ident[:])
                return i
            sch.op("pe", tr, reads=[f"{tag}_xn{j}" for j in range(nt)] + ["ident"], writes=[f"{tag}_pT"])
            sch.op("act", lambda e, c=c: e.activation(out=dstT[:, c, 0:nrows], in_=pT[:, 0:nrows], func=AF.Copy),
                   reads=[f"{tag}_pT"], writes=[f"{tag}_T"])

    def mem_kv(self, es, wmkv, memnT, kmemT, vmem, tag):
        sch = self.sch
        pk = self.ps(es, f"{tag}_pk", [128, 512], F32)
        for m in range(2):
            def f(e, m=m):
                for c in range(8):
                    i = e.matmul(pk[:, 0:256], lhsT=wmkv[:, c, m * 128:(m + 1) * 128], rhs=memnT[:, c, :],
                                 start=(c == 0), stop=(c == 7))
                return i
            sch.op("pe", f, reads=[f"{tag}w", f"{tag}n_T"], writes=[f"{tag}_pk"])
            sch.op("dve", lambda e, m=m: e.tensor_copy(out=kmemT[:, m, :], in_=pk[:, 0:256]),
                   reads=[f"{tag}_pk"], writes=["kmemT"])
        sch.op("dve", lambda e: e.memset(vmem[:], 0.0), writes=["vmem"])
        for mt in range(2):
            def f(e, mt=mt):
                for c in range(8):
                    i = e.matmul(pk[:, 0:256], lhsT=memnT[:, c, mt * 128:(mt + 1) * 128], rhs=wmkv[:, c, 256:512],
                                 start=(c == 0), stop=(c == 7))
                return i
            sch.op("pe", f, reads=[f"{tag}w", f"{tag}n_T"], writes=[f"{tag}_pk"])
            for hh in range(4):
                half = hh % 2
                sch.op("dve", lambda e, mt=mt, hh=hh, half=half: e.tensor_copy(
                    out=vmem[:, mt, hh, half * 64:(half + 1) * 64], in_=pk[:, hh * 64:(hh + 1) * 64]),
                    reads=[f"{tag}_pk"], writes=["vmem"])

    def build(self):
        nc, S = self.nc, self.S
        x = self.din("x", [S, D])
        mem = self.din("mem", [NMEM, D])
        pos = self.din("pos", [1, S], I32)
        for n, shp in [("a_pre_g", [D]), ("a_w_in", [D, A_COLS]), ("a_q_a_g", [384]), ("a_w_uq", [384, 1152]),
                       ("a_kv_a_g", [256]), ("a_w_ukv", [256, 1536]), ("a_mem_g", [D]), ("a_w_mem_kv", [D, 512]),
                       ("a_w_out", [D, D]), ("a_post_g", [1, D]),
                       ("b_pre_g", [D]), ("b_w_in", [D, B_COLS]), ("b_gate_bias", [8, 1]), ("b_conv_wT", [768, 4]),
                       ("b_conv_b", [768]), ("b_w_q", [768, 96]), ("b_w_k", [768, 96]), ("b_w_v", [768, 192]),
                       ("b_head_g", [1, 768]), ("b_skip", [768]), ("b_mem_g", [D]), ("b_w_mem_kv", [D, 512]),
                       ("b_w_out", [D, D]), ("b_post_g", [1, D])]:
            self.din(n, shp)
        self.din("c_ident", [128, 128], BF16)
        self.din("c_ones", [128, 128], BF16)
        self.din("c_maskT", [128, 128], BF16)
        self.din("c_mask01", [128, 128], BF16)
        self.din("c_onesh", [128, 256], BF16)
        self.din("c_invf", [64, 1])
        self.din("c_sgn", [64, 1])
        self.din("c_identf", [128, 128])
        self.din("c_sel", [4, 4 * 128])
        out = nc.dram_tensor("out", [S, D], F32, kind="ExternalOutput").ap()
        self.T["out"] = out
        self.dscratch("x1", [S, D], F32)
        self.dscratch("gT", [D, S], BF16)
        self.dscratch("cgT", [D, S], BF16)

        with ExitStack() as top:
            self.sch = Sched(nc, top)
            self.sch.excl = self._excl
            ident = self.sb(top, "ident", [128, 128], BF16)
            ones = self.sb(top, "ones", [128, 128], BF16)
            maskT = self.sb(top, "maskT", [128, 128], BF16)
            mask01 = self.sb(top, "mask01", [128, 128], BF16)
            onesh = self.sb(top, "onesh", [128, 2, 128], BF16)
            self.C = dict(ident=ident, ones=ones, maskT=maskT, mask01=mask01, onesh=onesh)
            if self.debug == "A":
                self.layer_A(top)
            else:
                self.layer_A(top, do_out=False)
                self.layer_B(top)
        return nc

    def layer_A(self, top, do_out=True):
        nc, sch, S, T = self.nc, self.sch, self.S, self.T
        C = self.C
        ident, ones, maskT = C["ident"], C["ones"], C["maskT"]
        KC8 = [(c * 128, 128) for c in range(8)]
        self.dscratch("cqT", [384, S], BF16)
        self.dscratch("ckvT", [256, S], BF16)
        self.dscratch("krT", [64, S], BF16)
        with ExitStack() as LA:
            cs1 = self.sb(LA, "cs1", [64, S], F32)
            cs2 = self.sb(LA, "cs2", [64, S], F32)
            with ExitStack() as LA1:
                win = self.sb(LA1, "winA", [128, 8, A_COLS], BF16)
                winsw = self.sb(LA1, "winswA", [128, 8, 64], BF16)
                kmemT = self.sb(LA1, "kmemT", [128, 2, 256], BF16)
                vmem = self.sb(LA1, "vmem", [128, 2, 4, 128], BF16)
                es = sch.begin()
                for nm, tl in [("ident", ident), ("ones", ones), ("maskT", maskT), ("mask01", C["mask01"])]:
                    sch.dma("sp", f"c_{nm}", lambda e, nm=nm, tl=tl: e.dma_start(out=tl[:], in_=T[f"c_{nm}"]),
                            writes=[nm])
                sch.dma("sp", "c_onesh", lambda e: e.dma_start(out=C["onesh"][:].rearrange("p a b -> p (a b)"),
                                                                in_=T["c_onesh"]), writes=["onesh"])
                self.wstage(es, A_COLS)
                wmkv = self.sb(es, "wmkvA", [128, 8, 512], BF16)
                memnT = self.sb(es, "memnT", [128, 8, 256], BF16)
                self.load_weight(es, win, T["a_w_in"], A_COLS, KC8, T["a_pre_g"], tag="winA")
                sch.op("dve", lambda e: e.tensor_copy(out=winsw[:, :, 0:32], in_=win[:, :, 672:704]),
                       reads=["winA"], writes=["winswA"])
                sch.op("dve", lambda e: e.tensor_copy(out=winsw[:, :, 32:64], in_=win[:, :, 640:672]),
                       reads=["winA"], writes=["winswA"])
                self.load_weight(es, wmkv, T["a_w_mem_kv"], 512, KC8, T["a_mem_g"], tag="memAw")
                self.norm_rows_T(es, T["mem"], NMEM, memnT, "memAn", ident)
                self.mem_kv(es, wmkv, memnT, kmemT, vmem, "memA")
                sch.end()
                if self.stop == "A0":
                    return
                es = sch.begin()
                rope_chunks = self.rope_tables(es, cs1, cs2)
                self.proj_phase(es, "A", T["x"], win, winsw, kmemT, vmem, dict(cs1=cs1, cs2=cs2, rope=rope_chunks))
                sch.end()
                if self.stop == "A1":
                    return
            self.attn_phase(cs1, cs2)
            if self.stop == "A2":
                return
        if do_out:
            self.out_phase("A", T["a_w_out"], T["a_post_g"], T["x"], T["x1"], KC8)

    def rope_tables(self, es, cs1, cs2):
        sch, S, T = self.sch, self.S, self.T
        CW = min(S, 1024)
        posi = self.sb(es, "posi", [64, CW], I32)
        ang = self.sb(es, "ang", [64, CW], F32)
        u = self.sb(es, "rt_u", [64, CW], F32)
        ki = self.sb(es, "rt_ki", [64, CW], I32)
        r = self.sb(es, "rt_r", [64, CW], F32)
        m = self.sb(es, "rt_m", [64, CW], F32)
        invf = self.sb(es, "invf", [64, 1], F32)
        sgn = self.sb(es, "sgn", [64, 1], F32)
        sch.dma("sp", "invf", lambda e: e.dma_start(out=invf[:], in_=T["c_invf"]), writes=["invf"])
        sch.dma("sp", "sgn", lambda e: e.dma_start(out=sgn[:], in_=T["c_sgn"]), writes=["sgn"])

        def chunk(ci):
            cols = slice(ci * CW, (ci + 1) * CW)
            sch.dma("sp", "posi", lambda e: e.dma_start(out=posi[:], in_=T["pos"][:, cols].partition_broadcast(64)),
                    writes=["posi"])
            sch.op("dve", lambda e: e.tensor_copy(out=ang[:], in_=posi[:]), reads=["posi"], writes=["ang"])
            sch.op("dve", lambda e: e.tensor_scalar(out=ang[:], in0=ang[:], scalar1=invf[:, 0:1], scalar2=None,
                                                    op0=ALU.mult), reads=["ang", "invf"], writes=["ang"])
            for which, dst in (("sin", cs2), ("cos", cs1)):
                off = 0.0 if which == "sin" else PI / 2
                sch.op("dve", lambda e, off=off: e.tensor_scalar(out=u[:], in0=ang[:], scalar1=off,
                                                                scalar2=1.0 / (2 * PI), op0=ALU.add, op1=ALU.mult),
                       reads=["ang"], writes=["rt_u"])
                sch.op("dve", lambda e: e.tensor_copy(out=ki[:], in_=u[:]), reads=["rt_u"], writes=["rt_ki"])
                sch.op("dve", lambda e: e.tensor_copy(out=u[:], in_=ki[:]), reads=["rt_ki"], writes=["rt_u"])
                sch.op("dve", lambda e: e.scalar_tensor_tensor(out=r[:], in0=u[:], scalar=-2 * PI, in1=ang[:],
                                                              op0=ALU.mult, op1=ALU.add),
                       reads=["rt_u", "ang"], writes=["rt_r"])
                if off != 0.0:
                    sch.op("dve", lambda e, off=off: e.tensor_scalar(out=r[:], in0=r[:], scalar1=off, scalar2=None,
                                                                    op0=ALU.add), reads=["rt_r"], writes=["rt_r"])
                sch.op("dve", lambda e: e.tensor_scalar(out=m[:], in0=r[:], scalar1=PI, scalar2=2 * PI,
                                                        op0=ALU.is_gt, op1=ALU.mult), reads=["rt_r"], writes=["rt_m"])
                sch.op("dve", lambda e: e.tensor_tensor(out=r[:], in0=r[:], in1=m[:], op=ALU.subtract),
                       reads=["rt_r", "rt_m"], writes=["rt_r"])
                sch.op("dve", lambda e: e.tensor_scalar(out=m[:], in0=r[:], scalar1=-PI, scalar2=2 * PI,
                                                        op0=ALU.is_lt, op1=ALU.mult), reads=["rt_r"], writes=["rt_m"])
                sch.op("dve", lambda e: e.tensor_tensor(out=r[:], in0=r[:], in1=m[:], op=ALU.add),
                       reads=["rt_r", "rt_m"], writes=["rt_r"])
                sch.op("dve", lambda e: e.tensor_scalar(out=r[:], in0=r[:], scalar1=-PI_LO, scalar2=PI_LO,
                                                        op0=ALU.max, op1=ALU.min), reads=["rt_r"], writes=["rt_r"])
                sch.op("act", lambda e, dst=dst: e.activation(out=dst[:, cols], in_=r[:], func=AF.Sin),
                       reads=["rt_r"], writes=[f"cs{ci}"])
            sch.op("dve", lambda e: e.tensor_scalar(out=cs2[:, cols], in0=cs2[:, cols], scalar1=sgn[:, 0:1],
                                                    scalar2=None, op0=ALU.mult), reads=[f"cs{ci}", "sgn"], writes=[f"cs{ci}"])

        return [sch.record(lambda ci=ci: chunk(ci)) for ci in range(S // CW)]

    def proj_phase(self, es, L, x_src, win, winsw, kmemT, vmem, P):
        nc, sch, S, T = self.nc, self.sch, self.S, self.T
        NG = self.NG
        ident, ones = self.C["ident"], self.C["ones"]
        V = {}
        V["xs"] = xs = [self.sb(es, f"xs{i}", [128, D], F32) for i in range(4)]
        V["xn"] = xn = [self.sb(es, f"xn{i}", [128, D], BF16) for i in range(4)]
        V["xnT"] = xnT = [self.sb(es, f"xnT{i}", [128, 8, 512], BF16) for i in range(2)]
        V["junk"] = junk = self.sb(es, "junk", [128, D], BF16)
        V["ss"] = ss = [self.sb(es, f"ss{i}", [128, 1], F32) for i in range(4)]
        V["gt"] = gt = [self.sb(es, f"gt{i}", [128, 512], BF16) for i in range(3)]
        V["gmem"] = gmem = [self.sb(es, f"gmem{i}", [128, 2, 512], BF16) for i in range(2)]
        V["qmT"] = qmT = self.sb(es, "qmT", [128, 2, 512], BF16)
        V["pTm"] = pTm = [self.sb(es, f"pTm{i}", [128, 512], BF16) for i in range(3)]
        V["rz"] = rz = self.sb(es, "rzm", [128, 512], F32)
        V["om"] = om = self.sb(es, "om", [128, 512], F32)
        V["cgm"] = cgm = [self.sb(es, f"cgm{i}", [128, 512], BF16) for i in range(2)]
        NPG = 4 if L == "A" else 5
        V["ptr"] = ptr = [self.ps(es, f"ptr{i}", [128, 1024], BF16) for i in range(1)]
        V["pg"] = pg = [self.ps(es, f"pg{i}", [128, 512], F32) for i in range(NPG)]
        if L == "A":
            V["pss"] = self.ps(es, "pss", [128, 512], F32)
        V["pom"] = self.ps(es, "pom", [128, 512], F32)
        V["pzm"] = self.ps(es, "pzm", [128, 512], F32)
        V["rpg"] = rpg = Rot("pg", NPG)
        V["rptr"] = Rot("ptr", 1)
        V["rpT"] = Rot("pTm", 3)
        V["rgt"] = Rot("gt", 3)
        V["rcgm"] = Rot("cgm", 2)
        V.update(L=L, win=win, winsw=winsw, kmemT=kmemT, vmem=vmem, P=P, vmask=vmem, onesh=self.C["onesh"])
        if L == "A":
            V["craw"] = self.sb(es, "craw", [128, 5, 512], F32)
            V["sq"] = self.sb(es, "sq", [128, 5, 512], BF16)
            V["rinv"] = self.sb(es, "rinv", [128, 512], F32)
            V["t1"] = self.sb(es, "rp_t1", [64, 512], F32)
            V["t2"] = self.sb(es, "rp_t2", [64, 512], F32)
            V["lat"] = [self.sb(es, f"lat{i}", [128, 5, 512], BF16) for i in range(2)]
            V["kro"] = [self.sb(es, f"kro{i}", [64, 512], BF16) for i in range(2)]
            V["GATE0"], V["QM0"] = 960, 704
        else:
            V["GATE0"], V["QM0"] = 1800, 1544
            self.proj_B_alloc(es, V)

        def load_x(t):
            sl = t % 4
            sch.dma("sp", f"xs{sl}", lambda e: e.dma_start(out=xs[sl][:], in_=x_src[t * 128:(t + 1) * 128, :]),
                    reads=["xsrc_dram"], writes=[f"xs{sl}"])
        V["load_x"] = load_x

        def fm_matmul(sl, wt, c0, msz, wname):
            k = rpg.next()
            def f(e):
                for c in range(8):
                    i = e.matmul(pg[k][0:msz, :], lhsT=wt[:, c, c0:c0 + msz], rhs=xnT[sl][:, c, :],
                                 start=(c == 0), stop=(c == 7))
                return i
            sch.op("pe", f, reads=[wname, f"xnT{sl}"], writes=[f"pg{k}"])
            return k
        V["fm_matmul"] = fm_matmul

        for t in range(4):
            load_x(t)
        rope = P.get("rope") if L == "A" else None
        if rope:
            sch.zip_emit([rope[0]])
        for g in range(NG):
            body = self._proj_group(g, V)
            extra = rope[g + 1] if (rope and g + 1 < len(rope)) else []
            sch.zip_emit([body, extra])

    def _proj_group(self, g, V):
        sch, T, NT = self.sch, self.T, self.NT
        L, xs, xn, xnT, junk, ss, gt, gmem, qmT, pTm, rz, om, cgm = [V[k] for k in (
            "L", "xs", "xn", "xnT", "junk", "ss", "gt", "gmem", "qmT", "pTm", "rz", "om", "cgm")]
        ptr, pg, pom, pzm, rpg, rptr, rpT, rgt, rcgm = [V[k] for k in (
            "ptr", "pg", "pom", "pzm", "rpg", "rptr", "rpT", "rgt", "rcgm")]
        win, kmemT, vmem, fm_matmul, GATE0, QM0, load_x = [V[k] for k in (
            "win", "kmemT", "vmem", "fm_matmul", "GATE0", "QM0", "load_x")]
        ident, ones = self.C["ident"], self.C["ones"]
        sl = g % 2
        tok = slice(g * 512, (g + 1) * 512)

        def front_a(t):
            xsl = t % 4
            sch.op("act", lambda e: e.activation(out=junk[:], in_=xs[xsl][:], func=AF.Square, accum_out=ss[xsl][:]),
                   reads=[f"xs{xsl}"], writes=[f"ss{xsl}"])
            sch.op("act", lambda e: e.activation(out=ss[xsl][:], in_=ss[xsl][:], func=AF.Ln, scale=1.0 / D, bias=EPS),
                   reads=[f"ss{xsl}"], writes=[f"ss{xsl}"])
            sch.op("act", lambda e: e.activation(out=ss[xsl][:], in_=ss[xsl][:], func=AF.Exp, scale=-0.5),
                   reads=[f"ss{xsl}"], writes=[f"ss{xsl}"])
            if t % 2 == 0:
                sch.op("dve", lambda e: e.tensor_scalar(out=xn[xsl][:], in0=xs[xsl][:], scalar1=ss[xsl][:, 0:1],
                                                        scalar2=None, op0=ALU.mult),
                       reads=[f"xs{xsl}", f"ss{xsl}"], writes=[f"xn{xsl}"])
            else:
                sch.op("act", lambda e: e.activation(out=xn[xsl][:], in_=xs[xsl][:], func=AF.Copy,
                                                     scale=ss[xsl][:, 0:1]),
                       reads=[f"xs{xsl}", f"ss{xsl}"], writes=[f"xn{xsl}"])

        def front_b(t):
            nsl = t % 4
            j = t % 4
            tsl = (t // 4) % 2
            k = rptr.next()
            def tr(e):
                for c in range(8):
                    i = e.transpose(out=ptr[k][:, c * 128:(c + 1) * 128], in_=xn[nsl][:, c * 128:(c + 1) * 128],
                                    identity=ident[:])
                return i
            sch.op("pe", tr, reads=[f"xn{nsl}", "ident"], writes=[f"ptr{k}"])
            src = ptr[k][:].rearrange("p (c t) -> p c t", c=8)
            dst = xnT[tsl][:, :, j * 128:(j + 1) * 128]
            if j % 2 == 0:
                sch.op("act", lambda e: e.activation(out=dst, in_=src, func=AF.Copy),
                       reads=[f"ptr{k}"], writes=[f"xnT{tsl}"])
            else:
                sch.op("dve", lambda e: e.tensor_copy(out=dst, in_=src), reads=[f"ptr{k}"], writes=[f"xnT{tsl}"])

        nxt = [4 * (g + 1) + j for j in range(4)] if g + 1 < self.NG else []
        wn = f"win{L}"

        def gate_tile(m):
            k = fm_matmul(sl, win, GATE0 + m * 128, 128, wn)
            if m >= 6:
                sch.op("act", lambda e: e.activation(out=gmem[sl][:, m - 6, :], in_=pg[k][:], func=AF.Silu),
                       reads=[f"pg{k}"], writes=[f"gmem{sl}"])
            else:
                gs = rgt.next()
                sch.op("act", lambda e: e.activation(out=gt[gs][:], in_=pg[k][:], func=AF.Silu),
                       reads=[f"pg{k}"], writes=[f"gt{gs}"])
                dst = T["gT"][m * 128:(m + 1) * 128, tok]
                sch.dma(STQ, f"gt{gs}", lambda e: e.dma_start(out=dst, in_=gt[gs][:]),
                        reads=[f"gt{gs}"], writes=["gT_dram"])

        def qm_tile(m):
            k = fm_matmul(sl, win, QM0 + m * 128, 128, wn)
            sch.op("dve", lambda e: e.tensor_copy(out=qmT[:, m, :], in_=pg[k][:]), reads=[f"pg{k}"], writes=["qmT"])

        vmask, onesh = V["vmask"], V["onesh"]
        units = [(pr, hb, mt) for pr in range(2) for hb in range(2) for mt in range(2)]
        kq = {}

        def m_qk(i):
            pr, hb, mt = units[i]
            p0 = hb * 64
            k = rpg.next()
            kq[i] = k
            sch.op("pe", lambda e: e.matmul(pg[k][:], lhsT=kmemT[p0:p0 + 64, pr, mt * 128:(mt + 1) * 128],
                                            rhs=qmT[p0:p0 + 64, pr, :], start=True, stop=True),
                   reads=["kmemT", "qmT"], writes=[f"pg{k}"])

        def m_rest(i):
            pr, hb, mt = units[i]
            k = kq[i]
            n = hb * 2 + mt
            kp = rpT.next()
            sch.op("act", lambda e: e.activation(out=pTm[kp][:], in_=pg[k][:], func=AF.Exp, scale=SCALE_M),
                   reads=[f"pg{k}"], writes=[f"pTm{kp}"])
            def pv(e):
                e.matmul(pom[:], lhsT=vmask[:, mt, 2 * pr + hb, :], rhs=pTm[kp][:], start=(n == 0), stop=(n == 3))
                return e.matmul(pzm[:], lhsT=onesh[:, hb, :], rhs=pTm[kp][:], start=(n == 0), stop=(n == 3))
            sch.op("pe", pv, reads=["vmask", "onesh", f"pTm{kp}"], writes=["pom", "pzm"])
            if n == 3:
                cs_ = rcgm.next()
                sch.op("act", lambda e: e.activation(out=rz[:], in_=pzm[:], func=AF.Ln), reads=["pzm"], writes=["rzm"])
                sch.op("act", lambda e: e.activation(out=rz[:], in_=rz[:], func=AF.Exp, scale=-1.0),
                       reads=["rzm"], writes=["rzm"])
                sch.op("dve", lambda e: e.tensor_tensor(out=om[:], in0=pom[:], in1=rz[:], op=ALU.mult),
                       reads=["pom", "rzm"], writes=["om"])
                sch.op("pool", lambda e: e.tensor_tensor(out=cgm[cs_][:], in0=om[:], in1=gmem[sl][:, pr, :],
                                                         op=ALU.mult),
                       reads=["om", f"gmem{sl}"], writes=[f"cgm{cs_}"])
                if L == "A":
                    dst = T["cgT"][768 + pr * 128:768 + (pr + 1) * 128, tok]
                else:
                    dst = T["cgTB"][8 + pr, :, tok]
                sch.dma(STQ, f"cgm{cs_}", lambda e: e.dma_start(out=dst, in_=cgm[cs_][:]),
                        reads=[f"cgm{cs_}"], writes=["cgT_dram"])

        def mem_attention():
            MAH = 2
            for i in range(MAH):
                m_qk(i)
            for i in range(len(units)):
                if i + MAH < len(units):
                    m_qk(i + MAH)
                m_rest(i)

        def seg1():
            if g == 0:
                for j in range(4):
                    front_a(j)
                    front_b(j)
            for t in nxt:
                load_x(t)
            if L == "B":
                self.proj_B_extra(g, sl, tok, V, part=0)

        def X():
            if L == "A":
                for m in range(8):
                    gate_tile(m)
                for m in range(2):
                    qm_tile(m)
                self.proj_A_latents(g, sl, tok, V)
            else:
                self.proj_B_gates(g, sl, tok, V)
                self.proj_B_extra(g, sl, tok, V, part=3)
                for m in range(2):
                    qm_tile(m)
                self.proj_B_extra(g, sl, tok, V, part=2)
            mem_attention()

        def Y():
            if L == "B":
                self.proj_B_extra(g, sl, tok, V, part=1)
            for t in nxt:
                front_a(t)

        def seg3():
            for t in nxt:
                front_b(t)
            if L == "B":
                self.proj_B_qkv(g, sl, tok, V)

        l1 = sch.record(seg1)
        lx = sch.record(X)
        ly = sch.record(Y)
        l3 = sch.record(seg3)
        return l1 + Sched.merge([lx, ly]) + l3

    def proj_A_latents(self, g, sl, tok, V):
        sch, T = self.sch, self.T
        ones = self.C["ones"]
        win, winsw, fm_matmul, pg, pss, craw, sq, rinv, t1, t2, P, lat, kro = [V[k] for k in (
            "win", "winsw", "fm_matmul", "pg", "pss", "craw", "sq", "rinv", "t1", "t2", "P", "lat", "kro")]
        cs1, cs2 = P["cs1"], P["cs2"]
        SK = _os.environ.get("SKIP", "").split(",")
        for base, nt_, dim in ((0, 3, 384.0), (3, 2, 256.0)):
            for m in range(nt_):
                i5 = base + m
                k = fm_matmul(sl, win, i5 * 128, 128, "winA")
                sch.op("dve", lambda e, i5=i5, k=k: e.tensor_copy(out=craw[:, i5, :], in_=pg[k][:]),
                       reads=[f"pg{k}"], writes=[f"craw{i5}"])
                sch.op("act", lambda e, i5=i5, k=k: e.activation(out=sq[:, i5, :], in_=craw[:, i5, :], func=AF.Square),
                       reads=[f"craw{i5}"], writes=[f"sq{i5}"])
            def f(e, base=base, nt_=nt_):
                for m in range(nt_):
                    i = e.matmul(pss[:], lhsT=ones[:], rhs=sq[:, base + m, :], start=(m == 0), stop=(m == nt_ - 1))
                return i
            sch.op("pe", f, reads=["ones"] + [f"sq{base + m}" for m in range(nt_)], writes=["pss"])
            sch.op("act", lambda e, dim=dim: e.activation(out=rinv[:], in_=pss[:], func=AF.Ln, scale=1.0 / dim,
                                                         bias=EPS), reads=["pss"], writes=["rinv"])
            sch.op("act", lambda e: e.activation(out=rinv[:], in_=rinv[:], func=AF.Exp, scale=-0.5),
                   reads=["rinv"], writes=["rinv"])
            for m in range(nt_):
                eng = "dve" if (m % 2 == 0 or "latpool" in SK) else "pool"
                sch.op(eng, lambda e, m=m, base=base: e.tensor_tensor(
                    out=lat[sl][:, base + m, :], in0=craw[:, base + m, :], in1=rinv[:], op=ALU.mult),
                    reads=[f"craw{base + m}", "rinv"], writes=[f"lat{sl}"])
        if "latdma" not in SK:
          sch.dma(STQ, f"latq{sl}", lambda e: e.dma_start(
            out=T["cqT"][:, tok].rearrange("(c p) t -> p c t", p=128), in_=lat[sl][:, 0:3, :]),
            reads=[f"lat{sl}"], writes=["lat_dram"])
        if "latdma" not in SK:
          sch.dma(STQ, f"latkv{sl}", lambda e: e.dma_start(
            out=T["ckvT"][:, tok].rearrange("(c p) t -> p c t", p=128), in_=lat[sl][:, 3:5, :]),
            reads=[f"lat{sl}"], writes=["lat_dram"])
        if "krope" in SK:
            return
        kn_ = fm_matmul(sl, win, 640, 64, "winA")
        ks_ = fm_matmul(sl, winsw, 0, 64, "winswA")
        sch.op("dve", lambda e: e.tensor_tensor(out=t1[:], in0=pg[kn_][0:64, :], in1=cs1[:, tok], op=ALU.mult),
               reads=[f"pg{kn_}", f"cs{(g * 512) // min(self.S, 1024)}"], writes=["rp_t1"])
        sch.op("dve", lambda e: e.tensor_tensor(out=t2[:], in0=pg[ks_][0:64, :], in1=cs2[:, tok], op=ALU.mult),
               reads=[f"pg{ks_}", f"cs{(g * 512) // min(self.S, 1024)}"], writes=["rp_t2"])
        sch.op("pool", lambda e: e.tensor_tensor(out=kro[sl][:], in0=t1[:], in1=t2[:], op=ALU.add),
               reads=["rp_t1", "rp_t2"], writes=[f"kro{sl}"])
        sch.dma(STQ, f"kro{sl}", lambda e: e.dma_start(out=T["krT"][:, tok], in_=kro[sl][:]),
                reads=[f"kro{sl}"], writes=["lat_dram"])

    def attn_phase(self, cs1, cs2):
        nc, sch, S, T = self.nc, self.sch, self.S, self.T
        NG, NT = self.NG, self.NT
        C = self.C
        ident, ones, maskT = C["ident"], C["ones"], C["maskT"]
        es = sch.begin()
        cqnT = self.sb(es, "cqnT", [128, 3, S], BF16)
        ckvnT = self.sb(es, "ckvnT", [128, 2, S], BF16)
        kropeT = self.sb(es, "kropeT", [128, S], BF16)
        sch.op("pool", lambda e: e.memset(kropeT[64:128, :], 0.0), writes=["kropeTz"])
        wuq = self.sb(es, "wuq", [128, 3, 1152], BF16)
        wuqsw = self.sb(es, "wuqsw", [128, 3, 6, 64], BF16)
        wukv = self.sb(es, "wukv", [128, 2, 1536], BF16)
        self.wstage(es, 1536)
        self.load_weight(es, wuq, T["a_w_uq"], 1152, [(0, 128), (128, 128), (256, 128)], T["a_q_a_g"], tag="wuq")
        for h in range(6):
            sch.op("dve", lambda e, h=h: e.tensor_copy(out=wuqsw[:, :, h, 0:32],
                                                      in_=wuq[:, :, h * 192 + 160:h * 192 + 192]),
                   reads=["wuq"], writes=["wuqsw"])
            sch.op("dve", lambda e, h=h: e.tensor_copy(out=wuqsw[:, :, h, 32:64],
                                                      in_=wuq[:, :, h * 192 + 128:h * 192 + 160]),
                   reads=["wuq"], writes=["wuqsw"])
        self.load_weight(es, wukv, T["a_w_ukv"], 1536, [(0, 128), (128, 128)], T["a_kv_a_g"], tag="wukv")
        for g in range(NG):
            tk = slice(g * 512, (g + 1) * 512)
            sch.dma("sp", f"ldq{g % 2}", lambda e, tk=tk: e.dma_start(
                out=cqnT[:, :, tk], in_=T["cqT"][:, tk].rearrange("(c p) t -> p c t", p=128)), writes=["cqnT"])
            sch.dma("sp", f"ldkv{g % 2}", lambda e, tk=tk: e.dma_start(
                out=ckvnT[:, :, tk], in_=T["ckvT"][:, tk].rearrange("(c p) t -> p c t", p=128)), writes=["ckvnT"])
        sch.dma("sp", "ldkr", lambda e: e.dma_start(out=kropeT[0:64, :], in_=T["krT"]), writes=["kropeT"])
        qn = [self.sb(es, f"qn{i}", [128, S], BF16) for i in range(2)]
        qr = [self.sb(es, f"qr{i}", [128, S], BF16) for i in range(2)]
        for i_ in range(2):
            sch.op("pool", lambda e, i_=i_: e.memset(qr[i_][64:128, :], 0.0), writes=[f"qrz{i_}"])
        kn = [self.sb(es, f"kn{i}", [128, S], BF16) for i in range(2)]
        vv = [self.sb(es, f"vv{i}", [128, NT, 128], BF16) for i in range(2)]
        pT = [self.sb(es, f"pT{i}", [128, 512], BF16) for i in range(3)]
        t1 = self.sb(es, "at_t1", [64, 512], F32)
        t2 = self.sb(es, "at_t2", [64, 512], F32)
        rz = [self.sb(es, f"rz{i}", [128, 512], F32) for i in range(2)]
        of = [self.sb(es, f"of{i}", [128, 512], F32) for i in range(2)]
        gl = [self.sb(es, f"gl{i}", [128, 512], BF16) for i in range(2)]
        cg = [self.sb(es, f"cg{i}", [128, 512], BF16) for i in range(2)]
        NSC = 4
        psc = [self.ps(es, f"psc{i}", [128, 512], F32) for i in range(NSC)]
        po = [self.ps(es, f"po{i}", [128, 512], F32) for i in range(2)]
        pz = [self.ps(es, f"pz{i}", [128, 512], F32) for i in range(2)]
        pq = psc
        zacc = [[self.sb(es, f"zacc{i}_{p}", [128, 512], F32) for p in range(2)] for i in range(2)]
        onesf = self.sb(es, "onesf_a", [128, 128], F32)
        sch.op("dve", lambda e: e.memset(onesf[:], 1.0), writes=["onesf"])
        rpq = Rot("pq", NSC)

        def prod_group(h, hs, g):
            tok = slice(g * 512, (g + 1) * 512)
            k = rpq.next()
            def f(e):
                for c in range(3):
                    i = e.matmul(pq[k][:], lhsT=wuq[:, c, h * 192:h * 192 + 128], rhs=cqnT[:, c, tok],
                                 start=(c == 0), stop=(c == 2))
                return i
            sch.op("pe", f, reads=["wuq", "cqnT"], writes=[f"psc{k}"])
            sch.op("act", lambda e: e.activation(out=qn[hs][:, tok], in_=pq[k][:], func=AF.Copy),
                   reads=[f"psc{k}"], writes=[f"qn{hs}_{g}"])
            k1 = rpq.next()
            def f1(e):
                for c in range(3):
                    i = e.matmul(pq[k1][0:64, :], lhsT=wuq[:, c, h * 192 + 128:h * 192 + 192], rhs=cqnT[:, c, tok],
                                 start=(c == 0), stop=(c == 2))
                return i
            sch.op("pe", f1, reads=["wuq", "cqnT"], writes=[f"psc{k1}"])
            sch.op("dve", lambda e: e.tensor_tensor(out=t1[:], in0=pq[k1][0:64, :], in1=cs1[:, tok], op=ALU.mult),
                   reads=[f"psc{k1}", "cs"], writes=["at_t1"])
            k2 = rpq.next()
            def f2(e):
                for c in range(3):
                    i = e.matmul(pq[k2][0:64, :], lhsT=wuqsw[:, c, h, :], rhs=cqnT[:, c, tok],
                                 start=(c == 0), stop=(c == 2))
                return i
            sch.op("pe", f2, reads=["wuqsw", "cqnT"], writes=[f"psc{k2}"])
            sch.op("dve", lambda e: e.tensor_tensor(out=t2[:], in0=pq[k2][0:64, :], in1=cs2[:, tok], op=ALU.mult),
                   reads=[f"psc{k2}", "cs"], writes=["at_t2"])
            sch.op("pool", lambda e: e.tensor_tensor(out=qr[hs][0:64, tok], in0=t1[:], in1=t2[:], op=ALU.add),
                   reads=["at_t1", "at_t2"], writes=[f"qr{hs}_{g}"])
            k3 = rpq.next()
            def f3(e):
                for c in range(2):
                    i = e.matmul(pq[k3][:], lhsT=wukv[:, c, h * 256:h * 256 + 128], rhs=ckvnT[:, c, tok],
                                 start=(c == 0), stop=(c == 1))
                return i
            sch.op("pe", f3, reads=["wukv", "ckvnT"], writes=[f"psc{k3}"])
            sch.op("act", lambda e: e.activation(out=kn[hs][:, tok], in_=pq[k3][:], func=AF.Copy),
                   reads=[f"psc{k3}"], writes=[f"kn{hs}_{g}"])
            k4 = rpq.next()
            def f4(e):
                for j in range(4):
                    t0 = g * 512 + j * 128
                    for c in range(2):
                        i = e.matmul(pq[k4][:, j * 128:(j + 1) * 128], lhsT=ckvnT[:, c, t0:t0 + 128],
                                     rhs=wukv[:, c, h * 256 + 128:h * 256 + 256], start=(c == 0), stop=(c == 1))
                return i
            sch.op("pe", f4, reads=["wukv", "ckvnT"], writes=[f"psc{k4}"])
            sch.op("dve", lambda e: e.tensor_copy(
                out=vv[hs][:, g * 4:(g + 1) * 4, :].rearrange("p j d -> p (j d)"), in_=pq[k4][:]),
                reads=[f"psc{k4}"], writes=[f"vv{hs}_{g}"])

        def emit_qk(h, hs, j, uu, kt):
            q0 = j * 512
            r = kt - 4 * j
            c0 = r * 128 if r > 0 else 0
            sb_ = uu % NSC
            def f(e):
                e.matmul(psc[sb_][:, c0:512], lhsT=kn[hs][:, kt * 128:(kt + 1) * 128],
                         rhs=qn[hs][:, q0 + c0:q0 + 512], start=True, stop=False)
                i = e.matmul(psc[sb_][:, c0:512], lhsT=kropeT[:, kt * 128:(kt + 1) * 128],
                             rhs=qr[hs][:, q0 + c0:q0 + 512], start=False, stop=(r < 0))
                if r >= 0:
                    i = e.matmul(psc[sb_][:, c0:c0 + 128], lhsT=ident[:], rhs=maskT[:], start=False, stop=True)
                return i
            sch.op("pe", f, reads=[f"kn{hs}_{kt // 4}", f"qn{hs}_{j}", f"qr{hs}_{j}", f"qrz{hs}", "kropeT", "kropeTz",
                                   "ident", "maskT"], writes=[f"psc{sb_}"])
            return c0

        def emit_rest(h, hs, j, js, uu, kt, c0, last):
            sb_ = uu % NSC
            pb = uu % 3
            sch.op("act", lambda e: e.activation(out=pT[pb][:, c0:512], in_=psc[sb_][:, c0:512], func=AF.Exp,
                                                 scale=SCALE_A),
                   reads=[f"psc{sb_}"], writes=[f"pT{pb}"])
            def pv(e):
                return e.matmul(po[js][:, c0:512], lhsT=vv[hs][:, kt, :], rhs=pT[pb][:, c0:512],
                                start=(kt == 0), stop=last)
            sch.op("pe", pv, reads=[f"vv{hs}_{kt // 4}", f"pT{pb}"], writes=[f"po{js}"])
            if kt % 6 in (3, 5):
                sch.op("pe", lambda e: e.matmul(pz[js][:, c0:512], lhsT=ones[:], rhs=pT[pb][:, c0:512],
                                                start=(kt == 3), stop=False, skip_group_check=True),
                       reads=["ones", f"pT{pb}"], writes=[f"pz{js}"])
                return
            par = 1 if kt % 6 == 1 else 0
            eng = "dve" if par == 0 else "pool"
            if kt < 2:
                if c0 > 0:
                    sch.op(eng, lambda e: e.memset(zacc[js][par][:, 0:c0], 0.0), writes=[f"zacc{js}_{par}"])
                sch.op(eng, lambda e: e.tensor_copy(out=zacc[js][par][:, c0:512], in_=pT[pb][:, c0:512]),
                       reads=[f"pT{pb}"], writes=[f"zacc{js}_{par}"])
            else:
                sch.op(eng, lambda e: e.tensor_tensor(out=zacc[js][par][:, c0:512], in0=zacc[js][par][:, c0:512],
                                                      in1=pT[pb][:, c0:512], op=ALU.add),
                       reads=[f"pT{pb}", f"zacc{js}_{par}"], writes=[f"zacc{js}_{par}"])

        def finalize(h, j, js):
            q0 = j * 512
            sch.op("dve", lambda e: e.tensor_tensor(out=zacc[js][0][:], in0=zacc[js][0][:], in1=zacc[js][1][:],
                                                    op=ALU.add),
                   reads=[f"zacc{js}_0", f"zacc{js}_1"], writes=[f"zacc{js}_0"])
            sch.op("pe", lambda e: e.matmul(pz[js][:], lhsT=onesf[:], rhs=zacc[js][0][:], start=False, stop=True,
                                            skip_group_check=True),
                   reads=["onesf", f"zacc{js}_0"], writes=[f"pz{js}"])
            sch.op("act", lambda e: e.activation(out=rz[js][:], in_=pz[js][:], func=AF.Ln),
                   reads=[f"pz{js}"], writes=[f"rz{js}"])
            sch.op("act", lambda e: e.activation(out=rz[js][:], in_=rz[js][:], func=AF.Exp, scale=-1.0),
                   reads=[f"rz{js}"], writes=[f"rz{js}"])
            sch.op("dve", lambda e: e.tensor_tensor(out=of[js][:], in0=po[js][:], in1=rz[js][:], op=ALU.mult),
                   reads=[f"po{js}", f"rz{js}"], writes=[f"of{js}"])
            sch.op("pool", lambda e: e.tensor_tensor(out=cg[js][:], in0=of[js][:], in1=gl[js][:], op=ALU.mult),
                   reads=[f"of{js}", f"gl{js}"], writes=[f"cg{js}"])
            sch.dma(STQ, f"cg{js}", lambda e: e.dma_start(
                out=T["cgT"][h * 128:(h + 1) * 128, q0:q0 + 512], in_=cg[js][:]),
                reads=[f"cg{js}"], writes=["cgT_dram"])

        def gate_load(h, j, js):
            q0 = j * 512
            sch.dma("sp", f"gl{js}", lambda e: e.dma_start(
                out=gl[js][:], in_=T["gT"][h * 128:(h + 1) * 128, q0:q0 + 512]),
                reads=["gT_dram"], writes=[f"gl{js}"])

        fin = 0
        u = 0
        AH = 2
        for h in range(6):
            hs = h % 2
            for g in range(NG):
                prod_group(h, hs, g)
            flat = []
            jsl = {}
            for j in range(NG):
                jsl[j] = fin % 2
                fin += 1
                n = 4 * j + 4
                for kt in range(n):
                    flat.append((j, jsl[j], kt, kt == n - 1))
            c0s = {}
            pending = []
            for a_ in range(min(AH, len(flat))):
                j_, js_, kt_, _ = flat[a_]
                c0s[a_] = emit_qk(h, hs, j_, u + a_, kt_)
            for idx, (j, js, kt, last) in enumerate(flat):
                if kt == 0:
                    gate_load(h, j, js)
                if idx + AH < len(flat):
                    j_, js_, kt_, _ = flat[idx + AH]
                    c0s[idx + AH] = emit_qk(h, hs, j_, u + idx + AH, kt_)
                emit_rest(h, hs, j, js, u + idx, kt, c0s[idx], last)
                if last:
                    pending.append((idx + 2, j, js))
                while pending and pending[0][0] <= idx:
                    _, pj, pjs = pending.pop(0)
                    finalize(h, pj, pjs)
            for _, pj, pjs in pending:
                finalize(h, pj, pjs)
            u += len(flat)
        sch.end()

    def out_weights(self, es, L, wout_ap, postg_ap, kchunks, wout, postg):
        sch = self.sch
        sch.dma("sp", f"postg{L}", lambda e: e.dma_start(out=postg[:], in_=postg_ap.partition_broadcast(128)),
                writes=["postg"])
        self.wstage(es, D)
        self.load_weight(es, wout, wout_ap, D, kchunks, None, tag=f"wout{L}")

    def out_phase(self, L, wout_ap, postg_ap, x_src, x_dst, kchunks, extra_fn=None, pre=None):
        nc, sch, S, T = self.nc, self.sch, self.S, self.T
        NG = self.NG
        nk = len(kchunks)
        ksz = [max(p0 + n for (_, n, p0) in ch) if isinstance(ch, list) else ch[1] for ch in kchunks]
        es = sch.begin()
        if pre is not None:
            wout, postg = pre
        else:
            wout = self.sb(es, f"wout{L}", [128, nk, D], BF16)
            postg = self.sb(es, f"postg{L}", [128, D], F32)
            self.out_weights(es, L, wout_ap, postg_ap, kchunks, wout, postg)
        cgs = [self.sb(es, f"cgs{i}", [128, nk, 512], BF16) for i in range(2)]
        xs = [self.sb(es, f"oxs{i}", [128, 4, D], F32) for i in range(2)]
        tt = [self.sb(es, f"ott{i}", [128, D], F32) for i in range(3)]
        junk = self.sb(es, "ojunk", [128, D], BF16)
        ssq = [self.sb(es, f"ossq{i}", [128, 3], F32) for i in range(3)]
        py = [[self.ps(es, f"py{i}_{n}", [128, 512], F32) for n in range(2)] for i in range(3)]
        cg_ap = T["cgT"] if L == "A" else T["cgTB"]

        def load(g):
            sl = g % 2
            tok = slice(g * 512, (g + 1) * 512)
            if L == "A":
                sch.dma("sp", f"cgs{sl}", lambda e: e.dma_start(
                    out=cgs[sl][:], in_=cg_ap[:, tok].rearrange("(c p) t -> p c t", p=128)),
                    reads=["cgT_dram"], writes=[f"cgs{sl}"])
            else:
                sch.dma("sp", f"cgs{sl}", lambda e: e.dma_start(
                    out=cgs[sl][:, 0:4, :], in_=cg_ap[0:8:2, :, tok].rearrange("c p t -> p c t")),
                    reads=["cgT_dram"], writes=[f"cgs{sl}"])
                for i_, (fa, fb) in enumerate(((1, 3), (5, 7))):
                    sch.dma("sp", f"cgsa{sl}", lambda e, i_=i_, fa=fa: e.dma_start(
                        out=cgs[sl][0:64, 4 + i_, :], in_=cg_ap[fa, 0:64, tok]),
                        reads=["cgT_dram"], writes=[f"cgs{sl}"])
                    sch.dma("sp", f"cgsb{sl}", lambda e, i_=i_, fb=fb: e.dma_start(
                        out=cgs[sl][64:128, 4 + i_, :], in_=cg_ap[fb, 0:64, tok]),
                        reads=["cgT_dram"], writes=[f"cgs{sl}"])
                sch.dma("sp", f"cgsm{sl}", lambda e: e.dma_start(
                    out=cgs[sl][:, 6:8, :], in_=cg_ap[8:10, :, tok].rearrange("c p t -> p c t")),
                    reads=["cgT_dram"], writes=[f"cgs{sl}"])
            sch.dma("sp", f"oxs{sl}", lambda e: e.dma_start(
                out=xs[sl][:], in_=x_src[tok, :].rearrange("(j p) d -> p j d", p=128)),
                reads=["xsrc_dram"], writes=[f"oxs{sl}"])

        def tile(g, sl, j, b):
            def f(e):
                for n in range(2):
                    for c, sz in enumerate(ksz):
                        i = e.matmul(py[b][n][:], lhsT=cgs[sl][0:sz, c, j * 128:(j + 1) * 128],
                                     rhs=wout[0:sz, c, n * 512:(n + 1) * 512], start=(c == 0), stop=(c == nk - 1))
                return i
            sch.op("pe", f, reads=[f"cgs{sl}", f"wout{L}"], writes=[f"py{b}_0", f"py{b}_1"])
            for n in range(2):
                sch.op("act", lambda e, n=n: e.activation(out=junk[:, 0:512], in_=py[b][n][:], func=AF.Square,
                                                         accum_out=ssq[b][:, n:n + 1]),
                       reads=[f"py{b}_{n}"], writes=[f"ossq{b}"])
            sch.op("dve", lambda e: e.tensor_tensor(out=ssq[b][:, 2:3], in0=ssq[b][:, 0:1], in1=ssq[b][:, 1:2],
                                                    op=ALU.add), reads=[f"ossq{b}"], writes=[f"ossq{b}"])
            sch.op("act", lambda e: e.activation(out=ssq[b][:, 2:3], in_=ssq[b][:, 2:3], func=AF.Ln, scale=1.0 / D,
                                                 bias=EPS), reads=[f"ossq{b}"], writes=[f"ossq{b}"])
            sch.op("act", lambda e: e.activation(out=ssq[b][:, 2:3], in_=ssq[b][:, 2:3], func=AF.Exp, scale=-0.5),
                   reads=[f"ossq{b}"], writes=[f"ossq{b}"])
            for n in range(2):
                sch.op("dve", lambda e, n=n: e.scalar_tensor_tensor(
                    out=tt[b][:, n * 512:(n + 1) * 512], in0=py[b][n][:], scalar=ssq[b][:, 2:3],
                    in1=postg[:, n * 512:(n + 1) * 512], op0=ALU.mult, op1=ALU.mult),
                    reads=[f"py{b}_{n}", f"ossq{b}", "postg"], writes=[f"ott{b}"])
            sch.op("pool", lambda e: e.tensor_tensor(out=xs[sl][:, j, :], in0=xs[sl][:, j, :], in1=tt[b][:],
                                                     op=ALU.add),
                   reads=[f"ott{b}", f"oxs{sl}"], writes=[f"oxs{sl}"])

        def store(g):
            sl = g % 2
            tok = slice(g * 512, (g + 1) * 512)
            sch.dma(STQ, f"ost{sl}", lambda e: e.dma_start(
                out=x_dst[tok, :].rearrange("(j p) d -> p j d", p=128), in_=xs[sl][:]),
                reads=[f"oxs{sl}"], writes=["xdst_dram"])

        extra = sch.record(lambda: extra_fn(es)) if extra_fn is not None else []
        npart = NG
        load(0)
        it = 0
        for g in range(NG):
            if g + 1 < NG:
                load(g + 1)
            body = []
            for j in range(4):
                body += sch.record(lambda j=j: tile(g, g % 2, j, it % 3))
                it += 1
            lo, hi = (len(extra) * g) // npart, (len(extra) * (g + 1)) // npart
            sch.zip_emit([body, extra[lo:hi]])
            store(g)
        sch.end()

    def layer_B(self, top):
        nc, sch, S, T = self.nc, self.sch, self.S, self.T
        NT = self.NT
        C = self.C
        ident = C["ident"]
        KC8 = [(c * 128, 128) for c in range(8)]
        for nm, shp, dt in [("gTB", [10, 128, S], BF16), ("cgTB", [10, 128, S], BF16), ("ucTB", [8, 128, S], BF16),
                            ("soTB", [8, 128, S], BF16), ("qTB", [4, 96, S], BF16), ("kTB", [4, 96, S], BF16),
                            ("ktokB", [S, 384], BF16), ("vtokB", [S, 768], BF16), ("ifB", [2, 4, S], F32)]:
            self.dscratch(nm, shp, dt)
        with ExitStack() as LB:
            kch = [[(FT[2 * i][0], 128, 0)] for i in range(4)]
            kch += [[(FT[1][0], 64, 0), (FT[3][0], 64, 64)], [(FT[5][0], 64, 0), (FT[7][0], 64, 64)]]
            kch += [[(768, 128, 0)], [(896, 128, 0)]]
            woutB = self.sb(LB, "woutB", [128, 8, D], BF16)
            postgB = self.sb(LB, "postgB", [128, D], F32)
            wsT = self.sb(LB, "wsT", [128, NT, 4], F32)
            eT = self.sb(LB, "eT", [128, NT, 4], F32)
            abc = self.sb(LB, "abc", [128, 4, NT], F32)
            with ExitStack() as LB1:
                win = self.sb(LB1, "winB", [128, 8, B_COLS], BF16)
                kmemT = self.sb(LB1, "kmemTB", [128, 2, 256], BF16)
                vmem = self.sb(LB1, "vmemB", [128, 2, 4, 128], BF16)
                wq = self.sb(LB1, "wqB", [128, 8, 96], BF16)
                wk = self.sb(LB1, "wkB", [128, 8, 96], BF16)
                wv = self.sb(LB1, "wvB", [128, 8, 192], BF16)
                cvw = self.sb(LB1, "cvw", [128, 8, 4], F32)
                cvb = self.sb(LB1, "cvb", [128, 8], F32)
                def b0(es):
                  if True:
                    self.wstage(es, B_COLS, n=3)
                    wmkv = self.sb(es, "wmkvB", [128, 8, 512], BF16)
                    memnT = self.sb(es, "memnTB", [128, 8, 256], BF16)
                    self.load_weight(es, win, T["b_w_in"], B_COLS, KC8, T["b_pre_g"], tag="winB")
                    self.load_weight(es, wmkv, T["b_w_mem_kv"], 512, KC8, T["b_mem_g"], tag="memBw")
                    self.load_weight(es, wq, T["b_w_q"], 96, FT, None, tag="wqB")
                    self.load_weight(es, wk, T["b_w_k"], 96, FT, None, tag="wkB")
                    self.load_weight(es, wv, T["b_w_v"], 192, FT, None, tag="wvB")
                    sch.op("dve", lambda e: e.memset(cvw[:], 0.0), writes=["cvw"])
                    sch.op("dve", lambda e: e.memset(cvb[:], 0.0), writes=["cvb"])
                    for ft, (r0, sz) in enumerate(FTP):
                        sz = min(sz, 768 - r0)
                        sch.dma("sp", f"cvw{ft % 2}", lambda e, ft=ft, r0=r0, sz=sz: e.dma_start(
                            out=cvw[0:sz, ft, :], in_=T["b_conv_wT"][r0:r0 + sz, :]), writes=["cvw"])
                        sch.dma("sp", f"cvb{ft % 2}", lambda e, ft=ft, r0=r0, sz=sz: e.dma_start(
                            out=cvb[0:sz, ft:ft + 1], in_=T["b_conv_b"][r0:r0 + sz].rearrange("(p o) -> p o", o=1)),
                            writes=["cvb"])
                    self.norm_rows_T(es, T["mem"], NMEM, memnT, "memBn", ident)
                    self.mem_kv(es, wmkv, memnT, kmemT, vmem, "memB")
                KC8 = [(c * 128, 128) for c in range(8)]
                self.out_phase("A", T["a_w_out"], T["a_post_g"], T["x"], T["x1"], KC8)
                es = sch.begin()
                b0(es)
                sch.end()
                es = sch.begin()
                self.proj_phase(es, "B", T["x1"], win, None, kmemT, vmem,
                                dict(wq=wq, wk=wk, wv=wv, cvw=cvw, cvb=cvb))
                sch.end()
            if self.stop == "B1":
                return
            self.gate_phase(wsT, eT, abc)
            if self.stop == "B2":
                return
            self.chunk_phase(wsT, eT, abc, prefetch=lambda es: self.out_weights(
                es, "B", T["b_w_out"], T["b_post_g"], kch, woutB, postgB))
            if self.stop == "B3":
                return
            self.out_phase("B", T["b_w_out"], T["b_post_g"], T["x1"], T["out"], kch, pre=(woutB, postgB))

    def proj_B_alloc(self, es, V):
        V["ug"] = [self.sb(es, f"ug{i}", [128, 8, 515], BF16) for i in range(2)]
        V["acc"] = [self.sb(es, f"cacc{i}", [128, 512], F32) for i in range(8)]
        V["ucg"] = [self.sb(es, f"ucg{i}", [128, 8, 512], BF16) for i in range(2)]
        V["sot"] = [self.sb(es, f"sot{i}", [128, 512], BF16) for i in range(3)]
        V["qkt"] = [self.sb(es, f"qkt{i}", [96, 512], BF16) for i in range(3)]
        V["ktk"] = [self.sb(es, f"ktk{i}", [128, 384], BF16) for i in range(2)]
        V["vtk"] = [self.sb(es, f"vtk{i}", [128, 768], BF16) for i in range(2)]
        V["ift"] = [self.sb(es, f"ift{i}", [4, 512], F32) for i in range(2)]
        V["rsot"] = Rot("sot", 3)
        V["rqkt"] = Rot("qkt", 3)
        V["rktk"] = Rot("ktk", 2)
        V["rvtk"] = Rot("vtk", 2)
        V["rift"] = Rot("ift", 2)
        V["racc"] = Rot("cacc", 2)
        ug = V["ug"]
        self.sch.op("dve", lambda e: e.memset(ug[0][:], 0.0), writes=[f"ug0_{ft}" for ft in range(8)] + ["ugh0"])
        self.sch.op("dve", lambda e: e.memset(ug[1][:], 0.0), writes=[f"ug1_{ft}" for ft in range(8)] + ["ugh1"])

    def proj_B_gates(self, g, sl, tok, V):
        sch, T = self.sch, self.T
        win, fm_matmul, pg, gt, gmem, rgt, GATE0 = [V[k] for k in ("win", "fm_matmul", "pg", "gt", "gmem", "rgt", "GATE0")]
        for ft, (r0, sz) in enumerate(FTP):
            k = fm_matmul(sl, win, GATE0 + r0, sz, "winB")
            gs = rgt.next()
            sch.op("act", lambda e, k=k, gs=gs, sz=sz: e.activation(out=gt[gs][0:sz, :], in_=pg[k][0:sz, :], func=AF.Silu),
                   reads=[f"pg{k}"], writes=[f"gt{gs}"])
            sch.dma(STQ, f"gt{gs}", lambda e, gs=gs, ft=ft, sz=sz: e.dma_start(
                out=T["gTB"][ft, 0:sz, tok], in_=gt[gs][0:sz, :]), reads=[f"gt{gs}"], writes=["gT_dram"])
        for m in range(2):
            k = fm_matmul(sl, win, GATE0 + 768 + m * 128, 128, "winB")
            sch.op("act", lambda e, k=k, m=m: e.activation(out=gmem[sl][:, m, :], in_=pg[k][:], func=AF.Silu),
                   reads=[f"pg{k}"], writes=[f"gmem{sl}"])

    def proj_B_extra(self, g, sl, tok, V, part):
        sch, T = self.sch, self.T
        win, fm_matmul, pg, P = V["win"], V["fm_matmul"], V["pg"], V["P"]
        ug, acc, ucg, sot, ift = [V[k] for k in ("ug", "acc", "ucg", "sot", "ift")]
        rsot, rift = V["rsot"], V["rift"]
        cvw, cvb = P["cvw"], P["cvb"]
        us = g % 2
        if part == 0:
            for ft, (r0, sz) in enumerate(FTP):
                k = fm_matmul(sl, win, r0, sz, "winB")
                sch.op("dve", lambda e, k=k, ft=ft, sz=sz: e.tensor_copy(out=ug[us][0:sz, ft, 3:515], in_=pg[k][0:sz, :]),
                       reads=[f"pg{k}"], writes=[f"ug{us}_{ft}"])
            return
        if part == 1:
            for ft, (r0, sz) in enumerate(FTP):
                sch.op("act", lambda e, ft=ft, sz=sz: e.activation(
                    out=acc[ft][0:sz, :], in_=ug[us][0:sz, ft, 0:512], func=AF.Identity,
                    scale=cvw[0:sz, ft, 0:1], bias=cvb[0:sz, ft:ft + 1]),
                    reads=[f"ug{us}_{ft}", f"ugh{us}", "cvw", "cvb"], writes=[f"cacc{ft}"])
            for j in range(1, 4):
                for ft, (r0, sz) in enumerate(FTP):
                    sch.op("dve", lambda e, ft=ft, sz=sz, j=j: e.scalar_tensor_tensor(
                        out=acc[ft][0:sz, :], in0=ug[us][0:sz, ft, j:j + 512], scalar=cvw[0:sz, ft, j:j + 1],
                        in1=acc[ft][0:sz, :], op0=ALU.mult, op1=ALU.add),
                        reads=[f"ug{us}_{ft}", f"ugh{us}", "cvw", f"cacc{ft}"], writes=[f"cacc{ft}"])
            for ft, (r0, sz) in enumerate(FTP):
                sch.op("act", lambda e, ft=ft, sz=sz: e.activation(out=ucg[us][0:sz, ft, :], in_=acc[ft][0:sz, :],
                                                                  func=AF.Silu),
                       reads=[f"cacc{ft}"], writes=[f"ucg{us}"])
            sch.op("dve", lambda e: e.tensor_copy(out=ug[1 - us][:, :, 0:3], in_=ug[us][:, :, 512:515]),
                   reads=[f"ug{us}_{ft}" for ft in range(8)], writes=[f"ugh{1 - us}"])
            sch.dma(STQ, f"ucst{us}", lambda e: e.dma_start(out=T["ucTB"][:, :, tok].rearrange("f p t -> p f t"),
                                                             in_=ucg[us][:]), reads=[f"ucg{us}"], writes=["uc_dram"])
            return
        if part == 3:
            for ft, (r0, sz) in enumerate(FTP):
                k = fm_matmul(sl, win, 776 + r0, sz, "winB")
                ss_ = rsot.next()
                sch.op("act", lambda e, k=k, ss_=ss_, sz=sz: e.activation(out=sot[ss_][0:sz, :], in_=pg[k][0:sz, :],
                                                                         func=AF.Sigmoid),
                       reads=[f"pg{k}"], writes=[f"sot{ss_}"])
                sch.dma(STQ, f"sot{ss_}", lambda e, ss_=ss_, ft=ft, sz=sz: e.dma_start(
                    out=T["soTB"][ft, 0:sz, tok], in_=sot[ss_][0:sz, :]), reads=[f"sot{ss_}"], writes=["so_dram"])
            return
        for w_ in range(2):
            k = fm_matmul(sl, win, 768 + 4 * w_, 4, "winB")
            is_ = rift.next()
            sch.op("dve", lambda e, k=k, is_=is_: e.tensor_copy(out=ift[is_][:], in_=pg[k][0:4, :]),
                   reads=[f"pg{k}"], writes=[f"ift{is_}"])
            sch.dma(STQ, f"ift{is_}", lambda e, is_=is_, w_=w_: e.dma_start(out=T["ifB"][w_, :, tok], in_=ift[is_][:]),
                    reads=[f"ift{is_}"], writes=["if_dram"])

    def proj_B_qkv(self, g, sl, tok, V):
        sch, T = self.sch, self.T
        pg, P, rpg = V["pg"], V["P"], V["rpg"]
        ug, ucg, qkt, ktk, vtk = [V[k] for k in ("ug", "ucg", "qkt", "ktk", "vtk")]
        rqkt, rktk, rvtk = V["rqkt"], V["rktk"], V["rvtk"]
        wq, wk, wv = P["wq"], P["wk"], P["wv"]
        us = g % 2
        ugr = [f"ug{us}_{ft}" for ft in range(8)]
        for h in range(4):
            for which, wt, wn, dname, scale in (("q", wq, "wqB", "qTB", 1.0), ("k", wk, "wkB", "kTB", SCALE_K)):
                k = rpg.next()
                def f(e, k=k, wt=wt, h=h):
                    for i_, (ft, sz) in enumerate(((2 * h, 128), (2 * h + 1, 64))):
                        ins = e.matmul(pg[k][0:96, :], lhsT=wt[0:sz, ft, :], rhs=ucg[us][0:sz, ft, :],
                                       start=(i_ == 0), stop=(i_ == 1))
                    return ins
                sch.op("pe", f, reads=[wn, f"ucg{us}"], writes=[f"pg{k}"])
                qs = rqkt.next()
                sch.op("act", lambda e, k=k, qs=qs, scale=scale: e.activation(out=qkt[qs][:], in_=pg[k][0:96, :],
                                                                             func=AF.Copy, scale=scale),
                       reads=[f"pg{k}"], writes=[f"qkt{qs}"])
                sch.dma(STQ, f"qkt{qs}", lambda e, qs=qs, dname=dname, h=h: e.dma_start(
                    out=T[dname][h, :, tok], in_=qkt[qs][:]), reads=[f"qkt{qs}"], writes=["qk_dram"])
        for j in range(4):
            t0 = j * 128
            k = rpg.next()
            def f(e, k=k, t0=t0):
                for h in range(4):
                    for i_, (ft, sz) in enumerate(((2 * h, 128), (2 * h + 1, 64))):
                        ins = e.matmul(pg[k][:, h * 96:(h + 1) * 96], lhsT=ucg[us][0:sz, ft, t0:t0 + 128],
                                       rhs=wk[0:sz, ft, :], start=(i_ == 0), stop=(i_ == 1))
                return ins
            sch.op("pe", f, reads=["wkB", f"ucg{us}"], writes=[f"pg{k}"])
            ks = rktk.next()
            sch.op("act", lambda e, k=k, ks=ks: e.activation(out=ktk[ks][:], in_=pg[k][:, 0:384], func=AF.Copy,
                                                            scale=SCALE_K), reads=[f"pg{k}"], writes=[f"ktk{ks}"])
            tk = slice(g * 512 + t0, g * 512 + t0 + 128)
            sch.dma(STQ, f"ktk{ks}", lambda e, ks=ks, tk=tk: e.dma_start(out=T["ktokB"][tk, :], in_=ktk[ks][:]),
                    reads=[f"ktk{ks}"], writes=["qk_dram"])
            vs = rvtk.next()
            for half in range(2):
                k = rpg.next()
                def f(e, k=k, t0=t0, half=half):
                    for hh in range(2):
                        h = half * 2 + hh
                        for i_, (ft, sz) in enumerate(((2 * h, 128), (2 * h + 1, 64))):
                            ins = e.matmul(pg[k][:, hh * 192:(hh + 1) * 192], lhsT=ug[us][0:sz, ft, 3 + t0:3 + t0 + 128],
                                           rhs=wv[0:sz, ft, :], start=(i_ == 0), stop=(i_ == 1))
                    return ins
                sch.op("pe", f, reads=["wvB"] + ugr, writes=[f"pg{k}"])
                sch.op("dve", lambda e, k=k, vs=vs, half=half: e.tensor_copy(
                    out=vtk[vs][:, half * 384:(half + 1) * 384], in_=pg[k][:, 0:384]),
                    reads=[f"pg{k}"], writes=[f"vtk{vs}"])
            sch.dma(STQ, f"vtk{vs}", lambda e, vs=vs, tk=tk: e.dma_start(out=T["vtokB"][tk, :], in_=vtk[vs][:]),
                    reads=[f"vtk{vs}"], writes=["qk_dram"])

    def gate_phase(self, wsT, eT, abc):
        nc, sch, S, T = self.nc, self.sch, self.S, self.T
        NT = self.NT
        L = 128
        NC = S // L
        es = sch.begin()
        it = self.sb(es, "g_i", [4, S], F32)
        ft_ = self.sb(es, "g_f", [4, S], F32)
        spl = self.sb(es, "g_spl", [4, S], F32)
        bneg = self.sb(es, "g_bneg", [4, S], F32)
        gg = self.sb(es, "g_g", [4, S], F32)
        GG = self.sb(es, "g_G", [4, S], F32)
        dd = self.sb(es, "g_d", [4, S], F32)
        ws = self.sb(es, "g_ws", [4, S], F32)
        ee = self.sb(es, "g_e", [4, S], F32)
        da = self.sb(es, "g_da", [4, NC], F32)
        aa = self.sb(es, "g_a", [4, NC], F32)
        gb = self.sb(es, "g_gb", [4, 2], F32)
        ngb = self.sb(es, "g_ngb", [4, 1], F32)
        identf = self.sb(es, "identf", [128, 128], F32)
        sel = self.sb(es, "sel", [4, 4, 128], F32)
        pst = self.ps(es, "g_pst", [128, 512], F32)
        sch.dma("sp", "gi", lambda e: e.dma_start(out=it[:], in_=T["ifB"][0]), writes=["g_i"])
        sch.dma("sp", "gf", lambda e: e.dma_start(out=ft_[:], in_=T["ifB"][1]), writes=["g_f"])
        sch.dma("sp", "gb0", lambda e: e.dma_start(out=gb[:, 0:1], in_=T["b_gate_bias"][0:4, :]), writes=["g_gb"])
        sch.dma("sp", "gb1", lambda e: e.dma_start(out=gb[:, 1:2], in_=T["b_gate_bias"][4:8, :]), writes=["g_gb"])
        sch.dma("sp", "idf", lambda e: e.dma_start(out=identf[:], in_=T["c_identf"]), writes=["identf"])
        sch.dma("sp", "sel", lambda e: e.dma_start(out=sel[:].rearrange("p h m -> p (h m)"), in_=T["c_sel"]),
                writes=["sel"])
        sch.op("dve", lambda e: e.tensor_scalar(out=ngb[:], in0=gb[:, 1:2], scalar1=-1.0, scalar2=None, op0=ALU.mult),
               reads=["g_gb"], writes=["g_ngb"])
        sch.op("act", lambda e: e.activation(out=spl[:], in_=ft_[:], func=AF.Exp, scale=-1.0, bias=ngb[:, 0:1]),
               reads=["g_f", "g_ngb"], writes=["g_spl"])
        sch.op("act", lambda e: e.activation(out=spl[:], in_=spl[:], func=AF.Ln, scale=1.0, bias=1.0),
               reads=["g_spl"], writes=["g_spl"])
        sch.op("dve", lambda e: e.tensor_tensor_scan(out=bneg[:], data0=spl[:], data1=spl[:], initial=0.0,
                                                     op0=ALU.add, op1=ALU.max), reads=["g_spl"], writes=["g_bneg"])
        sch.op("dve", lambda e: e.scalar_tensor_tensor(out=gg[:], in0=it[:], scalar=gb[:, 0:1], in1=bneg[:],
                                                       op0=ALU.add, op1=ALU.add),
               reads=["g_i", "g_gb", "g_bneg"], writes=["g_g"])
        sch.op("dve", lambda e: e.tensor_tensor_scan(out=GG[:], data0=gg[:], data1=gg[:], initial=0.0,
                                                     op0=ALU.max, op1=ALU.max), reads=["g_g"], writes=["g_G"])
        Gv = GG[:].rearrange("p (c l) -> p c l", l=L)
        Rv = Gv[:, :, L - 1:L]
        Rb = Rv.broadcast_to([4, NC, L])
        sch.op("dve", lambda e: e.tensor_tensor(out=dd[:].rearrange("p (c l) -> p c l", l=L),
                                                in0=gg[:].rearrange("p (c l) -> p c l", l=L), in1=Rb,
                                                op=ALU.subtract), reads=["g_g", "g_G"], writes=["g_d"])
        sch.op("act", lambda e: e.activation(out=ws[:], in_=dd[:], func=AF.Exp), reads=["g_d"], writes=["g_ws"])
        sch.op("dve", lambda e: e.tensor_tensor(out=dd[:].rearrange("p (c l) -> p c l", l=L),
                                                in0=bneg[:].rearrange("p (c l) -> p c l", l=L), in1=Rb,
                                                op=ALU.subtract), reads=["g_bneg", "g_G", "g_ws"], writes=["g_d"])
        sch.op("act", lambda e: e.activation(out=ee[:], in_=dd[:], func=AF.Exp), reads=["g_d"], writes=["g_e"])
        Rflat = GG[:, L - 1::L] if False else None
        sch.op("dve", lambda e: e.memset(da[:], 0.0), writes=["g_da"])
        if NC > 1:
            sch.op("dve", lambda e: e.tensor_tensor(out=da[:, 1:NC].unsqueeze(2), in0=Rv[:, 0:NC - 1, :],
                                                    in1=Rv[:, 1:NC, :], op=ALU.subtract),
                   reads=["g_G", "g_da"], writes=["g_da"])
        sch.op("act", lambda e: e.activation(out=aa[:], in_=da[:], func=AF.Exp), reads=["g_da"], writes=["g_a"])
        for nm, src, dst in (("g_ws", ws, wsT), ("g_e", ee, eT)):
            def tr(e, src=src):
                for c in range(NT):
                    i = e.transpose(out=pst[:, c * 4:(c + 1) * 4], in_=src[0:4, c * 128:(c + 1) * 128],
                                    identity=identf[0:4, 0:4])
                return i
            sch.op("pe", tr, reads=[nm, "identf"], writes=["g_pst"])
            sch.op("dve", lambda e, dst=dst: e.tensor_copy(out=dst[:].rearrange("p c h -> p (c h)"),
                                                          in_=pst[:, 0:NT * 4]), reads=["g_pst"], writes=[nm + "T"])
        def ab(e):
            for h in range(4):
                i = e.matmul(pst[:, h * NC:(h + 1) * NC], lhsT=sel[0:4, h, :], rhs=aa[0:4, :], start=True, stop=True)
            return i
        sch.op("pe", ab, reads=["sel", "g_a"], writes=["g_pst"])
        sch.op("dve", lambda e: e.tensor_copy(out=abc[:].rearrange("p h c -> p (h c)"), in_=pst[:, 0:4 * NC]),
               reads=["g_pst"], writes=["abc"])
        sch.end()

    def chunk_phase(self, wsT, eT, abc, prefetch=None):
        nc, sch, S, T = self.nc, self.sch, self.S, self.T
        NT = self.NT
        C_ = self.C
        ident, mask01 = C_["ident"], C_["mask01"]
        es = sch.begin()
        qTc = [self.sb(es, f"qTc{i}", [96, 4, 128], BF16) for i in range(2)]
        kTc = [self.sb(es, f"kTc{i}", [96, 4, 128], BF16) for i in range(2)]
        ktc = [self.sb(es, f"ktc{i}", [128, 384], BF16) for i in range(2)]
        vtc = [self.sb(es, f"vtc{i}", [128, 768], BF16) for i in range(2)]
        soc = [self.sb(es, f"soc{i}", [128, 8, 128], BF16) for i in range(4)]
        ucc = [self.sb(es, f"ucc{i}", [128, 8, 128], BF16) for i in range(4)]
        gtc = [self.sb(es, f"gtc{i}", [128, 8, 128], BF16) for i in range(4)]
        vp = [self.sb(es, f"vp{i}", [128, 4, 193], BF16) for i in range(2)]
        Sm = [self.sb(es, f"Sm{i}", [128, 4, 128], BF16) for i in range(2)]
        Ct = self.sb(es, "Ct", [96, 4, 193], F32)
        Cst = self.sb(es, "Cst", [96, 4, 193], F32)
        Chat = [self.sb(es, f"Chat{i}", [96, 4, 193], BF16) for i in range(2)]
        hout = [self.sb(es, f"hout{i}", [128, 4, 192], F32) for i in range(2)]
        hn = [self.sb(es, f"hn{i}", [128, 896], BF16) for i in range(2)]
        den = [self.sb(es, f"den{i}", [128, 4], F32) for i in range(2)]
        ssh = [self.sb(es, f"ssh{i}", [128, 4], F32) for i in range(2)]
        junk = self.sb(es, "cjunk", [128, 192], BF16)
        m1 = [self.sb(es, f"m1_{i}", [128, 8, 128], F32) for i in range(2)]
        m2 = [self.sb(es, f"m2_{i}", [128, 8, 128], F32) for i in range(2)]
        cgc = [self.sb(es, f"cgc{i}", [128, 8, 128], BF16) for i in range(2)]
        skipb = self.sb(es, "skipb", [128, 8, 128], F32)
        skp = self.sb(es, "skp", [128, 8], F32)
        onesf = self.sb(es, "onesf", [128, 128], F32)
        headg = self.sb(es, "headg", [128, 768], F32)
        pss = [self.ps(es, f"c_pss{i}", [128, 512], F32) for i in range(1)] * 2
        pacc4 = [self.ps(es, f"c_pacc{i}", [128, 512], F32) for i in range(4)]
        pU = [self.ps(es, f"c_pU{i}", [128, 512], F32) for i in range(2)]
        pT = self.ps(es, "c_pT", [128, 1024], BF16)
        sch.dma("sp", "headg", lambda e: e.dma_start(out=headg[:], in_=T["b_head_g"].partition_broadcast(128)),
                writes=["headg"])
        sch.op("dve", lambda e: e.memset(skp[:], 0.0), writes=["skp"])
        for i_ in range(2):
            sch.op("dve", lambda e, i_=i_: e.memset(hn[i_][:], 0.0), writes=[f"hn{i_}"])
        for ft, (r0, sz) in enumerate(FTP):
            sz = min(sz, 768 - r0)
            sch.dma("sp", f"skp{ft % 2}", lambda e, ft=ft, r0=r0, sz=sz: e.dma_start(
                out=skp[0:sz, ft:ft + 1], in_=T["b_skip"][r0:r0 + sz].rearrange("(p o) -> p o", o=1)), writes=["skp"])
        sch.op("dve", lambda e: e.memset(onesf[:], 1.0), writes=["onesf"])
        sch.op("dve", lambda e: e.memset(skipb[:], 0.0), writes=["skipb"])
        sch.op("dve", lambda e: e.memset(Cst[:], 0.0), writes=["Cst"])
        for ft, (r0, sz) in enumerate(FTP):
            sch.op("dve", lambda e, ft=ft, sz=sz: e.tensor_scalar(out=skipb[0:sz, ft, :], in0=onesf[0:sz, :],
                                                                 scalar1=skp[0:sz, ft:ft + 1], scalar2=None,
                                                                 op0=ALU.mult),
                   reads=["onesf", "skp", "skipb"], writes=["skipb"])

        def load(c):
            sl = c % 2
            tk = slice(c * 128, (c + 1) * 128)
            sch.dma("sp", f"qTc{sl}", lambda e: e.dma_start(out=qTc[sl][:], in_=T["qTB"][:, :, tk].rearrange("h d t -> d h t")),
                    writes=[f"qTc{sl}"])
            sch.dma("sp", f"kTc{sl}", lambda e: e.dma_start(out=kTc[sl][:], in_=T["kTB"][:, :, tk].rearrange("h d t -> d h t")),
                    writes=[f"kTc{sl}"])
            sch.dma("sp", f"ktc{sl}", lambda e: e.dma_start(out=ktc[sl][:], in_=T["ktokB"][tk, :]), writes=[f"ktc{sl}"])
            sch.dma("sp", f"vtc{sl}", lambda e: e.dma_start(out=vtc[sl][:], in_=T["vtokB"][tk, :]), writes=[f"vtc{sl}"])
            s3 = c % 4
            sch.dma("sp", f"soc{s3}", lambda e: e.dma_start(out=soc[s3][:], in_=T["soTB"][:, :, tk].rearrange("f p t -> p f t")),
                    writes=[f"soc{s3}"])
            sch.dma("sp", f"ucc{s3}", lambda e: e.dma_start(out=ucc[s3][:], in_=T["ucTB"][:, :, tk].rearrange("f p t -> p f t")),
                    writes=[f"ucc{s3}"])
            sch.dma("sp", f"gtc{s3}", lambda e: e.dma_start(out=gtc[s3][:], in_=T["gTB"][0:8, :, tk].rearrange("f p t -> p f t")),
                    writes=[f"gtc{s3}"])

        def chunk(c, sl):
            tk = slice(c * 128, (c + 1) * 128)
            r1, r2 = Rec(), Rec()
            pacc = pacc4[2 * (c % 2):2 * (c % 2) + 2]
            pn = [f"c_pacc{2 * (c % 2) + i_}" for i_ in range(2)]
            b2 = c % 2
            for h in range(4):
                r1.op("act", lambda e, h=h: e.activation(out=vp[sl][:, h, 0:192], in_=vtc[sl][:, h * 192:(h + 1) * 192],
                                                         func=AF.Copy, scale=wsT[:, c, h:h + 1]),
                       reads=[f"vtc{sl}", "wsT"], writes=[f"vp{sl}"])
            r1.op("dve", lambda e: e.tensor_copy(out=vp[sl][:, :, 192:193], in_=wsT[:, c, :].unsqueeze(2)),
                   reads=["wsT", f"vp{sl}"], writes=[f"vp{sl}"])
            for h in range(4):
                r1.op("act", lambda e, h=h: e.activation(out=Ct[:, h, :], in_=Cst[:, h, :], func=AF.Copy,
                                                         scale=abc[0:96, h, c:c + 1]),
                       reads=["Cst", "abc"], writes=["Ct"])
            r1.op("act", lambda e: e.activation(out=Chat[b2][:], in_=Ct[:], func=AF.Copy),
                   reads=["Ct"], writes=[f"Chat{b2}"])
            def sT(e):
                for h in range(4):
                    i = e.matmul(pss[b2][:, h * 128:(h + 1) * 128], lhsT=kTc[sl][:, h, :], rhs=qTc[sl][:, h, :],
                                 start=True, stop=True)
                return i
            r1.op("pe", sT, reads=[f"kTc{sl}", f"qTc{sl}"], writes=["c_pss0"])
            r1.op("dve", lambda e: e.tensor_tensor(
                out=Sm[b2][:], in0=pss[b2][:].rearrange("p (h t) -> p h t", h=4),
                in1=mask01[:].unsqueeze(1).broadcast_to([128, 4, 128]), op=ALU.mult),
                reads=["c_pss0", "mask01"], writes=[f"Sm{b2}"])
            def accf(e):
                for h in range(4):
                    o = pacc[h // 2][:, (h % 2) * 193:(h % 2) * 193 + 193]
                    e.matmul(o, lhsT=qTc[sl][:, h, :], rhs=Chat[b2][:, h, :], start=True, stop=False)
                    i = e.matmul(o, lhsT=Sm[b2][:, h, :], rhs=vp[sl][:, h, :], start=False, stop=True)
                return i
            r1.op("pe", accf, reads=[f"qTc{sl}", f"Chat{b2}", f"Sm{b2}", f"vp{sl}"], writes=[pn[0], pn[1]])
            def uf(e):
                for h in range(4):
                    i = e.matmul(pU[h // 2][0:96, (h % 2) * 193:(h % 2) * 193 + 193], lhsT=ktc[sl][:, h * 96:(h + 1) * 96],
                                 rhs=vp[sl][:, h, :], start=True, stop=True)
                return i
            r1.op("pe", uf, reads=[f"ktc{sl}", f"vp{sl}"], writes=["c_pU0", "c_pU1"])
            for bb in range(2):
                r1.op("dve", lambda e, bb=bb: e.tensor_tensor(
                    out=Cst[:, 2 * bb:2 * bb + 2, :].rearrange("p h d -> p (h d)"),
                    in0=Ct[:, 2 * bb:2 * bb + 2, :].rearrange("p h d -> p (h d)"), in1=pU[bb][0:96, 0:386], op=ALU.add),
                    reads=["Ct", f"c_pU{bb}"], writes=["Cst"])
            for bb in range(2):
                av = pacc[bb][:, 0:386].rearrange("p (h d) -> p h d", d=193)
                r2.op("act", lambda e, bb=bb, av=av: e.activation(out=den[sl][:, 2 * bb:2 * bb + 2].unsqueeze(2),
                                                                  in_=av[:, :, 192:193], func=AF.Abs),
                       reads=[pn[bb]], writes=[f"den{sl}"])
            r2.op("dve", lambda e: e.tensor_tensor(out=den[sl][:], in0=den[sl][:], in1=eT[:, c, :], op=ALU.max),
                   reads=[f"den{sl}", "eT"], writes=[f"den{sl}"])
            r2.op("dve", lambda e: e.reciprocal(out=den[sl][:], in_=den[sl][:]), reads=[f"den{sl}"], writes=[f"den{sl}"])
            for bb in range(2):
                av = pacc[bb][:, 0:386].rearrange("p (h d) -> p h d", d=193)
                r2.op("dve", lambda e, bb=bb, av=av: e.tensor_tensor(
                    out=hout[sl][:, 2 * bb:2 * bb + 2, :], in0=av[:, :, 0:192],
                    in1=den[sl][:, 2 * bb:2 * bb + 2].unsqueeze(2).broadcast_to([128, 2, 192]), op=ALU.mult),
                    reads=[pn[bb], f"den{sl}"], writes=[f"hout{sl}"])
            for h in range(4):
                r2.op("act", lambda e, h=h: e.activation(out=junk[:], in_=hout[sl][:, h, :], func=AF.Square,
                                                         accum_out=ssh[sl][:, h:h + 1]),
                       reads=[f"hout{sl}"], writes=[f"ssh{sl}"])
            r2.op("act", lambda e: e.activation(out=ssh[sl][:], in_=ssh[sl][:], func=AF.Ln, scale=1.0 / 192, bias=EPS),
                   reads=[f"ssh{sl}"], writes=[f"ssh{sl}"])
            r2.op("act", lambda e: e.activation(out=ssh[sl][:], in_=ssh[sl][:], func=AF.Exp, scale=-0.5),
                   reads=[f"ssh{sl}"], writes=[f"ssh{sl}"])
            for h in range(4):
                r2.op("dve", lambda e, h=h: e.scalar_tensor_tensor(
                    out=hn[sl][:, h * 192:(h + 1) * 192], in0=hout[sl][:, h, :], scalar=ssh[sl][:, h:h + 1],
                    in1=headg[:, h * 192:(h + 1) * 192], op0=ALU.mult, op1=ALU.mult),
                    reads=[f"hout{sl}", f"ssh{sl}", "headg"], writes=[f"hn{sl}"])
            r2a, r2 = r2, Rec()
            def tr(e):
                for ft, (r0, sz) in enumerate(FTP):
                    i = e.transpose(out=pT[0:sz, ft * 128:(ft + 1) * 128], in_=hn[sl][:, r0:r0 + sz], identity=ident[:])
                return i
            r2.op("pe", tr, reads=[f"hn{sl}", "ident"], writes=["c_pT"])
            s3 = c % 4
            r2.op("pool", lambda e: e.tensor_tensor(out=m2[sl][:], in0=ucc[s3][:], in1=skipb[:], op=ALU.mult),
                   reads=[f"ucc{s3}", "skipb"], writes=[f"m2_{sl}"])
            r2.op("dve", lambda e: e.tensor_tensor(out=m1[sl][:].rearrange("p f t -> p (f t)"), in0=pT[:],
                                                    in1=soc[s3][:].rearrange("p f t -> p (f t)"), op=ALU.mult),
                   reads=["c_pT", f"soc{s3}"], writes=[f"m1_{sl}"])
            r2.op("dve", lambda e: e.tensor_tensor(out=m1[sl][:], in0=m1[sl][:], in1=m2[sl][:], op=ALU.add),
                   reads=[f"m2_{sl}", f"m1_{sl}"], writes=[f"m1_{sl}"])
            r2.op("pool", lambda e: e.tensor_tensor(out=cgc[sl][:], in0=m1[sl][:], in1=gtc[s3][:], op=ALU.mult),
                   reads=[f"m1_{sl}", f"gtc{s3}"], writes=[f"cgc{sl}"])
            r2.dma(STQ, f"cgc{sl}", lambda e: e.dma_start(out=T["cgTB"][0:8, :, tk].rearrange("f p t -> p f t"),
                                                          in_=cgc[sl][:]), reads=[f"cgc{sl}"], writes=["cg_dram"])

            return r1, r2a, r2

        def zip_emit(lists):
            pos = [0] * len(lists)
            while True:
                best, bf = None, None
                for i, l in enumerate(lists):
                    if pos[i] < len(l):
                        f = pos[i] / len(l)
                        if best is None or f < bf:
                            best, bf = i, f
                if best is None:
                    break
                kind, args, kw = lists[best][pos[best]]
                pos[best] += 1
                getattr(sch, kind)(*args, **kw)

        load(0)
        pa, pb = [], []
        for c in range(NT):
            if c + 1 < NT:
                load(c + 1)
            r1, r2a, r2b = chunk(c, c % 2)
            if prefetch is not None and c == min(4, NT - 1):
                prefetch(es)
            zip_emit([r1.items, pa, pb])
            pb = []
            pa, pb = r2a.items, pb
            nxt_b = r2b.items
            if c == 0:
                hold_b = nxt_b
            else:
                pb = hold_b
                hold_b = nxt_b
        zip_emit([pa, pb])
        zip_emit([hold_b])
        sch.end()


def make_consts():
    c = {}
    c["c_ident"] = np.eye(128, dtype=np.float32).astype(ml_dtypes.bfloat16)
    c["c_identf"] = np.eye(128, dtype=np.float32)
    c["c_ones"] = np.ones((128, 128), np.float32).astype(ml_dtypes.bfloat16)
    k = np.arange(128)[:, None]
    q = np.arange(128)[None, :]
    c["c_maskT"] = np.where(k <= q, 0.0, -30000.0).astype(np.float32).astype(ml_dtypes.bfloat16)
    c["c_mask01"] = np.where(k <= q, 1.0, 0.0).astype(np.float32).astype(ml_dtypes.bfloat16)
    oh = np.zeros((128, 2, 128), np.float32)
    oh[:, 0, 0:64] = 1.0
    oh[:, 1, 64:128] = 1.0
    c["c_onesh"] = oh.reshape(128, 256).astype(ml_dtypes.bfloat16)
    inv_freq = (10000.0 ** (-np.arange(0, 64, 2, dtype=np.float32) / np.float32(64))).astype(np.float32)
    c["c_invf"] = np.concatenate([inv_freq, inv_freq]).reshape(64, 1).astype(np.float32)
    c["c_sgn"] = np.concatenate([-np.ones(32), np.ones(32)]).reshape(64, 1).astype(np.float32)
    sel = np.zeros((4, 4, 128), np.float32)
    for h in range(4):
        sel[h, h, :] = 1.0
    c["c_sel"] = sel.reshape(4, 512)
    return c


def make_in_maps(inputs, S, ncores=8):
    consts = make_consts()
    shared = {}
    f = lambda a: np.ascontiguousarray(np.asarray(a, dtype=np.float32))
    shared["a_pre_g"] = f(inputs["a_pre_g"][0])
    shared["a_w_in"] = f(inputs["a_w_in"][0])
    shared["a_q_a_g"] = f(inputs["a_q_a_g"][0])
    shared["a_w_uq"] = f(inputs["a_w_uq"][0])
    shared["a_kv_a_g"] = f(inputs["a_kv_a_g"][0])
    shared["a_w_ukv"] = f(inputs["a_w_ukv"][0])
    shared["a_mem_g"] = f(inputs["a_mem_g"][0])
    shared["a_w_mem_kv"] = f(inputs["a_w_mem_kv"][0])
    shared["a_w_out"] = f(inputs["a_w_out"][0])
    shared["a_post_g"] = f(inputs["a_post_g"][0]).reshape(1, D)
    shared["b_pre_g"] = f(inputs["b_pre_g"][0])
    shared["b_w_in"] = f(inputs["b_w_in"][0])
    shared["b_gate_bias"] = f(inputs["b_gate_bias"][0]).reshape(8, 1)
    shared["b_conv_wT"] = f(np.asarray(inputs["b_conv_w"][0]).T)
    shared["b_conv_b"] = f(inputs["b_conv_b"][0])
    shared["b_w_q"] = f(inputs["b_w_q"][0]).reshape(768, 96)
    shared["b_w_k"] = f(inputs["b_w_k"][0]).reshape(768, 96)
    shared["b_w_v"] = f(inputs["b_w_v"][0]).reshape(768, 192)
    shared["b_head_g"] = f(inputs["b_head_g"][0]).reshape(1, 768)
    shared["b_skip"] = f(inputs["b_skip"][0])
    shared["b_mem_g"] = f(inputs["b_mem_g"][0])
    shared["b_w_mem_kv"] = f(inputs["b_w_mem_kv"][0])
    shared["b_w_out"] = f(inputs["b_w_out"][0])
    shared["b_post_g"] = f(inputs["b_post_g"][0]).reshape(1, D)
    shared.update(consts)
    maps = []
    for b in range(ncores):
        m = dict(shared)
        m["x"] = f(inputs["x"][b, :S])
        m["mem"] = f(inputs["mem"][b])
        m["pos"] = np.ascontiguousarray(np.asarray(inputs["positions"][b, :S], dtype=np.int32)).reshape(1, S)
        maps.append(m)
    return maps


_CACHE = {}


def kernel(**inputs):
    S = 4096
    if S not in _CACHE:
        _CACHE[S] = Builder(S).build()
    nc = _CACHE[S]
    maps = make_in_maps(inputs, S)
    res = run_bass_kernel_spmd(nc, maps, core_ids=list(range(8)))
    return np.stack([np.asarray(r["out"], dtype=np.float32) for r in res.results], axis=0)
```

```python
import math
from contextlib import ExitStack

import numpy as np
import ml_dtypes
import concourse.bass as bass
import concourse.mybir as mybir
from concourse.bass_utils import run_bass_kernel_spmd

F32 = mybir.dt.float32
BF16 = mybir.dt.bfloat16
I32 = mybir.dt.int32
AF = mybir.ActivationFunctionType
ALU = mybir.AluOpType

import os as _os
STQ = _os.environ.get("STQ", "sp")
D = 1024
NMEM = 256
EPS = 1e-6
PI = math.pi
PI_LO = 3.1415925
A_COLS = 1984
B_COLS = 2824
SCALE_A = 192.0 ** -0.5
SCALE_M = 64.0 ** -0.5
SCALE_K = 96.0 ** -0.5
FT = []
for _h in range(4):
    FT.append((_h * 192, 128))
    FT.append((_h * 192 + 128, 64))
FTP = [(r0, 128) for (r0, sz) in FT]


class Sched:
    ENGS = ("pe", "act", "dve", "pool", "sp")
    NPOOL = 56

    def __init__(self, nc, top):
        self.nc = nc
        self.phase_no = 0
        self.active = False
        self.excl = set()
        self.sem = {e: top.enter_context(nc.semaphore(f"eng_{e}")) for e in self.ENGS}
        self.tick = {e: 0 for e in self.ENGS}
        self.top = top
        self.dpool = []
        self.waited = {}
        self.rec = None

    def record(self, f):
        assert self.rec is None
        self.rec = []
        try:
            f()
        finally:
            lst, self.rec = self.rec, None
        return lst

    @staticmethod
    def merge(lists):
        pos = [0] * len(lists)
        out = []
        while True:
            best, bf = None, None
            for i, l in enumerate(lists):
                if pos[i] < len(l):
                    f = pos[i] / len(l)
                    if best is None or f < bf:
                        best, bf = i, f
            if best is None:
                return out
            out.append(lists[best][pos[best]])
            pos[best] += 1

    def zip_emit(self, lists):
        pos = [0] * len(lists)
        while True:
            best, bf = None, None
            for i, l in enumerate(lists):
                if pos[i] < len(l):
                    f = pos[i] / len(l)
                    if best is None or f < bf:
                        best, bf = i, f
            if best is None:
                break
            kind, args, kw = lists[best][pos[best]]
            pos[best] += 1
            getattr(self, kind)(*args, **kw)

    def begin(self):
        assert not self.active
        self.active = True
        self.phase_no += 1
        self.es = ExitStack()
        self.q = {e: [] for e in self.ENGS}
        self.res = {}
        self.dkey = {}
        self.nops = 0
        return self.es

    def _st(self, key):
        st = self.res.get(key)
        if st is None:
            st = {"w": None, "r": {}}
            self.res[key] = st
        return st

    def _deps(self, reads, writes):
        deps = []
        for r in reads:
            st = self._st(r)
            if st["w"] is not None:
                deps.append(st["w"])
            if r in self.excl:
                for src, t in st["r"].items():
                    deps.append((src, t))
        for w in writes:
            st = self._st(w)
            if st["w"] is not None:
                deps.append(st["w"])
            for src, t in st["r"].items():
                deps.append((src, t))
        return deps

    def _waits(self, engine, deps):
        need = {}
        for src, t in deps:
            if src == engine and engine == "pe":
                continue
            if self.waited.get((engine, src), 0) >= t:
                continue
            if need.get(src, 0) < t:
                need[src] = t
        out = []
        for src, t in need.items():
            self.waited[(engine, src)] = t
            if isinstance(src, str):
                out.append((self.sem[src], t))
            else:
                out.append((self.dpool[src][0], t))
        return out

    def _commit(self, token_src, t, reads, writes):
        for r in reads:
            st = self._st(r)
            if st["r"].get(token_src, 0) < t:
                st["r"][token_src] = t
        for w in writes:
            st = self._st(w)
            st["w"] = (token_src, t)
            st["r"] = {}

    def op(self, engine, fn, reads=(), writes=()):
        if self.rec is not None:
            self.rec.append(("op", (engine, fn), dict(reads=reads, writes=writes)))
            return
        waits = self._waits(engine, self._deps(reads, writes))
        self.tick[engine] += 1
        t = self.tick[engine]
        self.q[engine].append((waits, fn, (self.sem[engine], 1)))
        self._commit(engine, t, reads, writes)
        self.nops += 1

    def dma(self, queue, key, fn, reads=(), writes=()):
        if self.rec is not None:
            self.rec.append(("dma", (queue, key, fn), dict(reads=reads, writes=writes)))
            return
        if key not in self.dkey:
            idx = len(self.dkey)
            assert idx < self.NPOOL, "too many DMA keys in one phase"
            if idx >= len(self.dpool):
                self.dpool.append([self.top.enter_context(self.nc.semaphore(f"dma_{idx}")), 0])
            self.dkey[key] = idx
        idx = self.dkey[key]
        waits = self._waits(queue, self._deps(reads, writes))
        self.dpool[idx][1] += 16
        cnt = self.dpool[idx][1]
        self.q[queue].append((waits, fn, (self.dpool[idx][0], 16)))
        self._commit(idx, cnt, reads, writes)
        self.nops += 1

    def end(self):
        nc = self.nc
        final_waits = []
        for key, idx in self.dkey.items():
            final_waits.append((self.dpool[idx][0], self.dpool[idx][1]))
        for e in self.ENGS:
            if e != "sp" and self.tick[e] > 0:
                final_waits.append((self.sem[e], self.tick[e]))
        for e in self.ENGS:
            self.q[e].append((list(final_waits), None, None))

        def replay(e, eng):
            for waits, fn, inc in self.q[e]:
                for s, v in waits:
                    eng.wait_ge(s, v)
                if fn is not None:
                    ins = fn(eng)
                    ins.then_inc(inc[0], inc[1])

        with nc.Block() as block:
            @block.tensor
            def _(eng):
                replay("pe", eng)

            @block.scalar
            def _(eng):
                replay("act", eng)

            @block.vector
            def _(eng):
                replay("dve", eng)

            @block.gpsimd
            def _(eng):
                replay("pool", eng)

            @block.sync
            def _(eng):
                replay("sp", eng)
        self.es.close()
        self.active = False


class Rec:
    def __init__(self):
        self.items = []

    def op(self, *a, **k):
        self.items.append(("op", a, k))

    def dma(self, *a, **k):
        self.items.append(("dma", a, k))


class Rot:
    def __init__(self, name, n):
        self.name, self.n, self.i = name, n, 0

    def next(self):
        k = self.i % self.n
        self.i += 1
        return k


class Builder:
    def __init__(self, S, debug=False, stop=None):
        self.stop = stop
        self.S = S
        self.NT = S // 128
        self.NG = S // 512
        self.debug = debug
        self.nc = bass.Bass("TRN2", target_bir_lowering=False)
        self.sch = None
        self.T = {}
        self._excl = set()

    def din(self, name, shape, dt=F32):
        self.T[name] = self.nc.dram_tensor(name, list(shape), dt, kind="ExternalInput").ap()
        return self.T[name]

    def dscratch(self, name, shape, dt):
        if self.debug:
            t = self.nc.dram_tensor(name, list(shape), dt, kind="ExternalOutput").ap()
        else:
            t = self.nc.dram_tensor(name, list(shape), dt).ap()
        self.T[name] = t
        return t

    def sb(self, es, name, shape, dt):
        self._uid = getattr(self, "_uid", 0) + 1
        return es.enter_context(self.nc.sbuf_tensor(f"{name}_u{self._uid}", list(shape), dt))

    def ps(self, es, name, shape, dt):
        self._excl.add(name)
        self._uid = getattr(self, "_uid", 0) + 1
        return es.enter_context(self.nc.psum_tensor(f"{name}_u{self._uid}", list(shape), dt))

    def load_weight(self, es, dst, w_ap, ncols, kchunks, g_ap=None, tag="w", col0=0):
        sch, nc = self.sch, self.nc
        if not hasattr(self, "_wstg"):
            raise RuntimeError
        stg = self._wstg
        wt_ = self._wtag
        gsb = None
        if g_ap is not None:
            gsb = self.sb(es, f"g_{tag}", [128, len(kchunks)], F32)
            for c, (r0, sz) in enumerate(kchunks):
                def f(e, c=c, r0=r0, sz=sz):
                    return e.dma_start(out=gsb[0:sz, c:c + 1],
                                       in_=g_ap[r0:r0 + sz].rearrange("(p o) -> p o", o=1))
                sch.dma("sp", f"g_{tag}{c}", f, writes=[f"g_{tag}"])
        for c, ch in enumerate(kchunks):
            pieces = ch if isinstance(ch, list) else [(ch[0], ch[1], 0)]
            sz = max(p0 + n for (_, n, p0) in pieces)
            slot = self._wrot.next()
            for (r0, n, p0) in pieces:
                def ld(e, slot=slot, r0=r0, n=n, p0=p0):
                    return e.dma_start(out=stg[slot][p0:p0 + n, 0:ncols], in_=w_ap[r0:r0 + n, col0:col0 + ncols])
                sch.dma("sp", f"wstg{wt_}_{slot}_{p0}", ld, writes=[f"wstg{wt_}_{slot}"])
            eng = "act" if (c % 2 == 0) else "dve"
            if gsb is not None:
                if eng == "act":
                    def cv(e, slot=slot, c=c, sz=sz):
                        return e.activation(out=dst[0:sz, c, 0:ncols], in_=stg[slot][0:sz, 0:ncols],
                                            func=AF.Copy, scale=gsb[0:sz, c:c + 1])
                else:
                    def cv(e, slot=slot, c=c, sz=sz):
                        return e.tensor_scalar(out=dst[0:sz, c, 0:ncols], in0=stg[slot][0:sz, 0:ncols],
                                               scalar1=gsb[0:sz, c:c + 1], scalar2=None, op0=ALU.mult)
                rd = [f"wstg{wt_}_{slot}", f"g_{tag}"]
            else:
                if eng == "act":
                    def cv(e, slot=slot, c=c, sz=sz):
                        return e.activation(out=dst[0:sz, c, 0:ncols], in_=stg[slot][0:sz, 0:ncols], func=AF.Copy)
                else:
                    def cv(e, slot=slot, c=c, sz=sz):
                        return e.tensor_copy(out=dst[0:sz, c, 0:ncols], in_=stg[slot][0:sz, 0:ncols])
                rd = [f"wstg{wt_}_{slot}"]
            sch.op(eng, cv, reads=rd, writes=[tag])

    def wstage(self, es, ncols, n=3):
        self._wstg = [self.sb(es, f"wstg{i}", [128, ncols], F32) for i in range(n)]
        self._wrot = Rot("wstg", n)
        self._wtag = getattr(self, "_wtag", 0) + 1

    def norm_rows_T(self, es, src_ap, nrows, dstT, tag, ident):
        sch = self.sch
        nt = nrows // 128
        xs = self.sb(es, f"{tag}_xs", [128, nt, D], F32)
        xn = self.sb(es, f"{tag}_xn", [128, nt, D], BF16)
        junk = self.sb(es, f"{tag}_junk", [128, D], BF16)
        ss = self.sb(es, f"{tag}_ss", [128, nt], F32)
        pT = self.ps(es, f"{tag}_pT", [128, 1024], BF16)
        sch.dma("sp", f"{tag}_x", lambda e: e.dma_start(out=xs[:], in_=src_ap.rearrange("(j p) d -> p j d", p=128)),
                writes=[f"{tag}_xs"])
        for j in range(nt):
            sch.op("act", lambda e, j=j: e.activation(out=junk[:], in_=xs[:, j, :], func=AF.Square,
                                                     accum_out=ss[:, j:j + 1]),
                   reads=[f"{tag}_xs"], writes=[f"{tag}_ss{j}"])
            sch.op("act", lambda e, j=j: e.activation(out=ss[:, j:j + 1], in_=ss[:, j:j + 1], func=AF.Sqrt,
                                                     scale=1.0 / D, bias=EPS),
                   reads=[f"{tag}_ss{j}"], writes=[f"{tag}_ss{j}"])
            sch.op("dve", lambda e, j=j: e.reciprocal(out=ss[:, j:j + 1], in_=ss[:, j:j + 1]),
                   reads=[f"{tag}_ss{j}"], writes=[f"{tag}_ss{j}"])
            sch.op("dve", lambda e, j=j: e.tensor_scalar(out=xn[:, j, :], in0=xs[:, j, :], scalar1=ss[:, j:j + 1],
                                                        scalar2=None, op0=ALU.mult),
                   reads=[f"{tag}_xs", f"{tag}_ss{j}"], writes=[f"{tag}_xn{j}"])
        for c in range(8):
            def tr(e, c=c):
                for j in range(nt):
                    i = e.transpose(out=pT[:, j * 128:(j + 1) * 128], in_=xn[:, j, c * 128:(c + 1) * 128],
                                    identity=ident[:])
                return i
            sch.op("pe", tr, reads=[f"{tag}_xn{j}" for j in range(nt)] + ["ident"], writes=[f"{tag}_pT"])
            sch.op("act", lambda e, c=c: e.activation(out=dstT[:, c, 0:nrows], in_=pT[:, 0:nrows], func=AF.Copy),
                   reads=[f"{tag}_pT"], writes=[f"{tag}_T"])

    def mem_kv(self, es, wmkv, memnT, kmemT, vmem, tag):
        sch = self.sch
        pk = self.ps(es, f"{tag}_pk", [128, 512], F32)
        for m in range(2):
            def f(e, m=m):
                for c in range(8):
                    i = e.matmul(pk[:, 0:256], lhsT=wmkv[:, c, m * 128:(m + 1) * 128], rhs=memnT[:, c, :],
                                 start=(c == 0), stop=(c == 7))
                return i
            sch.op("pe", f, reads=[f"{tag}w", f"{tag}n_T"], writes=[f"{tag}_pk"])
            sch.op("dve", lambda e, m=m: e.tensor_copy(out=kmemT[:, m, :], in_=pk[:, 0:256]),
                   reads=[f"{tag}_pk"], writes=["kmemT"])
        sch.op("dve", lambda e: e.memset(vmem[:], 0.0), writes=["vmem"])
        for mt in range(2):
            def f(e, mt=mt):
                for c in range(8):
                    i = e.matmul(pk[:, 0:256], lhsT=memnT[:, c, mt * 128:(mt + 1) * 128], rhs=wmkv[:, c, 256:512],
                                 start=(c == 0), stop=(c == 7))
                return i
            sch.op("pe", f, reads=[f"{tag}w", f"{tag}n_T"], writes=[f"{tag}_pk"])
            for hh in range(4):
                half = hh % 2
                sch.op("dve", lambda e, mt=mt, hh=hh, half=half: e.tensor_copy(
                    out=vmem[:, mt, hh, half * 64:(half + 1) * 64], in_=pk[:, hh * 64:(hh + 1) * 64]),
                    reads=[f"{tag}_pk"], writes=["vmem"])

    def build(self):
        nc, S = self.nc, self.S
        x = self.din("x", [S, D])
        mem = self.din("mem", [NMEM, D])
        pos = self.din("pos", [1, S], I32)
        for n, shp in [("a_pre_g", [D]), ("a_w_in", [D, A_COLS]), ("a_q_a_g", [384]), ("a_w_uq", [384, 1152]),
                       ("a_kv_a_g", [256]), ("a_w_ukv", [256, 1536]), ("a_mem_g", [D]), ("a_w_mem_kv", [D, 512]),
                       ("a_w_out", [D, D]), ("a_post_g", [1, D]),
                       ("b_pre_g", [D]), ("b_w_in", [D, B_COLS]), ("b_gate_bias", [8, 1]), ("b_conv_wT", [768, 4]),
                       ("b_conv_b", [768]), ("b_w_q", [768, 96]), ("b_w_k", [768, 96]), ("b_w_v", [768, 192]),
                       ("b_head_g", [1, 768]), ("b_skip", [768]), ("b_mem_g", [D]), ("b_w_mem_kv", [D, 512]),
                       ("b_w_out", [D, D]), ("b_post_g", [1, D])]:
            self.din(n, shp)
        self.din("c_ident", [128, 128], BF16)
        self.din("c_ones", [128, 128], BF16)
        self.din("c_maskT", [128, 128], BF16)
        self.din("c_mask01", [128, 128], BF16)
        self.din("c_onesh", [128, 256], BF16)
        self.din("c_invf", [64, 1])
        self.din("c_sgn", [64, 1])
        self.din("c_identf", [128, 128])
        self.din("c_sel", [4, 4 * 128])
        out = nc.dram_tensor("out", [S, D], F32, kind="ExternalOutput").ap()
        self.T["out"] = out
        self.dscratch("x1", [S, D], F32)
        self.dscratch("gT", [D, S], BF16)
        self.dscratch("cgT", [D, S], BF16)

        with ExitStack() as top:
            self.sch = Sched(nc, top)
            self.sch.excl = self._excl
            ident = self.sb(top, "ident", [128, 128], BF16)
            ones = self.sb(top, "ones", [128, 128], BF16)
            maskT = self.sb(top, "maskT", [128, 128], BF16)
            mask01 = self.sb(top, "mask01", [128, 128], BF16)
            onesh = self.sb(top, "onesh", [128, 2, 128], BF16)
            self.C = dict(ident=ident, ones=ones, maskT=maskT, mask01=mask01, onesh=onesh)
            if self.debug == "A":
                self.layer_A(top)
            else:
                self.layer_A(top, do_out=False)
                self.layer_B(top)
        return nc

    def layer_A(self, top, do_out=True):
        nc, sch, S, T = self.nc, self.sch, self.S, self.T
        C = self.C
        ident, ones, maskT = C["ident"], C["ones"], C["maskT"]
        KC8 = [(c * 128, 128) for c in range(8)]
        self.dscratch("cqT", [384, S], BF16)
        self.dscratch("ckvT", [256, S], BF16)
        self.dscratch("krT", [64, S], BF16)
        with ExitStack() as LA:
            cs1 = self.sb(LA, "cs1", [64, S], F32)
            cs2 = self.sb(LA, "cs2", [64, S], F32)
            with ExitStack() as LA1:
                win = self.sb(LA1, "winA", [128, 8, A_COLS], BF16)
                winsw = self.sb(LA1, "winswA", [128, 8, 64], BF16)
                kmemT = self.sb(LA1, "kmemT", [128, 2, 256], BF16)
                vmem = self.sb(LA1, "vmem", [128, 2, 4, 128], BF16)
                es = sch.begin()
                for nm, tl in [("ident", ident), ("ones", ones), ("maskT", maskT), ("mask01", C["mask01"])]:
                    sch.dma("sp", f"c_{nm}", lambda e, nm=nm, tl=tl: e.dma_start(out=tl[:], in_=T[f"c_{nm}"]),
                            writes=[nm])
                sch.dma("sp", "c_onesh", lambda e: e.dma_start(out=C["onesh"][:].rearrange("p a b -> p (a b)"),
                                                                in_=T["c_onesh"]), writes=["onesh"])
                self.wstage(es, A_COLS)
                wmkv = self.sb(es, "wmkvA", [128, 8, 512], BF16)
                memnT = self.sb(es, "memnT", [128, 8, 256], BF16)
                self.load_weight(es, win, T["a_w_in"], A_COLS, KC8, T["a_pre_g"], tag="winA")
                sch.op("dve", lambda e: e.tensor_copy(out=winsw[:, :, 0:32], in_=win[:, :, 672:704]),
                       reads=["winA"], writes=["winswA"])
                sch.op("dve", lambda e: e.tensor_copy(out=winsw[:, :, 32:64], in_=win[:, :, 640:672]),
                       reads=["winA"], writes=["winswA"])
                self.load_weight(es, wmkv, T["a_w_mem_kv"], 512, KC8, T["a_mem_g"], tag="memAw")
                self.norm_rows_T(es, T["mem"], NMEM, memnT, "memAn", ident)
                self.mem_kv(es, wmkv, memnT, kmemT, vmem, "memA")
                sch.end()
                if self.stop == "A0":
                    return
                es = sch.begin()
                rope_chunks = self.rope_tables(es, cs1, cs2)
                self.proj_phase(es, "A", T["x"], win, winsw, kmemT, vmem, dict(cs1=cs1, cs2=cs2, rope=rope_chunks))
                sch.end()
                if self.stop == "A1":
                    return
            self.attn_phase(cs1, cs2)
            if self.stop == "A2":
                return
        if do_out:
            self.out_phase("A", T["a_w_out"], T["a_post_g"], T["x"], T["x1"], KC8)

    def rope_tables(self, es, cs1, cs2):
        sch, S, T = self.sch, self.S, self.T
        CW = min(S, 1024)
        posi = self.sb(es, "posi", [64, CW], I32)
        ang = self.sb(es, "ang", [64, CW], F32)
        u = self.sb(es, "rt_u", [64, CW], F32)
        ki = self.sb(es, "rt_ki", [64, CW], I32)
        r = self.sb(es, "rt_r", [64, CW], F32)
        m = self.sb(es, "rt_m", [64, CW], F32)
        invf = self.sb(es, "invf", [64, 1], F32)
        sgn = self.sb(es, "sgn", [64, 1], F32)
        sch.dma("sp", "invf", lambda e: e.dma_start(out=invf[:], in_=T["c_invf"]), writes=["invf"])
        sch.dma("sp", "sgn", lambda e: e.dma_start(out=sgn[:], in_=T["c_sgn"]), writes=["sgn"])

        def chunk(ci):
            cols = slice(ci * CW, (ci + 1) * CW)
            sch.dma("sp", "posi", lambda e: e.dma_start(out=posi[:], in_=T["pos"][:, cols].partition_broadcast(64)),
                    writes=["posi"])
            sch.op("dve", lambda e: e.tensor_copy(out=ang[:], in_=posi[:]), reads=["posi"], writes=["ang"])
            sch.op("dve", lambda e: e.tensor_scalar(out=ang[:], in0=ang[:], scalar1=invf[:, 0:1], scalar2=None,
                                                    op0=ALU.mult), reads=["ang", "invf"], writes=["ang"])
            for which, dst in (("sin", cs2), ("cos", cs1)):
                off = 0.0 if which == "sin" else PI / 2
                sch.op("dve", lambda e, off=off: e.tensor_scalar(out=u[:], in0=ang[:], scalar1=off,
                                                                scalar2=1.0 / (2 * PI), op0=ALU.add, op1=ALU.mult),
                       reads=["ang"], writes=["rt_u"])
                sch.op("dve", lambda e: e.tensor_copy(out=ki[:], in_=u[:]), reads=["rt_u"], writes=["rt_ki"])
                sch.op("dve", lambda e: e.tensor_copy(out=u[:], in_=ki[:]), reads=["rt_ki"], writes=["rt_u"])
                sch.op("dve", lambda e: e.scalar_tensor_tensor(out=r[:], in0=u[:], scalar=-2 * PI, in1=ang[:],
                                                              op0=ALU.mult, op1=ALU.add),
                       reads=["rt_u", "ang"], writes=["rt_r"])
                if off != 0.0:
                    sch.op("dve", lambda e, off=off: e.tensor_scalar(out=r[:], in0=r[:], scalar1=off, scalar2=None,
                                                                    op0=ALU.add), reads=["rt_r"], writes=["rt_r"])
                sch.op("dve", lambda e: e.tensor_scalar(out=m[:], in0=r[:], scalar1=PI, scalar2=2 * PI,
                                                        op0=ALU.is_gt, op1=ALU.mult), reads=["rt_r"], writes=["rt_m"])
                sch.op("dve", lambda e: e.tensor_tensor(out=r[:], in0=r[:], in1=m[:], op=ALU.subtract),
                       reads=["rt_r", "rt_m"], writes=["rt_r"])
                sch.op("dve", lambda e: e.tensor_scalar(out=m[:], in0=r[:], scalar1=-PI, scalar2=2 * PI,
                                                        op0=ALU.is_lt, op1=ALU.mult), reads=["rt_r"], writes=["rt_m"])
                sch.op("dve", lambda e: e.tensor_tensor(out=r[:], in0=r[:], in1=m[:], op=ALU.add),
                       reads=["rt_r", "rt_m"], writes=["rt_r"])
                sch.op("dve", lambda e: e.tensor_scalar(out=r[:], in0=r[:], scalar1=-PI_LO, scalar2=PI_LO,
                                                        op0=ALU.max, op1=ALU.min), reads=["rt_r"], writes=["rt_r"])
                sch.op("act", lambda e, dst=dst: e.activation(out=dst[:, cols], in_=r[:], func=AF.Sin),
                       reads=["rt_r"], writes=[f"cs{ci}"])
            sch.op("dve", lambda e: e.tensor_scalar(out=cs2[:, cols], in0=cs2[:, cols], scalar1=sgn[:, 0:1],
                                                    scalar2=None, op0=ALU.mult), reads=[f"cs{ci}", "sgn"], writes=[f"cs{ci}"])

        return [sch.record(lambda ci=ci: chunk(ci)) for ci in range(S // CW)]

    def proj_phase(self, es, L, x_src, win, winsw, kmemT, vmem, P):
        nc, sch, S, T = self.nc, self.sch, self.S, self.T
        NG = self.NG
        ident, ones = self.C["ident"], self.C["ones"]
        V = {}
        V["xs"] = xs = [self.sb(es, f"xs{i}", [128, D], F32) for i in range(4)]
        V["xn"] = xn = [self.sb(es, f"xn{i}", [128, D], BF16) for i in range(4)]
        V["xnT"] = xnT = [self.sb(es, f"xnT{i}", [128, 8, 512], BF16) for i in range(2)]
        V["junk"] = junk = self.sb(es, "junk", [128, D], BF16)
        V["ss"] = ss = [self.sb(es, f"ss{i}", [128, 1], F32) for i in range(4)]
        V["gt"] = gt = [self.sb(es, f"gt{i}", [128, 512], BF16) for i in range(3)]
        V["gmem"] = gmem = [self.sb(es, f"gmem{i}", [128, 2, 512], BF16) for i in range(2)]
        V["qmT"] = qmT = self.sb(es, "qmT", [128, 2, 512], BF16)
        V["pTm"] = pTm = [self.sb(es, f"pTm{i}", [128, 512], BF16) for i in range(3)]
        V["rz"] = rz = self.sb(es, "rzm", [128, 512], F32)
        V["om"] = om = self.sb(es, "om", [128, 512], F32)
        V["cgm"] = cgm = [self.sb(es, f"cgm{i}", [128, 512], BF16) for i in range(2)]
        NPG = 4 if L == "A" else 5
        V["ptr"] = ptr = [self.ps(es, f"ptr{i}", [128, 1024], BF16) for i in range(1)]
        V["pg"] = pg = [self.ps(es, f"pg{i}", [128, 512], F32) for i in range(NPG)]
        if L == "A":
            V["pss"] = self.ps(es, "pss", [128, 512], F32)
        V["pom"] = self.ps(es, "pom", [128, 512], F32)
        V["pzm"] = self.ps(es, "pzm", [128, 512], F32)
        V["rpg"] = rpg = Rot("pg", NPG)
        V["rptr"] = Rot("ptr", 1)
        V["rpT"] = Rot("pTm", 3)
        V["rgt"] = Rot("gt", 3)
        V["rcgm"] = Rot("cgm", 2)
        V.update(L=L, win=win, winsw=winsw, kmemT=kmemT, vmem=vmem, P=P, vmask=vmem, onesh=self.C["onesh"])
        if L == "A":
            V["craw"] = self.sb(es, "craw", [128, 5, 512], F32)
            V["sq"] = self.sb(es, "sq", [128, 5, 512], BF16)
            V["rinv"] = self.sb(es, "rinv", [128, 512], F32)
            V["t1"] = self.sb(es, "rp_t1", [64, 512], F32)
            V["t2"] = self.sb(es, "rp_t2", [64, 512], F32)
            V["lat"] = [self.sb(es, f"lat{i}", [128, 5, 512], BF16) for i in range(2)]
            V["kro"] = [self.sb(es, f"kro{i}", [64, 512], BF16) for i in range(2)]
            V["GATE0"], V["QM0"] = 960, 704
        else:
            V["GATE0"], V["QM0"] = 1800, 1544
            self.proj_B_alloc(es, V)

        def load_x(t):
            sl = t % 4
            sch.dma("sp", f"xs{sl}", lambda e: e.dma_start(out=xs[sl][:], in_=x_src[t * 128:(t + 1) * 128, :]),
                    reads=["xsrc_dram"], writes=[f"xs{sl}"])
        V["load_x"] = load_x

        def fm_matmul(sl, wt, c0, msz, wname):
            k = rpg.next()
            def f(e):
                for c in range(8):
                    i = e.matmul(pg[k][0:msz, :], lhsT=wt[:, c, c0:c0 + msz], rhs=xnT[sl][:, c, :],
                                 start=(c == 0), stop=(c == 7))
                return i
            sch.op("pe", f, reads=[wname, f"xnT{sl}"], writes=[f"pg{k}"])
            return k
        V["fm_matmul"] = fm_matmul

        for t in range(4):
            load_x(t)
        rope = P.get("rope") if L == "A" else None
        if rope:
            sch.zip_emit([rope[0]])
        for g in range(NG):
            body = self._proj_group(g, V)
            extra = rope[g + 1] if (rope and g + 1 < len(rope)) else []
            sch.zip_emit([body, extra])

    def _proj_group(self, g, V):
        sch, T, NT = self.sch, self.T, self.NT
        L, xs, xn, xnT, junk, ss, gt, gmem, qmT, pTm, rz, om, cgm = [V[k] for k in (
            "L", "xs", "xn", "xnT", "junk", "ss", "gt", "gmem", "qmT", "pTm", "rz", "om", "cgm")]
        ptr, pg, pom, pzm, rpg, rptr, rpT, rgt, rcgm = [V[k] for k in (
            "ptr", "pg", "pom", "pzm", "rpg", "rptr", "rpT", "rgt", "rcgm")]
        win, kmemT, vmem, fm_matmul, GATE0, QM0, load_x = [V[k] for k in (
            "win", "kmemT", "vmem", "fm_matmul", "GATE0", "QM0", "load_x")]
        ident, ones = self.C["ident"], self.C["ones"]
        sl = g % 2
        tok = slice(g * 512, (g + 1) * 512)

        def front_a(t):
            xsl = t % 4
            sch.op("act", lambda e: e.activation(out=junk[:], in_=xs[xsl][:], func=AF.Square, accum_out=ss[xsl][:]),
                   reads=[f"xs{xsl}"], writes=[f"ss{xsl}"])
            sch.op("act", lambda e: e.activation(out=ss[xsl][:], in_=ss[xsl][:], func=AF.Ln, scale=1.0 / D, bias=EPS),
                   reads=[f"ss{xsl}"], writes=[f"ss{xsl}"])
            sch.op("act", lambda e: e.activation(out=ss[xsl][:], in_=ss[xsl][:], func=AF.Exp, scale=-0.5),
                   reads=[f"ss{xsl}"], writes=[f"ss{xsl}"])
            if t % 2 == 0:
                sch.op("dve", lambda e: e.tensor_scalar(out=xn[xsl][:], in0=xs[xsl][:], scalar1=ss[xsl][:, 0:1],
                                                        scalar2=None, op0=ALU.mult),
                       reads=[f"xs{xsl}", f"ss{xsl}"], writes=[f"xn{xsl}"])
            else:
                sch.op("act", lambda e: e.activation(out=xn[xsl][:], in_=xs[xsl][:], func=AF.Copy,
                                                     scale=ss[xsl][:, 0:1]),
                       reads=[f"xs{xsl}", f"ss{xsl}"], writes=[f"xn{xsl}"])

        def front_b(t):
            nsl = t % 4
            j = t % 4
            tsl = (t // 4) % 2
            k = rptr.next()
            def tr(e):
                for c in range(8):
                    i = e.transpose(out=ptr[k][:, c * 128:(c + 1) * 128], in_=xn[nsl][:, c * 128:(c + 1) * 128],
                                    identity=ident[:])
                return i
            sch.op("pe", tr, reads=[f"xn{nsl}", "ident"], writes=[f"ptr{k}"])
            src = ptr[k][:].rearrange("p (c t) -> p c t", c=8)
            dst = xnT[tsl][:, :, j * 128:(j + 1) * 128]
            if j % 2 == 0:
                sch.op("act", lambda e: e.activation(out=dst, in_=src, func=AF.Copy),
                       reads=[f"ptr{k}"], writes=[f"xnT{tsl}"])
            else:
                sch.op("dve", lambda e: e.tensor_copy(out=dst, in_=src), reads=[f"ptr{k}"], writes=[f"xnT{tsl}"])

        nxt = [4 * (g + 1) + j for j in range(4)] if g + 1 < self.NG else []
        wn = f"win{L}"

        def gate_tile(m):
            k = fm_matmul(sl, win, GATE0 + m * 128, 128, wn)
            if m >= 6:
                sch.op("act", lambda e: e.activation(out=gmem[sl][:, m - 6, :], in_=pg[k][:], func=AF.Silu),
                       reads=[f"pg{k}"], writes=[f"gmem{sl}"])
            else:
                gs = rgt.next()
                sch.op("act", lambda e: e.activation(out=gt[gs][:], in_=pg[k][:], func=AF.Silu),
                       reads=[f"pg{k}"], writes=[f"gt{gs}"])
                dst = T["gT"][m * 128:(m + 1) * 128, tok]
                sch.dma(STQ, f"gt{gs}", lambda e: e.dma_start(out=dst, in_=gt[gs][:]),
                        reads=[f"gt{gs}"], writes=["gT_dram"])

        def qm_tile(m):
            k = fm_matmul(sl, win, QM0 + m * 128, 128, wn)
            sch.op("dve", lambda e: e.tensor_copy(out=qmT[:, m, :], in_=pg[k][:]), reads=[f"pg{k}"], writes=["qmT"])

        vmask, onesh = V["vmask"], V["onesh"]
        units = [(pr, hb, mt) for pr in range(2) for hb in range(2) for mt in range(2)]
        kq = {}

        def m_qk(i):
            pr, hb, mt = units[i]
            p0 = hb * 64
            k = rpg.next()
            kq[i] = k
            sch.op("pe", lambda e: e.matmul(pg[k][:], lhsT=kmemT[p0:p0 + 64, pr, mt * 128:(mt + 1) * 128],
                                            rhs=qmT[p0:p0 + 64, pr, :], start=True, stop=True),
                   reads=["kmemT", "qmT"], writes=[f"pg{k}"])

        def m_rest(i):
            pr, hb, mt = units[i]
            k = kq[i]
            n = hb * 2 + mt
            kp = rpT.next()
            sch.op("act", lambda e: e.activation(out=pTm[kp][:], in_=pg[k][:], func=AF.Exp, scale=SCALE_M),
                   reads=[f"pg{k}"], writes=[f"pTm{kp}"])
            def pv(e):
                e.matmul(pom[:], lhsT=vmask[:, mt, 2 * pr + hb, :], rhs=pTm[kp][:], start=(n == 0), stop=(n == 3))
                return e.matmul(pzm[:], lhsT=onesh[:, hb, :], rhs=pTm[kp][:], start=(n == 0), stop=(n == 3))
            sch.op("pe", pv, reads=["vmask", "onesh", f"pTm{kp}"], writes=["pom", "pzm"])
            if n == 3:
                cs_ = rcgm.next()
                sch.op("act", lambda e: e.activation(out=rz[:], in_=pzm[:], func=AF.Ln), reads=["pzm"], writes=["rzm"])
                sch.op("act", lambda e: e.activation(out=rz[:], in_=rz[:], func=AF.Exp, scale=-1.0),
                       reads=["rzm"], writes=["rzm"])
                sch.op("dve", lambda e: e.tensor_tensor(out=om[:], in0=pom[:], in1=rz[:], op=ALU.mult),
                       reads=["pom", "rzm"], writes=["om"])
                sch.op("pool", lambda e: e.tensor_tensor(out=cgm[cs_][:], in0=om[:], in1=gmem[sl][:, pr, :],
                                                         op=ALU.mult),
                       reads=["om", f"gmem{sl}"], writes=[f"cgm{cs_}"])
                if L == "A":
                    dst = T["cgT"][768 + pr * 128:768 + (pr + 1) * 128, tok]
                else:
                    dst = T["cgTB"][8 + pr, :, tok]
                sch.dma(STQ, f"cgm{cs_}", lambda e: e.dma_start(out=dst, in_=cgm[cs_][:]),
                        reads=[f"cgm{cs_}"], writes=["cgT_dram"])

        def mem_attention():
            MAH = 2
            for i in range(MAH):
                m_qk(i)
            for i in range(len(units)):
                if i + MAH < len(units):
                    m_qk(i + MAH)
                m_rest(i)

        def seg1():
            if g == 0:
                for j in range(4):
                    front_a(j)
                    front_b(j)
            for t in nxt:
                load_x(t)
            if L == "B":
                self.proj_B_extra(g, sl, tok, V, part=0)

        def X():
            if L == "A":
                for m in range(8):
                    gate_tile(m)
                for m in range(2):
                    qm_tile(m)
                self.proj_A_latents(g, sl, tok, V)
            else:
                self.proj_B_gates(g, sl, tok, V)
                self.proj_B_extra(g, sl, tok, V, part=3)
                for m in range(2):
                    qm_tile(m)
                self.proj_B_extra(g, sl, tok, V, part=2)
            mem_attention()

        def Y():
            if L == "B":
                self.proj_B_extra(g, sl, tok, V, part=1)
            for t in nxt:
                front_a(t)

        def seg3():
            for t in nxt:
                front_b(t)
            if L == "B":
                self.proj_B_qkv(g, sl, tok, V)

        l1 = sch.record(seg1)
        lx = sch.record(X)
        ly = sch.record(Y)
        l3 = sch.record(seg3)
        return l1 + Sched.merge([lx, ly]) + l3

    def proj_A_latents(self, g, sl, tok, V):
        sch, T = self.sch, self.T
        ones = self.C["ones"]
        win, winsw, fm_matmul, pg, pss, craw, sq, rinv, t1, t2, P, lat, kro = [V[k] for k in (
            "win", "winsw", "fm_matmul", "pg", "pss", "craw", "sq", "rinv", "t1", "t2", "P", "lat", "kro")]
        cs1, cs2 = P["cs1"], P["cs2"]
        SK = _os.environ.get("SKIP", "").split(",")
        for base, nt_, dim in ((0, 3, 384.0), (3, 2, 256.0)):
            for m in range(nt_):
                i5 = base + m
                k = fm_matmul(sl, win, i5 * 128, 128, "winA")
                sch.op("dve", lambda e, i5=i5, k=k: e.tensor_copy(out=craw[:, i5, :], in_=pg[k][:]),
                       reads=[f"pg{k}"], writes=[f"craw{i5}"])
                sch.op("act", lambda e, i5=i5, k=k: e.activation(out=sq[:, i5, :], in_=craw[:, i5, :], func=AF.Square),
                       reads=[f"craw{i5}"], writes=[f"sq{i5}"])
            def f(e, base=base, nt_=nt_):
                for m in range(nt_):
                    i = e.matmul(pss[:], lhsT=ones[:], rhs=sq[:, base + m, :], start=(m == 0), stop=(m == nt_ - 1))
                return i
            sch.op("pe", f, reads=["ones"] + [f"sq{base + m}" for m in range(nt_)], writes=["pss"])
            sch.op("act", lambda e, dim=dim: e.activation(out=rinv[:], in_=pss[:], func=AF.Ln, scale=1.0 / dim,
                                                         bias=EPS), reads=["pss"], writes=["rinv"])
            sch.op("act", lambda e: e.activation(out=rinv[:], in_=rinv[:], func=AF.Exp, scale=-0.5),
                   reads=["rinv"], writes=["rinv"])
            for m in range(nt_):
                eng = "dve" if (m % 2 == 0 or "latpool" in SK) else "pool"
                sch.op(eng, lambda e, m=m, base=base: e.tensor_tensor(
                    out=lat[sl][:, base + m, :], in0=craw[:, base + m, :], in1=rinv[:], op=ALU.mult),
                    reads=[f"craw{base + m}", "rinv"], writes=[f"lat{sl}"])
        if "latdma" not in SK:
          sch.dma(STQ, f"latq{sl}", lambda e: e.dma_start(
            out=T["cqT"][:, tok].rearrange("(c p) t -> p c t", p=128), in_=lat[sl][:, 0:3, :]),
            reads=[f"lat{sl}"], writes=["lat_dram"])
        if "latdma" not in SK:
          sch.dma(STQ, f"latkv{sl}", lambda e: e.dma_start(
            out=T["ckvT"][:, tok].rearrange("(c p) t -> p c t", p=128), in_=lat[sl][:, 3:5, :]),
            reads=[f"lat{sl}"], writes=["lat_dram"])
        if "krope" in SK:
            return
        kn_ = fm_matmul(sl, win, 640, 64, "winA")
        ks_ = fm_matmul(sl, winsw, 0, 64, "winswA")
        sch.op("dve", lambda e: e.tensor_tensor(out=t1[:], in0=pg[kn_][0:64, :], in1=cs1[:, tok], op=ALU.mult),
               reads=[f"pg{kn_}", f"cs{(g * 512) // min(self.S, 1024)}"], writes=["rp_t1"])
        sch.op("dve", lambda e: e.tensor_tensor(out=t2[:], in0=pg[ks_][0:64, :], in1=cs2[:, tok], op=ALU.mult),
               reads=[f"pg{ks_}", f"cs{(g * 512) // min(self.S, 1024)}"], writes=["rp_t2"])
        sch.op("pool", lambda e: e.tensor_tensor(out=kro[sl][:], in0=t1[:], in1=t2[:], op=ALU.add),
               reads=["rp_t1", "rp_t2"], writes=[f"kro{sl}"])
        sch.dma(STQ, f"kro{sl}", lambda e: e.dma_start(out=T["krT"][:, tok], in_=kro[sl][:]),
                reads=[f"kro{sl}"], writes=["lat_dram"])

    def attn_phase(self, cs1, cs2):
        nc, sch, S, T = self.nc, self.sch, self.S, self.T
        NG, NT = self.NG, self.NT
        C = self.C
        ident, ones, maskT = C["ident"], C["ones"], C["maskT"]
        es = sch.begin()
        cqnT = self.sb(es, "cqnT", [128, 3, S], BF16)
        ckvnT = self.sb(es, "ckvnT", [128, 2, S], BF16)
        kropeT = self.sb(es, "kropeT", [128, S], BF16)
        sch.op("pool", lambda e: e.memset(kropeT[64:128, :], 0.0), writes=["kropeTz"])
        wuq = self.sb(es, "wuq", [128, 3, 1152], BF16)
        wuqsw = self.sb(es, "wuqsw", [128, 3, 6, 64], BF16)
        wukv = self.sb(es, "wukv", [128, 2, 1536], BF16)
        self.wstage(es, 1536)
        self.load_weight(es, wuq, T["a_w_uq"], 1152, [(0, 128), (128, 128), (256, 128)], T["a_q_a_g"], tag="wuq")
        for h in range(6):
            sch.op("dve", lambda e, h=h: e.tensor_copy(out=wuqsw[:, :, h, 0:32],
                                                      in_=wuq[:, :, h * 192 + 160:h * 192 + 192]),
                   reads=["wuq"], writes=["wuqsw"])
            sch.op("dve", lambda e, h=h: e.tensor_copy(out=wuqsw[:, :, h, 32:64],
                                                      in_=wuq[:, :, h * 192 + 128:h * 192 + 160]),
                   reads=["wuq"], writes=["wuqsw"])
        self.load_weight(es, wukv, T["a_w_ukv"], 1536, [(0, 128), (128, 128)], T["a_kv_a_g"], tag="wukv")
        for g in range(NG):
            tk = slice(g * 512, (g + 1) * 512)
            sch.dma("sp", f"ldq{g % 2}", lambda e, tk=tk: e.dma_start(
                out=cqnT[:, :, tk], in_=T["cqT"][:, tk].rearrange("(c p) t -> p c t", p=128)), writes=["cqnT"])
            sch.dma("sp", f"ldkv{g % 2}", lambda e, tk=tk: e.dma_start(
                out=ckvnT[:, :, tk], in_=T["ckvT"][:, tk].rearrange("(c p) t -> p c t", p=128)), writes=["ckvnT"])
        sch.dma("sp", "ldkr", lambda e: e.dma_start(out=kropeT[0:64, :], in_=T["krT"]), writes=["kropeT"])
        qn = [self.sb(es, f"qn{i}", [128, S], BF16) for i in range(2)]
        qr = [self.sb(es, f"qr{i}", [128, S], BF16) for i in range(2)]
        for i_ in range(2):
            sch.op("pool", lambda e, i_=i_: e.memset(qr[i_][64:128, :], 0.0), writes=[f"qrz{i_}"])
        kn = [self.sb(es, f"kn{i}", [128, S], BF16) for i in range(2)]
        vv = [self.sb(es, f"vv{i}", [128, NT, 128], BF16) for i in range(2)]
        pT = [self.sb(es, f"pT{i}", [128, 512], BF16) for i in range(3)]
        t1 = self.sb(es, "at_t1", [64, 512], F32)
        t2 = self.sb(es, "at_t2", [64, 512], F32)
        rz = [self.sb(es, f"rz{i}", [128, 512], F32) for i in range(2)]
        of = [self.sb(es, f"of{i}", [128, 512], F32) for i in range(2)]
        gl = [self.sb(es, f"gl{i}", [128, 512], BF16) for i in range(2)]
        cg = [self.sb(es, f"cg{i}", [128, 512], BF16) for i in range(2)]
        NSC = 4
        psc = [self.ps(es, f"psc{i}", [128, 512], F32) for i in range(NSC)]
        po = [self.ps(es, f"po{i}", [128, 512], F32) for i in range(2)]
        pz = [self.ps(es, f"pz{i}", [128, 512], F32) for i in range(2)]
        pq = psc
        zacc = [[self.sb(es, f"zacc{i}_{p}", [128, 512], F32) for p in range(2)] for i in range(2)]
        onesf = self.sb(es, "onesf_a", [128, 128], F32)
        sch.op("dve", lambda e: e.memset(onesf[:], 1.0), writes=["onesf"])
        rpq = Rot("pq", NSC)

        def prod_group(h, hs, g):
            tok = slice(g * 512, (g + 1) * 512)
            k = rpq.next()
            def f(e):
                for c in range(3):
                    i = e.matmul(pq[k][:], lhsT=wuq[:, c, h * 192:h * 192 + 128], rhs=cqnT[:, c, tok],
                                 start=(c == 0), stop=(c == 2))
                return i
            sch.op("pe", f, reads=["wuq", "cqnT"], writes=[f"psc{k}"])
            sch.op("act", lambda e: e.activation(out=qn[hs][:, tok], in_=pq[k][:], func=AF.Copy),
                   reads=[f"psc{k}"], writes=[f"qn{hs}_{g}"])
            k1 = rpq.next()
            def f1(e):
                for c in range(3):
                    i = e.matmul(pq[k1][0:64, :], lhsT=wuq[:, c, h * 192 + 128:h * 192 + 192], rhs=cqnT[:, c, tok],
                                 start=(c == 0), stop=(c == 2))
                return i
            sch.op("pe", f1, reads=["wuq", "cqnT"], writes=[f"psc{k1}"])
            sch.op("dve", lambda e: e.tensor_tensor(out=t1[:], in0=pq[k1][0:64, :], in1=cs1[:, tok], op=ALU.mult),
                   reads=[f"psc{k1}", "cs"], writes=["at_t1"])
            k2 = rpq.next()
            def f2(e):
                for c in range(3):
                    i = e.matmul(pq[k2][0:64, :], lhsT=wuqsw[:, c, h, :], rhs=cqnT[:, c, tok],
                                 start=(c == 0), stop=(c == 2))
                return i
            sch.op("pe", f2, reads=["wuqsw", "cqnT"], writes=[f"psc{k2}"])
            sch.op("dve", lambda e: e.tensor_tensor(out=t2[:], in0=pq[k2][0:64, :], in1=cs2[:, tok], op=ALU.mult),
                   reads=[f"psc{k2}", "cs"], writes=["at_t2"])
            sch.op("pool", lambda e: e.tensor_tensor(out=qr[hs][0:64, tok], in0=t1[:], in1=t2[:], op=ALU.add),
                   reads=["at_t1", "at_t2"], writes=[f"qr{hs}_{g}"])
            k3 = rpq.next()
            def f3(e):
                for c in range(2):
                    i = e.matmul(pq[k3][:], lhsT=wukv[:, c, h * 256:h * 256 + 128], rhs=ckvnT[:, c, tok],
                                 start=(c == 0), stop=(c == 1))
                return i
            sch.op("pe", f3, reads=["wukv", "ckvnT"], writes=[f"psc{k3}"])
            sch.op("act", lambda e: e.activation(out=kn[hs][:, tok], in_=pq[k3][:], func=AF.Copy),
                   reads=[f"psc{k3}"], writes=[f"kn{hs}_{g}"])
            k4 = rpq.next()
            def f4(e):
                for j in range(4):
                    t0 = g * 512 + j * 128
                    for c in range(2):
                        i = e.matmul(pq[k4][:, j * 128:(j + 1) * 128], lhsT=ckvnT[:, c, t0:t0 + 128],
                                     rhs=wukv[:, c, h * 256 + 128:h * 256 + 256], start=(c == 0), stop=(c == 1))
                return i
            sch.op("pe", f4, reads=["wukv", "ckvnT"], writes=[f"psc{k4}"])
            sch.op("dve", lambda e: e.tensor_copy(
                out=vv[hs][:, g * 4:(g + 1) * 4, :].rearrange("p j d -> p (j d)"), in_=pq[k4][:]),
                reads=[f"psc{k4}"], writes=[f"vv{hs}_{g}"])

        def emit_qk(h, hs, j, uu, kt):
            q0 = j * 512
            r = kt - 4 * j
            c0 = r * 128 if r > 0 else 0
            sb_ = uu % NSC
            def f(e):
                e.matmul(psc[sb_][:, c0:512], lhsT=kn[hs][:, kt * 128:(kt + 1) * 128],
                         rhs=qn[hs][:, q0 + c0:q0 + 512], start=True, stop=False)
                i = e.matmul(psc[sb_][:, c0:512], lhsT=kropeT[:, kt * 128:(kt + 1) * 128],
                             rhs=qr[hs][:, q0 + c0:q0 + 512], start=False, stop=(r < 0))
                if r >= 0:
                    i = e.matmul(psc[sb_][:, c0:c0 + 128], lhsT=ident[:], rhs=maskT[:], start=False, stop=True)
                return i
            sch.op("pe", f, reads=[f"kn{hs}_{kt // 4}", f"qn{hs}_{j}", f"qr{hs}_{j}", f"qrz{hs}", "kropeT", "kropeTz",
                                   "ident", "maskT"], writes=[f"psc{sb_}"])
            return c0

        def emit_rest(h, hs, j, js, uu, kt, c0, last):
            sb_ = uu % NSC
            pb = uu % 3
            sch.op("act", lambda e: e.activation(out=pT[pb][:, c0:512], in_=psc[sb_][:, c0:512], func=AF.Exp,
                                                 scale=SCALE_A),
                   reads=[f"psc{sb_}"], writes=[f"pT{pb}"])
            def pv(e):
                return e.matmul(po[js][:, c0:512], lhsT=vv[hs][:, kt, :], rhs=pT[pb][:, c0:512],
                                start=(kt == 0), stop=last)
            sch.op("pe", pv, reads=[f"vv{hs}_{kt // 4}", f"pT{pb}"], writes=[f"po{js}"])
            if kt % 6 in (3, 5):
                sch.op("pe", lambda e: e.matmul(pz[js][:, c0:512], lhsT=ones[:], rhs=pT[pb][:, c0:512],
                                                start=(kt == 3), stop=False, skip_group_check=True),
                       reads=["ones", f"pT{pb}"], writes=[f"pz{js}"])
                return
            par = 1 if kt % 6 == 1 else 0
            eng = "dve" if par == 0 else "pool"
            if kt < 2:
                if c0 > 0:
                    sch.op(eng, lambda e: e.memset(zacc[js][par][:, 0:c0], 0.0), writes=[f"zacc{js}_{par}"])
                sch.op(eng, lambda e: e.tensor_copy(out=zacc[js][par][:, c0:512], in_=pT[pb][:, c0:512]),
                       reads=[f"pT{pb}"], writes=[f"zacc{js}_{par}"])
            else:
                sch.op(eng, lambda e: e.tensor_tensor(out=zacc[js][par][:, c0:512], in0=zacc[js][par][:, c0:512],
                                                      in1=pT[pb][:, c0:512], op=ALU.add),
                       reads=[f"pT{pb}", f"zacc{js}_{par}"], writes=[f"zacc{js}_{par}"])

        def finalize(h, j, js):
            q0 = j * 512
            sch.op("dve", lambda e: e.tensor_tensor(out=zacc[js][0][:], in0=zacc[js][0][:], in1=zacc[js][1][:],
                                                    op=ALU.add),
                   reads=[f"zacc{js}_0", f"zacc{js}_1"], writes=[f"zacc{js}_0"])
            sch.op("pe", lambda e: e.matmul(pz[js][:], lhsT=onesf[:], rhs=zacc[js][0][:], start=False, stop=True,
                                            skip_group_check=True),
                   reads=["onesf", f"zacc{js}_0"], writes=[f"pz{js}"])
            sch.op("act", lambda e: e.activation(out=rz[js][:], in_=pz[js][:], func=AF.Ln),
                   reads=[f"pz{js}"], writes=[f"rz{js}"])
            sch.op("act", lambda e: e.activation(out=rz[js][:], in_=rz[js][:], func=AF.Exp, scale=-1.0),
                   reads=[f"rz{js}"], writes=[f"rz{js}"])
            sch.op("dve", lambda e: e.tensor_tensor(out=of[js][:], in0=po[js][:], in1=rz[js][:], op=ALU.mult),
                   reads=[f"po{js}", f"rz{js}"], writes=[f"of{js}"])
            sch.op("pool", lambda e: e.tensor_tensor(out=cg[js][:], in0=of[js][:], in1=gl[js][:], op=ALU.mult),
                   reads=[f"of{js}", f"gl{js}"], writes=[f"cg{js}"])
            sch.dma(STQ, f"cg{js}", lambda e: e.dma_start(
                out=T["cgT"][h * 128:(h + 1) * 128, q0:q0 + 512], in_=cg[js][:]),
                reads=[f"cg{js}"], writes=["cgT_dram"])

        def gate_load(h, j, js):
            q0 = j * 512
            sch.dma("sp", f"gl{js}", lambda e: e.dma_start(
                out=gl[js][:], in_=T["gT"][h * 128:(h + 1) * 128, q0:q0 + 512]),
                reads=["gT_dram"], writes=[f"gl{js}"])

        fin = 0
        u = 0
        AH = 3
        for h in range(6):
            hs = h % 2
            for g in range(NG):
                prod_group(h, hs, g)
            flat = []
            jsl = {}
            for j in range(NG):
                jsl[j] = fin % 2
                fin += 1
                n = 4 * j + 4
                for kt in range(n):
                    flat.append((j, jsl[j], kt, kt == n - 1))
            c0s = {}
            pending = []
            for a_ in range(min(AH, len(flat))):
                j_, js_, kt_, _ = flat[a_]
                c0s[a_] = emit_qk(h, hs, j_, u + a_, kt_)
            for idx, (j, js, kt, last) in enumerate(flat):
                if kt == 0:
                    gate_load(h, j, js)
                if idx + AH < len(flat):
                    j_, js_, kt_, _ = flat[idx + AH]
                    c0s[idx + AH] = emit_qk(h, hs, j_, u + idx + AH, kt_)
                emit_rest(h, hs, j, js, u + idx, kt, c0s[idx], last)
                if last:
                    pending.append((idx + 2, j, js))
                while pending and pending[0][0] <= idx:
                    _, pj, pjs = pending.pop(0)
                    finalize(h, pj, pjs)
            for _, pj, pjs in pending:
                finalize(h, pj, pjs)
            u += len(flat)
        sch.end()

    def out_weights(self, es, L, wout_ap, postg_ap, kchunks, wout, postg):
        sch = self.sch
        sch.dma("sp", f"postg{L}", lambda e: e.dma_start(out=postg[:], in_=postg_ap.partition_broadcast(128)),
                writes=["postg"])
        self.wstage(es, D)
        self.load_weight(es, wout, wout_ap, D, kchunks, None, tag=f"wout{L}")

    def out_phase(self, L, wout_ap, postg_ap, x_src, x_dst, kchunks, extra_fn=None, pre=None):
        nc, sch, S, T = self.nc, self.sch, self.S, self.T
        NG = self.NG
        nk = len(kchunks)
        ksz = [max(p0 + n for (_, n, p0) in ch) if isinstance(ch, list) else ch[1] for ch in kchunks]
        es = sch.begin()
        if pre is not None:
            wout, postg = pre
        else:
            wout = self.sb(es, f"wout{L}", [128, nk, D], BF16)
            postg = self.sb(es, f"postg{L}", [128, D], F32)
            self.out_weights(es, L, wout_ap, postg_ap, kchunks, wout, postg)
        cgs = [self.sb(es, f"cgs{i}", [128, nk, 512], BF16) for i in range(2)]
        xs = [self.sb(es, f"oxs{i}", [128, 4, D], F32) for i in range(2)]
        tt = [self.sb(es, f"ott{i}", [128, D], F32) for i in range(3)]
        junk = self.sb(es, "ojunk", [128, D], BF16)
        ssq = [self.sb(es, f"ossq{i}", [128, 3], F32) for i in range(3)]
        py = [[self.ps(es, f"py{i}_{n}", [128, 512], F32) for n in range(2)] for i in range(3)]
        cg_ap = T["cgT"] if L == "A" else T["cgTB"]

        def load(g):
            sl = g % 2
            tok = slice(g * 512, (g + 1) * 512)
            if L == "A":
                sch.dma("sp", f"cgs{sl}", lambda e: e.dma_start(
                    out=cgs[sl][:], in_=cg_ap[:, tok].rearrange("(c p) t -> p c t", p=128)),
                    reads=["cgT_dram"], writes=[f"cgs{sl}"])
            else:
                sch.dma("sp", f"cgs{sl}", lambda e: e.dma_start(
                    out=cgs[sl][:, 0:4, :], in_=cg_ap[0:8:2, :, tok].rearrange("c p t -> p c t")),
                    reads=["cgT_dram"], writes=[f"cgs{sl}"])
                for i_, (fa, fb) in enumerate(((1, 3), (5, 7))):
                    sch.dma("sp", f"cgsa{sl}", lambda e, i_=i_, fa=fa: e.dma_start(
                        out=cgs[sl][0:64, 4 + i_, :], in_=cg_ap[fa, 0:64, tok]),
                        reads=["cgT_dram"], writes=[f"cgs{sl}"])
                    sch.dma("sp", f"cgsb{sl}", lambda e, i_=i_, fb=fb: e.dma_start(
                        out=cgs[sl][64:128, 4 + i_, :], in_=cg_ap[fb, 0:64, tok]),
                        reads=["cgT_dram"], writes=[f"cgs{sl}"])
                sch.dma("sp", f"cgsm{sl}", lambda e: e.dma_start(
                    out=cgs[sl][:, 6:8, :], in_=cg_ap[8:10, :, tok].rearrange("c p t -> p c t")),
                    reads=["cgT_dram"], writes=[f"cgs{sl}"])
            sch.dma("sp", f"oxs{sl}", lambda e: e.dma_start(
                out=xs[sl][:], in_=x_src[tok, :].rearrange("(j p) d -> p j d", p=128)),
                reads=["xsrc_dram"], writes=[f"oxs{sl}"])

        def tile(g, sl, j, b):
            def f(e):
                for n in range(2):
                    for c, sz in enumerate(ksz):
                        i = e.matmul(py[b][n][:], lhsT=cgs[sl][0:sz, c, j * 128:(j + 1) * 128],
                                     rhs=wout[0:sz, c, n * 512:(n + 1) * 512], start=(c == 0), stop=(c == nk - 1))
                return i
            sch.op("pe", f, reads=[f"cgs{sl}", f"wout{L}"], writes=[f"py{b}_0", f"py{b}_1"])
            for n in range(2):
                sch.op("act", lambda e, n=n: e.activation(out=junk[:, 0:512], in_=py[b][n][:], func=AF.Square,
                                                         accum_out=ssq[b][:, n:n + 1]),
                       reads=[f"py{b}_{n}"], writes=[f"ossq{b}"])
            sch.op("dve", lambda e: e.tensor_tensor(out=ssq[b][:, 2:3], in0=ssq[b][:, 0:1], in1=ssq[b][:, 1:2],
                                                    op=ALU.add), reads=[f"ossq{b}"], writes=[f"ossq{b}"])
            sch.op("act", lambda e: e.activation(out=ssq[b][:, 2:3], in_=ssq[b][:, 2:3], func=AF.Ln, scale=1.0 / D,
                                                 bias=EPS), reads=[f"ossq{b}"], writes=[f"ossq{b}"])
            sch.op("act", lambda e: e.activation(out=ssq[b][:, 2:3], in_=ssq[b][:, 2:3], func=AF.Exp, scale=-0.5),
                   reads=[f"ossq{b}"], writes=[f"ossq{b}"])
            for n in range(2):
                sch.op("dve", lambda e, n=n: e.scalar_tensor_tensor(
                    out=tt[b][:, n * 512:(n + 1) * 512], in0=py[b][n][:], scalar=ssq[b][:, 2:3],
                    in1=postg[:, n * 512:(n + 1) * 512], op0=ALU.mult, op1=ALU.mult),
                    reads=[f"py{b}_{n}", f"ossq{b}", "postg"], writes=[f"ott{b}"])
            sch.op("pool", lambda e: e.tensor_tensor(out=xs[sl][:, j, :], in0=xs[sl][:, j, :], in1=tt[b][:],
                                                     op=ALU.add),
                   reads=[f"ott{b}", f"oxs{sl}"], writes=[f"oxs{sl}"])

        def store(g):
            sl = g % 2
            tok = slice(g * 512, (g + 1) * 512)
            sch.dma(STQ, f"ost{sl}", lambda e: e.dma_start(
                out=x_dst[tok, :].rearrange("(j p) d -> p j d", p=128), in_=xs[sl][:]),
                reads=[f"oxs{sl}"], writes=["xdst_dram"])

        extra = sch.record(lambda: extra_fn(es)) if extra_fn is not None else []
        npart = NG
        load(0)
        it = 0
        for g in range(NG):
            if g + 1 < NG:
                load(g + 1)
            body = []
            for j in range(4):
                body += sch.record(lambda j=j: tile(g, g % 2, j, it % 3))
                it += 1
            lo, hi = (len(extra) * g) // npart, (len(extra) * (g + 1)) // npart
            sch.zip_emit([body, extra[lo:hi]])
            store(g)
        sch.end()

    def layer_B(self, top):
        nc, sch, S, T = self.nc, self.sch, self.S, self.T
        NT = self.NT
        C = self.C
        ident = C["ident"]
        KC8 = [(c * 128, 128) for c in range(8)]
        for nm, shp, dt in [("gTB", [10, 128, S], BF16), ("cgTB", [10, 128, S], BF16), ("ucTB", [8, 128, S], BF16),
                            ("soTB", [8, 128, S], BF16), ("qTB", [4, 96, S], BF16), ("kTB", [4, 96, S], BF16),
                            ("ktokB", [S, 384], BF16), ("vtokB", [S, 768], BF16), ("ifB", [2, 4, S], F32)]:
            self.dscratch(nm, shp, dt)
        with ExitStack() as LB:
            kch = [[(FT[2 * i][0], 128, 0)] for i in range(4)]
            kch += [[(FT[1][0], 64, 0), (FT[3][0], 64, 64)], [(FT[5][0], 64, 0), (FT[7][0], 64, 64)]]
            kch += [[(768, 128, 0)], [(896, 128, 0)]]
            woutB = self.sb(LB, "woutB", [128, 8, D], BF16)
            postgB = self.sb(LB, "postgB", [128, D], F32)
            wsT = self.sb(LB, "wsT", [128, NT, 4], F32)
            eT = self.sb(LB, "eT", [128, NT, 4], F32)
            abc = self.sb(LB, "abc", [128, 4, NT], F32)
            with ExitStack() as LB1:
                win = self.sb(LB1, "winB", [128, 8, B_COLS], BF16)
                kmemT = self.sb(LB1, "kmemTB", [128, 2, 256], BF16)
                vmem = self.sb(LB1, "vmemB", [128, 2, 4, 128], BF16)
                wq = self.sb(LB1, "wqB", [128, 8, 96], BF16)
                wk = self.sb(LB1, "wkB", [128, 8, 96], BF16)
                wv = self.sb(LB1, "wvB", [128, 8, 192], BF16)
                cvw = self.sb(LB1, "cvw", [128, 8, 4], F32)
                cvb = self.sb(LB1, "cvb", [128, 8], F32)
                def b0(es):
                  if True:
                    self.wstage(es, B_COLS, n=3)
                    wmkv = self.sb(es, "wmkvB", [128, 8, 512], BF16)
                    memnT = self.sb(es, "memnTB", [128, 8, 256], BF16)
                    self.load_weight(es, win, T["b_w_in"], B_COLS, KC8, T["b_pre_g"], tag="winB")
                    self.load_weight(es, wmkv, T["b_w_mem_kv"], 512, KC8, T["b_mem_g"], tag="memBw")
                    self.load_weight(es, wq, T["b_w_q"], 96, FT, None, tag="wqB")
                    self.load_weight(es, wk, T["b_w_k"], 96, FT, None, tag="wkB")
                    self.load_weight(es, wv, T["b_w_v"], 192, FT, None, tag="wvB")
                    sch.op("dve", lambda e: e.memset(cvw[:], 0.0), writes=["cvw"])
                    sch.op("dve", lambda e: e.memset(cvb[:], 0.0), writes=["cvb"])
                    for ft, (r0, sz) in enumerate(FTP):
                        sz = min(sz, 768 - r0)
                        sch.dma("sp", f"cvw{ft % 2}", lambda e, ft=ft, r0=r0, sz=sz: e.dma_start(
                            out=cvw[0:sz, ft, :], in_=T["b_conv_wT"][r0:r0 + sz, :]), writes=["cvw"])
                        sch.dma("sp", f"cvb{ft % 2}", lambda e, ft=ft, r0=r0, sz=sz: e.dma_start(
                            out=cvb[0:sz, ft:ft + 1], in_=T["b_conv_b"][r0:r0 + sz].rearrange("(p o) -> p o", o=1)),
                            writes=["cvb"])
                    self.norm_rows_T(es, T["mem"], NMEM, memnT, "memBn", ident)
                    self.mem_kv(es, wmkv, memnT, kmemT, vmem, "memB")
                KC8 = [(c * 128, 128) for c in range(8)]
                self.out_phase("A", T["a_w_out"], T["a_post_g"], T["x"], T["x1"], KC8)
                es = sch.begin()
                b0(es)
                sch.end()
                es = sch.begin()
                self.proj_phase(es, "B", T["x1"], win, None, kmemT, vmem,
                                dict(wq=wq, wk=wk, wv=wv, cvw=cvw, cvb=cvb))
                sch.end()
            if self.stop == "B1":
                return
            self.gate_phase(wsT, eT, abc)
            if self.stop == "B2":
                return
            self.chunk_phase(wsT, eT, abc, prefetch=lambda es: self.out_weights(
                es, "B", T["b_w_out"], T["b_post_g"], kch, woutB, postgB))
            if self.stop == "B3":
                return
            self.out_phase("B", T["b_w_out"], T["b_post_g"], T["x1"], T["out"], kch, pre=(woutB, postgB))

    def proj_B_alloc(self, es, V):
        V["ug"] = [self.sb(es, f"ug{i}", [128, 8, 515], BF16) for i in range(2)]
        V["acc"] = [self.sb(es, f"cacc{i}", [128, 512], F32) for i in range(8)]
        V["ucg"] = [self.sb(es, f"ucg{i}", [128, 8, 512], BF16) for i in range(2)]
        V["sot"] = [self.sb(es, f"sot{i}", [128, 512], BF16) for i in range(3)]
        V["qkt"] = [self.sb(es, f"qkt{i}", [96, 512], BF16) for i in range(3)]
        V["ktk"] = [self.sb(es, f"ktk{i}", [128, 384], BF16) for i in range(2)]
        V["vtk"] = [self.sb(es, f"vtk{i}", [128, 768], BF16) for i in range(2)]
        V["ift"] = [self.sb(es, f"ift{i}", [4, 512], F32) for i in range(2)]
        V["rsot"] = Rot("sot", 3)
        V["rqkt"] = Rot("qkt", 3)
        V["rktk"] = Rot("ktk", 2)
        V["rvtk"] = Rot("vtk", 2)
        V["rift"] = Rot("ift", 2)
        V["racc"] = Rot("cacc", 2)
        ug = V["ug"]
        self.sch.op("dve", lambda e: e.memset(ug[0][:], 0.0), writes=[f"ug0_{ft}" for ft in range(8)] + ["ugh0"])
        self.sch.op("dve", lambda e: e.memset(ug[1][:], 0.0), writes=[f"ug1_{ft}" for ft in range(8)] + ["ugh1"])

    def proj_B_gates(self, g, sl, tok, V):
        sch, T = self.sch, self.T
        win, fm_matmul, pg, gt, gmem, rgt, GATE0 = [V[k] for k in ("win", "fm_matmul", "pg", "gt", "gmem", "rgt", "GATE0")]
        for ft, (r0, sz) in enumerate(FTP):
            k = fm_matmul(sl, win, GATE0 + r0, sz, "winB")
            gs = rgt.next()
            sch.op("act", lambda e, k=k, gs=gs, sz=sz: e.activation(out=gt[gs][0:sz, :], in_=pg[k][0:sz, :], func=AF.Silu),
                   reads=[f"pg{k}"], writes=[f"gt{gs}"])
            sch.dma(STQ, f"gt{gs}", lambda e, gs=gs, ft=ft, sz=sz: e.dma_start(
                out=T["gTB"][ft, 0:sz, tok], in_=gt[gs][0:sz, :]), reads=[f"gt{gs}"], writes=["gT_dram"])
        for m in range(2):
            k = fm_matmul(sl, win, GATE0 + 768 + m * 128, 128, "winB")
            sch.op("act", lambda e, k=k, m=m: e.activation(out=gmem[sl][:, m, :], in_=pg[k][:], func=AF.Silu),
                   reads=[f"pg{k}"], writes=[f"gmem{sl}"])

    def proj_B_extra(self, g, sl, tok, V, part):
        sch, T = self.sch, self.T
        win, fm_matmul, pg, P = V["win"], V["fm_matmul"], V["pg"], V["P"]
        ug, acc, ucg, sot, ift = [V[k] for k in ("ug", "acc", "ucg", "sot", "ift")]
        rsot, rift = V["rsot"], V["rift"]
        cvw, cvb = P["cvw"], P["cvb"]
        us = g % 2
        if part == 0:
            for ft, (r0, sz) in enumerate(FTP):
                k = fm_matmul(sl, win, r0, sz, "winB")
                sch.op("dve", lambda e, k=k, ft=ft, sz=sz: e.tensor_copy(out=ug[us][0:sz, ft, 3:515], in_=pg[k][0:sz, :]),
                       reads=[f"pg{k}"], writes=[f"ug{us}_{ft}"])
            return
        if part == 1:
            for ft, (r0, sz) in enumerate(FTP):
                sch.op("act", lambda e, ft=ft, sz=sz: e.activation(
                    out=acc[ft][0:sz, :], in_=ug[us][0:sz, ft, 0:512], func=AF.Identity,
                    scale=cvw[0:sz, ft, 0:1], bias=cvb[0:sz, ft:ft + 1]),
                    reads=[f"ug{us}_{ft}", f"ugh{us}", "cvw", "cvb"], writes=[f"cacc{ft}"])
            for j in range(1, 4):
                for ft, (r0, sz) in enumerate(FTP):
                    sch.op("dve", lambda e, ft=ft, sz=sz, j=j: e.scalar_tensor_tensor(
                        out=acc[ft][0:sz, :], in0=ug[us][0:sz, ft, j:j + 512], scalar=cvw[0:sz, ft, j:j + 1],
                        in1=acc[ft][0:sz, :], op0=ALU.mult, op1=ALU.add),
                        reads=[f"ug{us}_{ft}", f"ugh{us}", "cvw", f"cacc{ft}"], writes=[f"cacc{ft}"])
            for ft, (r0, sz) in enumerate(FTP):
                sch.op("act", lambda e, ft=ft, sz=sz: e.activation(out=ucg[us][0:sz, ft, :], in_=acc[ft][0:sz, :],
                                                                  func=AF.Silu),
                       reads=[f"cacc{ft}"], writes=[f"ucg{us}"])
            sch.op("dve", lambda e: e.tensor_copy(out=ug[1 - us][:, :, 0:3], in_=ug[us][:, :, 512:515]),
                   reads=[f"ug{us}_{ft}" for ft in range(8)], writes=[f"ugh{1 - us}"])
            sch.dma(STQ, f"ucst{us}", lambda e: e.dma_start(out=T["ucTB"][:, :, tok].rearrange("f p t -> p f t"),
                                                             in_=ucg[us][:]), reads=[f"ucg{us}"], writes=["uc_dram"])
            return
        if part == 3:
            for ft, (r0, sz) in enumerate(FTP):
                k = fm_matmul(sl, win, 776 + r0, sz, "winB")
                ss_ = rsot.next()
                sch.op("act", lambda e, k=k, ss_=ss_, sz=sz: e.activation(out=sot[ss_][0:sz, :], in_=pg[k][0:sz, :],
                                                                         func=AF.Sigmoid),
                       reads=[f"pg{k}"], writes=[f"sot{ss_}"])
                sch.dma(STQ, f"sot{ss_}", lambda e, ss_=ss_, ft=ft, sz=sz: e.dma_start(
                    out=T["soTB"][ft, 0:sz, tok], in_=sot[ss_][0:sz, :]), reads=[f"sot{ss_}"], writes=["so_dram"])
            return
        for w_ in range(2):
            k = fm_matmul(sl, win, 768 + 4 * w_, 4, "winB")
            is_ = rift.next()
            sch.op("dve", lambda e, k=k, is_=is_: e.tensor_copy(out=ift[is_][:], in_=pg[k][0:4, :]),
                   reads=[f"pg{k}"], writes=[f"ift{is_}"])
            sch.dma(STQ, f"ift{is_}", lambda e, is_=is_, w_=w_: e.dma_start(out=T["ifB"][w_, :, tok], in_=ift[is_][:]),
                    reads=[f"ift{is_}"], writes=["if_dram"])

    def proj_B_qkv(self, g, sl, tok, V):
        sch, T = self.sch, self.T
        pg, P, rpg = V["pg"], V["P"], V["rpg"]
        ug, ucg, qkt, ktk, vtk = [V[k] for k in ("ug", "ucg", "qkt", "ktk", "vtk")]
        rqkt, rktk, rvtk = V["rqkt"], V["rktk"], V["rvtk"]
        wq, wk, wv = P["wq"], P["wk"], P["wv"]
        us = g % 2
        ugr = [f"ug{us}_{ft}" for ft in range(8)]
        for h in range(4):
            for which, wt, wn, dname, scale in (("q", wq, "wqB", "qTB", 1.0), ("k", wk, "wkB", "kTB", SCALE_K)):
                k = rpg.next()
                def f(e, k=k, wt=wt, h=h):
                    for i_, (ft, sz) in enumerate(((2 * h, 128), (2 * h + 1, 64))):
                        ins = e.matmul(pg[k][0:96, :], lhsT=wt[0:sz, ft, :], rhs=ucg[us][0:sz, ft, :],
                                       start=(i_ == 0), stop=(i_ == 1))
                    return ins
                sch.op("pe", f, reads=[wn, f"ucg{us}"], writes=[f"pg{k}"])
                qs = rqkt.next()
                sch.op("act", lambda e, k=k, qs=qs, scale=scale: e.activation(out=qkt[qs][:], in_=pg[k][0:96, :],
                                                                             func=AF.Copy, scale=scale),
                       reads=[f"pg{k}"], writes=[f"qkt{qs}"])
                sch.dma(STQ, f"qkt{qs}", lambda e, qs=qs, dname=dname, h=h: e.dma_start(
                    out=T[dname][h, :, tok], in_=qkt[qs][:]), reads=[f"qkt{qs}"], writes=["qk_dram"])
        for j in range(4):
            t0 = j * 128
            k = rpg.next()
            def f(e, k=k, t0=t0):
                for h in range(4):
                    for i_, (ft, sz) in enumerate(((2 * h, 128), (2 * h + 1, 64))):
                        ins = e.matmul(pg[k][:, h * 96:(h + 1) * 96], lhsT=ucg[us][0:sz, ft, t0:t0 + 128],
                                       rhs=wk[0:sz, ft, :], start=(i_ == 0), stop=(i_ == 1))
                return ins
            sch.op("pe", f, reads=["wkB", f"ucg{us}"], writes=[f"pg{k}"])
            ks = rktk.next()
            sch.op("act", lambda e, k=k, ks=ks: e.activation(out=ktk[ks][:], in_=pg[k][:, 0:384], func=AF.Copy,
                                                            scale=SCALE_K), reads=[f"pg{k}"], writes=[f"ktk{ks}"])
            tk = slice(g * 512 + t0, g * 512 + t0 + 128)
            sch.dma(STQ, f"ktk{ks}", lambda e, ks=ks, tk=tk: e.dma_start(out=T["ktokB"][tk, :], in_=ktk[ks][:]),
                    reads=[f"ktk{ks}"], writes=["qk_dram"])
            vs = rvtk.next()
            for half in range(2):
                k = rpg.next()
                def f(e, k=k, t0=t0, half=half):
                    for hh in range(2):
                        h = half * 2 + hh
                        for i_, (ft, sz) in enumerate(((2 * h, 128), (2 * h + 1, 64))):
                            ins = e.matmul(pg[k][:, hh * 192:(hh + 1) * 192], lhsT=ug[us][0:sz, ft, 3 + t0:3 + t0 + 128],
                                           rhs=wv[0:sz, ft, :], start=(i_ == 0), stop=(i_ == 1))
                    return ins
                sch.op("pe", f, reads=["wvB"] + ugr, writes=[f"pg{k}"])
                sch.op("dve", lambda e, k=k, vs=vs, half=half: e.tensor_copy(
                    out=vtk[vs][:, half * 384:(half + 1) * 384], in_=pg[k][:, 0:384]),
                    reads=[f"pg{k}"], writes=[f"vtk{vs}"])
            sch.dma(STQ, f"vtk{vs}", lambda e, vs=vs, tk=tk: e.dma_start(out=T["vtokB"][tk, :], in_=vtk[vs][:]),
                    reads=[f"vtk{vs}"], writes=["qk_dram"])

    def gate_phase(self, wsT, eT, abc):
        nc, sch, S, T = self.nc, self.sch, self.S, self.T
        NT = self.NT
        L = 128
        NC = S // L
        es = sch.begin()
        it = self.sb(es, "g_i", [4, S], F32)
        ft_ = self.sb(es, "g_f", [4, S], F32)
        spl = self.sb(es, "g_spl", [4, S], F32)
        bneg = self.sb(es, "g_bneg", [4, S], F32)
        gg = self.sb(es, "g_g", [4, S], F32)
        GG = self.sb(es, "g_G", [4, S], F32)
        dd = self.sb(es, "g_d", [4, S], F32)
        ws = self.sb(es, "g_ws", [4, S], F32)
        ee = self.sb(es, "g_e", [4, S], F32)
        da = self.sb(es, "g_da", [4, NC], F32)
        aa = self.sb(es, "g_a", [4, NC], F32)
        gb = self.sb(es, "g_gb", [4, 2], F32)
        ngb = self.sb(es, "g_ngb", [4, 1], F32)
        identf = self.sb(es, "identf", [128, 128], F32)
        sel = self.sb(es, "sel", [4, 4, 128], F32)
        pst = self.ps(es, "g_pst", [128, 512], F32)
        sch.dma("sp", "gi", lambda e: e.dma_start(out=it[:], in_=T["ifB"][0]), writes=["g_i"])
        sch.dma("sp", "gf", lambda e: e.dma_start(out=ft_[:], in_=T["ifB"][1]), writes=["g_f"])
        sch.dma("sp", "gb0", lambda e: e.dma_start(out=gb[:, 0:1], in_=T["b_gate_bias"][0:4, :]), writes=["g_gb"])
        sch.dma("sp", "gb1", lambda e: e.dma_start(out=gb[:, 1:2], in_=T["b_gate_bias"][4:8, :]), writes=["g_gb"])
        sch.dma("sp", "idf", lambda e: e.dma_start(out=identf[:], in_=T["c_identf"]), writes=["identf"])
        sch.dma("sp", "sel", lambda e: e.dma_start(out=sel[:].rearrange("p h m -> p (h m)"), in_=T["c_sel"]),
                writes=["sel"])
        sch.op("dve", lambda e: e.tensor_scalar(out=ngb[:], in0=gb[:, 1:2], scalar1=-1.0, scalar2=None, op0=ALU.mult),
               reads=["g_gb"], writes=["g_ngb"])
        sch.op("act", lambda e: e.activation(out=spl[:], in_=ft_[:], func=AF.Exp, scale=-1.0, bias=ngb[:, 0:1]),
               reads=["g_f", "g_ngb"], writes=["g_spl"])
        sch.op("act", lambda e: e.activation(out=spl[:], in_=spl[:], func=AF.Ln, scale=1.0, bias=1.0),
               reads=["g_spl"], writes=["g_spl"])
        sch.op("dve", lambda e: e.tensor_tensor_scan(out=bneg[:], data0=spl[:], data1=spl[:], initial=0.0,
                                                     op0=ALU.add, op1=ALU.max), reads=["g_spl"], writes=["g_bneg"])
        sch.op("dve", lambda e: e.scalar_tensor_tensor(out=gg[:], in0=it[:], scalar=gb[:, 0:1], in1=bneg[:],
                                                       op0=ALU.add, op1=ALU.add),
               reads=["g_i", "g_gb", "g_bneg"], writes=["g_g"])
        sch.op("dve", lambda e: e.tensor_tensor_scan(out=GG[:], data0=gg[:], data1=gg[:], initial=0.0,
                                                     op0=ALU.max, op1=ALU.max), reads=["g_g"], writes=["g_G"])
        Gv = GG[:].rearrange("p (c l) -> p c l", l=L)
        Rv = Gv[:, :, L - 1:L]
        Rb = Rv.broadcast_to([4, NC, L])
        sch.op("dve", lambda e: e.tensor_tensor(out=dd[:].rearrange("p (c l) -> p c l", l=L),
                                                in0=gg[:].rearrange("p (c l) -> p c l", l=L), in1=Rb,
                                                op=ALU.subtract), reads=["g_g", "g_G"], writes=["g_d"])
        sch.op("act", lambda e: e.activation(out=ws[:], in_=dd[:], func=AF.Exp), reads=["g_d"], writes=["g_ws"])
        sch.op("dve", lambda e: e.tensor_tensor(out=dd[:].rearrange("p (c l) -> p c l", l=L),
                                                in0=bneg[:].rearrange("p (c l) -> p c l", l=L), in1=Rb,
                                                op=ALU.subtract), reads=["g_bneg", "g_G", "g_ws"], writes=["g_d"])
        sch.op("act", lambda e: e.activation(out=ee[:], in_=dd[:], func=AF.Exp), reads=["g_d"], writes=["g_e"])
        Rflat = GG[:, L - 1::L] if False else None
        sch.op("dve", lambda e: e.memset(da[:], 0.0), writes=["g_da"])
        if NC > 1:
            sch.op("dve", lambda e: e.tensor_tensor(out=da[:, 1:NC].unsqueeze(2), in0=Rv[:, 0:NC - 1, :],
                                                    in1=Rv[:, 1:NC, :], op=ALU.subtract),
                   reads=["g_G", "g_da"], writes=["g_da"])
        sch.op("act", lambda e: e.activation(out=aa[:], in_=da[:], func=AF.Exp), reads=["g_da"], writes=["g_a"])
        for nm, src, dst in (("g_ws", ws, wsT), ("g_e", ee, eT)):
            def tr(e, src=src):
                for c in range(NT):
                    i = e.transpose(out=pst[:, c * 4:(c + 1) * 4], in_=src[0:4, c * 128:(c + 1) * 128],
                                    identity=identf[0:4, 0:4])
                return i
            sch.op("pe", tr, reads=[nm, "identf"], writes=["g_pst"])
            sch.op("dve", lambda e, dst=dst: e.tensor_copy(out=dst[:].rearrange("p c h -> p (c h)"),
                                                          in_=pst[:, 0:NT * 4]), reads=["g_pst"], writes=[nm + "T"])
        def ab(e):
            for h in range(4):
                i = e.matmul(pst[:, h * NC:(h + 1) * NC], lhsT=sel[0:4, h, :], rhs=aa[0:4, :], start=True, stop=True)
            return i
        sch.op("pe", ab, reads=["sel", "g_a"], writes=["g_pst"])
        sch.op("dve", lambda e: e.tensor_copy(out=abc[:].rearrange("p h c -> p (h c)"), in_=pst[:, 0:4 * NC]),
               reads=["g_pst"], writes=["abc"])
        sch.end()

    def chunk_phase(self, wsT, eT, abc, prefetch=None):
        nc, sch, S, T = self.nc, self.sch, self.S, self.T
        NT = self.NT
        C_ = self.C
        ident, mask01 = C_["ident"], C_["mask01"]
        es = sch.begin()
        qTc = [self.sb(es, f"qTc{i}", [96, 4, 128], BF16) for i in range(2)]
        kTc = [self.sb(es, f"kTc{i}", [96, 4, 128], BF16) for i in range(2)]
        ktc = [self.sb(es, f"ktc{i}", [128, 384], BF16) for i in range(2)]
        vtc = [self.sb(es, f"vtc{i}", [128, 768], BF16) for i in range(2)]
        soc = [self.sb(es, f"soc{i}", [128, 8, 128], BF16) for i in range(4)]
        ucc = [self.sb(es, f"ucc{i}", [128, 8, 128], BF16) for i in range(4)]
        gtc = [self.sb(es, f"gtc{i}", [128, 8, 128], BF16) for i in range(4)]
        vp = [self.sb(es, f"vp{i}", [128, 4, 193], BF16) for i in range(2)]
        Sm = [self.sb(es, f"Sm{i}", [128, 4, 128], BF16) for i in range(2)]
        Ct = self.sb(es, "Ct", [96, 4, 193], F32)
        Cst = self.sb(es, "Cst", [96, 4, 193], F32)
        Chat = [self.sb(es, f"Chat{i}", [96, 4, 193], BF16) for i in range(2)]
        hout = [self.sb(es, f"hout{i}", [128, 4, 192], F32) for i in range(2)]
        hn = [self.sb(es, f"hn{i}", [128, 896], BF16) for i in range(2)]
        den = [self.sb(es, f"den{i}", [128, 4], F32) for i in range(2)]
        ssh = [self.sb(es, f"ssh{i}", [128, 4], F32) for i in range(2)]
        junk = self.sb(es, "cjunk", [128, 192], BF16)
        m1 = [self.sb(es, f"m1_{i}", [128, 8, 128], F32) for i in range(2)]
        m2 = [self.sb(es, f"m2_{i}", [128, 8, 128], F32) for i in range(2)]
        cgc = [self.sb(es, f"cgc{i}", [128, 8, 128], BF16) for i in range(2)]
        skipb = self.sb(es, "skipb", [128, 8, 128], F32)
        skp = self.sb(es, "skp", [128, 8], F32)
        onesf = self.sb(es, "onesf", [128, 128], F32)
        headg = self.sb(es, "headg", [128, 768], F32)
        pss = [self.ps(es, f"c_pss{i}", [128, 512], F32) for i in range(1)] * 2
        pacc4 = [self.ps(es, f"c_pacc{i}", [128, 512], F32) for i in range(4)]
        pU = [self.ps(es, f"c_pU{i}", [128, 512], F32) for i in range(2)]
        pT = self.ps(es, "c_pT", [128, 1024], BF16)
        sch.dma("sp", "headg", lambda e: e.dma_start(out=headg[:], in_=T["b_head_g"].partition_broadcast(128)),
                writes=["headg"])
        sch.op("dve", lambda e: e.memset(skp[:], 0.0), writes=["skp"])
        for i_ in range(2):
            sch.op("dve", lambda e, i_=i_: e.memset(hn[i_][:], 0.0), writes=[f"hn{i_}"])
        for ft, (r0, sz) in enumerate(FTP):
            sz = min(sz, 768 - r0)
            sch.dma("sp", f"skp{ft % 2}", lambda e, ft=ft, r0=r0, sz=sz: e.dma_start(
                out=skp[0:sz, ft:ft + 1], in_=T["b_skip"][r0:r0 + sz].rearrange("(p o) -> p o", o=1)), writes=["skp"])
        sch.op("dve", lambda e: e.memset(onesf[:], 1.0), writes=["onesf"])
        sch.op("dve", lambda e: e.memset(skipb[:], 0.0), writes=["skipb"])
        sch.op("dve", lambda e: e.memset(Cst[:], 0.0), writes=["Cst"])
        for ft, (r0, sz) in enumerate(FTP):
            sch.op("dve", lambda e, ft=ft, sz=sz: e.tensor_scalar(out=skipb[0:sz, ft, :], in0=onesf[0:sz, :],
                                                                 scalar1=skp[0:sz, ft:ft + 1], scalar2=None,
                                                                 op0=ALU.mult),
                   reads=["onesf", "skp", "skipb"], writes=["skipb"])

        def load(c):
            sl = c % 2
            tk = slice(c * 128, (c + 1) * 128)
            sch.dma("sp", f"qTc{sl}", lambda e: e.dma_start(out=qTc[sl][:], in_=T["qTB"][:, :, tk].rearrange("h d t -> d h t")),
                    writes=[f"qTc{sl}"])
            sch.dma("sp", f"kTc{sl}", lambda e: e.dma_start(out=kTc[sl][:], in_=T["kTB"][:, :, tk].rearrange("h d t -> d h t")),
                    writes=[f"kTc{sl}"])
            sch.dma("sp", f"ktc{sl}", lambda e: e.dma_start(out=ktc[sl][:], in_=T["ktokB"][tk, :]), writes=[f"ktc{sl}"])
            sch.dma("sp", f"vtc{sl}", lambda e: e.dma_start(out=vtc[sl][:], in_=T["vtokB"][tk, :]), writes=[f"vtc{sl}"])
            s3 = c % 4
            sch.dma("sp", f"soc{s3}", lambda e: e.dma_start(out=soc[s3][:], in_=T["soTB"][:, :, tk].rearrange("f p t -> p f t")),
                    writes=[f"soc{s3}"])
            sch.dma("sp", f"ucc{s3}", lambda e: e.dma_start(out=ucc[s3][:], in_=T["ucTB"][:, :, tk].rearrange("f p t -> p f t")),
                    writes=[f"ucc{s3}"])
            sch.dma("sp", f"gtc{s3}", lambda e: e.dma_start(out=gtc[s3][:], in_=T["gTB"][0:8, :, tk].rearrange("f p t -> p f t")),
                    writes=[f"gtc{s3}"])

        def chunk(c, sl):
            tk = slice(c * 128, (c + 1) * 128)
            r1, r2 = Rec(), Rec()
            pacc = pacc4[2 * (c % 2):2 * (c % 2) + 2]
            pn = [f"c_pacc{2 * (c % 2) + i_}" for i_ in range(2)]
            b2 = c % 2
            for h in range(4):
                r1.op("act", lambda e, h=h: e.activation(out=vp[sl][:, h, 0:192], in_=vtc[sl][:, h * 192:(h + 1) * 192],
                                                         func=AF.Copy, scale=wsT[:, c, h:h + 1]),
                       reads=[f"vtc{sl}", "wsT"], writes=[f"vp{sl}"])
            r1.op("dve", lambda e: e.tensor_copy(out=vp[sl][:, :, 192:193], in_=wsT[:, c, :].unsqueeze(2)),
                   reads=["wsT", f"vp{sl}"], writes=[f"vp{sl}"])
            for h in range(4):
                r1.op("act", lambda e, h=h: e.activation(out=Ct[:, h, :], in_=Cst[:, h, :], func=AF.Copy,
                                                         scale=abc[0:96, h, c:c + 1]),
                       reads=["Cst", "abc"], writes=["Ct"])
            r1.op("act", lambda e: e.activation(out=Chat[b2][:], in_=Ct[:], func=AF.Copy),
                   reads=["Ct"], writes=[f"Chat{b2}"])
            def sT(e):
                for h in range(4):
                    i = e.matmul(pss[b2][:, h * 128:(h + 1) * 128], lhsT=kTc[sl][:, h, :], rhs=qTc[sl][:, h, :],
                                 start=True, stop=True)
                return i
            r1.op("pe", sT, reads=[f"kTc{sl}", f"qTc{sl}"], writes=["c_pss0"])
            r1.op("dve", lambda e: e.tensor_tensor(
                out=Sm[b2][:], in0=pss[b2][:].rearrange("p (h t) -> p h t", h=4),
                in1=mask01[:].unsqueeze(1).broadcast_to([128, 4, 128]), op=ALU.mult),
                reads=["c_pss0", "mask01"], writes=[f"Sm{b2}"])
            def accf(e):
                for h in range(4):
                    o = pacc[h // 2][:, (h % 2) * 193:(h % 2) * 193 + 193]
                    e.matmul(o, lhsT=qTc[sl][:, h, :], rhs=Chat[b2][:, h, :], start=True, stop=False)
                    i = e.matmul(o, lhsT=Sm[b2][:, h, :], rhs=vp[sl][:, h, :], start=False, stop=True)
                return i
            r1.op("pe", accf, reads=[f"qTc{sl}", f"Chat{b2}", f"Sm{b2}", f"vp{sl}"], writes=[pn[0], pn[1]])
            def uf(e):
                for h in range(4):
                    i = e.matmul(pU[h // 2][0:96, (h % 2) * 193:(h % 2) * 193 + 193], lhsT=ktc[sl][:, h * 96:(h + 1) * 96],
                                 rhs=vp[sl][:, h, :], start=True, stop=True)
                return i
            r1.op("pe", uf, reads=[f"ktc{sl}", f"vp{sl}"], writes=["c_pU0", "c_pU1"])
            for bb in range(2):
                r1.op("dve", lambda e, bb=bb: e.tensor_tensor(
                    out=Cst[:, 2 * bb:2 * bb + 2, :].rearrange("p h d -> p (h d)"),
                    in0=Ct[:, 2 * bb:2 * bb + 2, :].rearrange("p h d -> p (h d)"), in1=pU[bb][0:96, 0:386], op=ALU.add),
                    reads=["Ct", f"c_pU{bb}"], writes=["Cst"])
            for bb in range(2):
                av = pacc[bb][:, 0:386].rearrange("p (h d) -> p h d", d=193)
                r2.op("act", lambda e, bb=bb, av=av: e.activation(out=den[sl][:, 2 * bb:2 * bb + 2].unsqueeze(2),
                                                                  in_=av[:, :, 192:193], func=AF.Abs),
                       reads=[pn[bb]], writes=[f"den{sl}"])
            r2.op("dve", lambda e: e.tensor_tensor(out=den[sl][:], in0=den[sl][:], in1=eT[:, c, :], op=ALU.max),
                   reads=[f"den{sl}", "eT"], writes=[f"den{sl}"])
            r2.op("dve", lambda e: e.reciprocal(out=den[sl][:], in_=den[sl][:]), reads=[f"den{sl}"], writes=[f"den{sl}"])
            for bb in range(2):
                av = pacc[bb][:, 0:386].rearrange("p (h d) -> p h d", d=193)
                r2.op("dve", lambda e, bb=bb, av=av: e.tensor_tensor(
                    out=hout[sl][:, 2 * bb:2 * bb + 2, :], in0=av[:, :, 0:192],
                    in1=den[sl][:, 2 * bb:2 * bb + 2].unsqueeze(2).broadcast_to([128, 2, 192]), op=ALU.mult),
                    reads=[pn[bb], f"den{sl}"], writes=[f"hout{sl}"])
            for h in range(4):
                r2.op("act", lambda e, h=h: e.activation(out=junk[:], in_=hout[sl][:, h, :], func=AF.Square,
                                                         accum_out=ssh[sl][:, h:h + 1]),
                       reads=[f"hout{sl}"], writes=[f"ssh{sl}"])
            r2.op("act", lambda e: e.activation(out=ssh[sl][:], in_=ssh[sl][:], func=AF.Ln, scale=1.0 / 192, bias=EPS),
                   reads=[f"ssh{sl}"], writes=[f"ssh{sl}"])
            r2.op("act", lambda e: e.activation(out=ssh[sl][:], in_=ssh[sl][:], func=AF.Exp, scale=-0.5),
                   reads=[f"ssh{sl}"], writes=[f"ssh{sl}"])
            for h in range(4):
                r2.op("dve", lambda e, h=h: e.scalar_tensor_tensor(
                    out=hn[sl][:, h * 192:(h + 1) * 192], in0=hout[sl][:, h, :], scalar=ssh[sl][:, h:h + 1],
                    in1=headg[:, h * 192:(h + 1) * 192], op0=ALU.mult, op1=ALU.mult),
                    reads=[f"hout{sl}", f"ssh{sl}", "headg"], writes=[f"hn{sl}"])
            r2a, r2 = r2, Rec()
            def tr(e):
                for ft, (r0, sz) in enumerate(FTP):
                    i = e.transpose(out=pT[0:sz, ft * 128:(ft + 1) * 128], in_=hn[sl][:, r0:r0 + sz], identity=ident[:])
                return i
            r2.op("pe", tr, reads=[f"hn{sl}", "ident"], writes=["c_pT"])
            s3 = c % 4
            r2.op("pool", lambda e: e.tensor_tensor(out=m2[sl][:], in0=ucc[s3][:], in1=skipb[:], op=ALU.mult),
                   reads=[f"ucc{s3}", "skipb"], writes=[f"m2_{sl}"])
            r2.op("dve", lambda e: e.tensor_tensor(out=m1[sl][:].rearrange("p f t -> p (f t)"), in0=pT[:],
                                                    in1=soc[s3][:].rearrange("p f t -> p (f t)"), op=ALU.mult),
                   reads=["c_pT", f"soc{s3}"], writes=[f"m1_{sl}"])
            r2.op("dve", lambda e: e.tensor_tensor(out=m1[sl][:], in0=m1[sl][:], in1=m2[sl][:], op=ALU.add),
                   reads=[f"m2_{sl}", f"m1_{sl}"], writes=[f"m1_{sl}"])
            r2.op("pool", lambda e: e.tensor_tensor(out=cgc[sl][:], in0=m1[sl][:], in1=gtc[s3][:], op=ALU.mult),
                   reads=[f"m1_{sl}", f"gtc{s3}"], writes=[f"cgc{sl}"])
            r2.dma(STQ, f"cgc{sl}", lambda e: e.dma_start(out=T["cgTB"][0:8, :, tk].rearrange("f p t -> p f t"),
                                                          in_=cgc[sl][:]), reads=[f"cgc{sl}"], writes=["cg_dram"])

            return r1, r2a, r2

        def zip_emit(lists):
            pos = [0] * len(lists)
            while True:
                best, bf = None, None
                for i, l in enumerate(lists):
                    if pos[i] < len(l):
                        f = pos[i] / len(l)
                        if best is None or f < bf:
                            best, bf = i, f
                if best is None:
                    break
                kind, args, kw = lists[best][pos[best]]
                pos[best] += 1
                getattr(sch, kind)(*args, **kw)

        load(0)
        pa, pb = [], []
        for c in range(NT):
            if c + 1 < NT:
                load(c + 1)
            r1, r2a, r2b = chunk(c, c % 2)
            if prefetch is not None and c == min(2, NT - 1):
                pf = sch.record(lambda: prefetch(es))
                pf_n = max(1, min(12, NT - 1 - c))
                pf_parts = [pf[(len(pf) * i_) // pf_n:(len(pf) * (i_ + 1)) // pf_n] for i_ in range(pf_n)]
            extra = []
            if prefetch is not None and c >= min(2, NT - 1) and pf_parts:
                extra = pf_parts.pop(0)
            zip_emit([r1.items, pa, pb, extra])
            pb = []
            pa, pb = r2a.items, pb
            nxt_b = r2b.items
            if c == 0:
                hold_b = nxt_b
            else:
                pb = hold_b
                hold_b = nxt_b
        zip_emit([pa, pb])
        zip_emit([hold_b])
        if prefetch is not None:
            for part in pf_parts:
                zip_emit([part])
        sch.end()


def make_consts():
    c = {}
    c["c_ident"] = np.eye(128, dtype=np.float32).astype(ml_dtypes.bfloat16)
    c["c_identf"] = np.eye(128, dtype=np.float32)
    c["c_ones"] = np.ones((128, 128), np.float32).astype(ml_dtypes.bfloat16)
    k = np.arange(128)[:, None]
    q = np.arange(128)[None, :]
    c["c_maskT"] = np.where(k <= q, 0.0, -30000.0).astype(np.float32).astype(ml_dtypes.bfloat16)
    c["c_mask01"] = np.where(k <= q, 1.0, 0.0).astype(np.float32).astype(ml_dtypes.bfloat16)
    oh = np.zeros((128, 2, 128), np.float32)
    oh[:, 0, 0:64] = 1.0
    oh[:, 1, 64:128] = 1.0
    c["c_onesh"] = oh.reshape(128, 256).astype(ml_dtypes.bfloat16)
    inv_freq = (10000.0 ** (-np.arange(0, 64, 2, dtype=np.float32) / np.float32(64))).astype(np.float32)
    c["c_invf"] = np.concatenate([inv_freq, inv_freq]).reshape(64, 1).astype(np.float32)
    c["c_sgn"] = np.concatenate([-np.ones(32), np.ones(32)]).reshape(64, 1).astype(np.float32)
    sel = np.zeros((4, 4, 128), np.float32)
    for h in range(4):
        sel[h, h, :] = 1.0
    c["c_sel"] = sel.reshape(4, 512)
    return c


def make_in_maps(inputs, S, ncores=8):
    consts = make_consts()
    shared = {}
    f = lambda a: np.ascontiguousarray(np.asarray(a, dtype=np.float32))
    shared["a_pre_g"] = f(inputs["a_pre_g"][0])
    shared["a_w_in"] = f(inputs["a_w_in"][0])
    shared["a_q_a_g"] = f(inputs["a_q_a_g"][0])
    shared["a_w_uq"] = f(inputs["a_w_uq"][0])
    shared["a_kv_a_g"] = f(inputs["a_kv_a_g"][0])
    shared["a_w_ukv"] = f(inputs["a_w_ukv"][0])
    shared["a_mem_g"] = f(inputs["a_mem_g"][0])
    shared["a_w_mem_kv"] = f(inputs["a_w_mem_kv"][0])
    shared["a_w_out"] = f(inputs["a_w_out"][0])
    shared["a_post_g"] = f(inputs["a_post_g"][0]).reshape(1, D)
    shared["b_pre_g"] = f(inputs["b_pre_g"][0])
    shared["b_w_in"] = f(inputs["b_w_in"][0])
    shared["b_gate_bias"] = f(inputs["b_gate_bias"][0]).reshape(8, 1)
    shared["b_conv_wT"] = f(np.asarray(inputs["b_conv_w"][0]).T)
    shared["b_conv_b"] = f(inputs["b_conv_b"][0])
    shared["b_w_q"] = f(inputs["b_w_q"][0]).reshape(768, 96)
    shared["b_w_k"] = f(inputs["b_w_k"][0]).reshape(768, 96)
    shared["b_w_v"] = f(inputs["b_w_v"][0]).reshape(768, 192)
    shared["b_head_g"] = f(inputs["b_head_g"][0]).reshape(1, 768)
    shared["b_skip"] = f(inputs["b_skip"][0])
    shared["b_mem_g"] = f(inputs["b_mem_g"][0])
    shared["b_w_mem_kv"] = f(inputs["b_w_mem_kv"][0])
    shared["b_w_out"] = f(inputs["b_w_out"][0])
    shared["b_post_g"] = f(inputs["b_post_g"][0]).reshape(1, D)
    shared.update(consts)
    maps = []
    for b in range(ncores):
        m = dict(shared)
        m["x"] = f(inputs["x"][b, :S])
        m["mem"] = f(inputs["mem"][b])
        m["pos"] = np.ascontiguousarray(np.asarray(inputs["positions"][b, :S], dtype=np.int32)).reshape(1, S)
        maps.append(m)
    return maps


_CACHE = {}


def kernel(**inputs):
    S = 4096
    if S not in _CACHE:
        _CACHE[S] = Builder(S).build()
    nc = _CACHE[S]
    maps = make_in_maps(inputs, S)
    res = run_bass_kernel_spmd(nc, maps, core_ids=list(range(8)))
    return np.stack([np.asarray(r["out"], dtype=np.float32) for r in res.results], axis=0)
```
